# Optimizing a Trainium2 kernel written in Bass

```python
import jax, jax.numpy as jnp
from jax import lax
import numpy as np

D_MODEL = 1024
BATCH = 8
SEQ = 2048
DEPTH = 2

GRID_W = 64
CTX_LEN = 256
EPS = 1e-6
N_MIXERS = 2
NEG_INF = -1e30

A_HEADS = 16
A_KV_HEADS = 2
A_HEAD_DIM = 64
A_GROUP = A_HEADS // A_KV_HEADS
A_WIDTH = A_HEADS * A_HEAD_DIM
A_KV_WIDTH = A_KV_HEADS * A_HEAD_DIM
A_SPLITS = (A_WIDTH, A_WIDTH + A_KV_WIDTH, A_WIDTH + 2 * A_KV_WIDTH)
A_IN_WIDTH = 2 * A_WIDTH + 2 * A_KV_WIDTH
WINDOW = 128
BLOCK = 128
ROPE_BASE = 10000.0
ROPE_FREQS = A_HEAD_DIM // 4

B_HEADS = 4
B_K_WIDTH = D_MODEL // 2
B_V_WIDTH = D_MODEL
B_KEY_DIM = B_K_WIDTH // B_HEADS
B_VAL_DIM = B_V_WIDTH // B_HEADS
B_SPLITS = (B_K_WIDTH, 2 * B_K_WIDTH, 2 * B_K_WIDTH + B_V_WIDTH)
B_IN_WIDTH = 2 * B_K_WIDTH + 2 * B_V_WIDTH
GATE_RANK = 16
GATE_TEMP = 16.0
CHUNK = 64

kernel_name = 'hybrid_swa_sink_gla_prefix_dit'


def rmsnorm(x, g):
    xf = x.astype(jnp.float32)
    y = xf * lax.rsqrt(jnp.mean(xf * xf, axis=-1, keepdims=True) + EPS)
    return (y * g.astype(jnp.float32)).astype(x.dtype)


def modulation(cvec, w_ada, b_ada, n):
    m = jax.nn.silu(cvec) @ w_ada[:, :n * D_MODEL] + b_ada[:n * D_MODEL]
    return jnp.split(m, n, axis=-1)


def modulate(h, shift, scale):
    return h * (1 + scale[..., None, :]) + shift[..., None, :]


def heads(t, n_heads):
    return t.reshape(t.shape[0], t.shape[1], n_heads, -1)


def axial_rope_tables(n_tokens):
    rows_n = n_tokens // GRID_W
    row = jnp.repeat(jnp.arange(rows_n, dtype=jnp.float32), GRID_W)
    col = jnp.tile(jnp.arange(GRID_W, dtype=jnp.float32), rows_n)
    inv_freq = ROPE_BASE ** (-jnp.arange(ROPE_FREQS, dtype=jnp.float32) / ROPE_FREQS)
    ang = jnp.stack([row[:, None] * inv_freq, col[:, None] * inv_freq], axis=1)
    return jnp.cos(ang), jnp.sin(ang)


def apply_axial_rope(x, cos, sin):
    b_, s_, h_, _ = x.shape
    xr = x.reshape(b_, s_, h_, 2, 2, ROPE_FREQS)
    x1, x2 = xr[..., 0, :], xr[..., 1, :]
    c = cos[None, :, None]
    s = sin[None, :, None]
    y = jnp.stack([x1 * c - x2 * s, x1 * s + x2 * c], axis=-2)
    return y.reshape(x.shape).astype(x.dtype)


def softmax_with_sink(parts, sink_logit):
    logits = jnp.concatenate(parts + [sink_logit], axis=-1)
    return jax.nn.softmax(logits, axis=-1)[..., :-1]


def attn_layer(x, xc, c, c_ctx, norm_g, w_ada, b_ada, w_in, sink, w_out, last):
    f32 = jnp.float32
    b_, s_, _ = x.shape
    n_ctx = xc.shape[1]
    nb = s_ // BLOCK
    qscale = A_HEAD_DIM ** -0.5
    sink_kg = sink.astype(f32).reshape(A_KV_HEADS, A_GROUP)

    shift, scl, gate = modulation(c, w_ada, b_ada, 3)
    h = modulate(rmsnorm(x, norm_g), shift, scl)
    q, k, v, g = jnp.split(h @ w_in, A_SPLITS, axis=-1)
    cos, sin = axial_rope_tables(s_)
    q = apply_axial_rope(heads(q, A_HEADS), cos, sin)
    k = apply_axial_rope(heads(k, A_KV_HEADS), cos, sin)
    v = heads(v, A_KV_HEADS)

    if last:
        shift_c, scl_c = modulation(c_ctx, w_ada, b_ada, 2)
        hc = modulate(rmsnorm(xc, norm_g), shift_c, scl_c)
        kc, vc = jnp.split(hc @ w_in[:, A_WIDTH:A_WIDTH + 2 * A_KV_WIDTH], 2, axis=-1)
    else:
        shift_c, scl_c, gate_c = modulation(c_ctx, w_ada, b_ada, 3)
        hc = modulate(rmsnorm(xc, norm_g), shift_c, scl_c)
        qc, kc, vc, gc = jnp.split(hc @ w_in, A_SPLITS, axis=-1)
    kc = heads(kc, A_KV_HEADS)
    vc = heads(vc, A_KV_HEADS)

    qb = q.reshape(b_, nb, BLOCK, A_KV_HEADS, A_GROUP, A_HEAD_DIM) * qscale
    pad = ((0, 0), (BLOCK, BLOCK), (0, 0), (0, 0))
    kp = jnp.pad(k, pad).reshape(b_, nb + 2, BLOCK, A_KV_HEADS, A_HEAD_DIM)
    vp = jnp.pad(v, pad).reshape(b_, nb + 2, BLOCK, A_KV_HEADS, A_HEAD_DIM)
    kwin = jnp.concatenate([kp[:, :-2], kp[:, 1:-1], kp[:, 2:]], axis=2)
    vwin = jnp.concatenate([vp[:, :-2], vp[:, 1:-1], vp[:, 2:]], axis=2)
    s_loc = jnp.einsum('bnqkgd,bnskd->bnkgqs', qb, kwin).astype(f32)
    s_ctx = jnp.einsum('bnqkgd,bckd->bnkgqc', qb, kc).astype(f32)
    qpos = jnp.arange(s_).reshape(nb, BLOCK)
    kpos = (jnp.arange(nb) * BLOCK - BLOCK)[:, None] + jnp.arange(3 * BLOCK)[None, :]
    kk = kpos[:, None, :]
    valid = (kk >= 0) & (kk < s_) & (jnp.abs(kk - qpos[:, :, None]) <= WINDOW)
    s_loc = jnp.where(valid[None, :, None, None], s_loc, NEG_INF)
    sink_b = jnp.broadcast_to(sink_kg[None, None, :, :, None, None], s_loc.shape[:-1] + (1,))
    p = softmax_with_sink([s_loc, s_ctx], sink_b)
    p_loc = p[..., :3 * BLOCK].astype(v.dtype)
    p_ctx = p[..., 3 * BLOCK:].astype(v.dtype)
    o = (jnp.einsum('bnkgqs,bnskd->bnqkgd', p_loc, vwin)
         + jnp.einsum('bnkgqc,bckd->bnqkgd', p_ctx, vc))
    o = o.reshape(b_, s_, A_WIDTH) * jax.nn.silu(g)
    x_new = x + gate[:, None, :] * (o @ w_out)

    if last:
        return x_new, None
    qcb = qc.reshape(b_, n_ctx, A_KV_HEADS, A_GROUP, A_HEAD_DIM) * qscale
    sc = jnp.einsum('bqkgd,bckd->bkgqc', qcb, kc).astype(f32)
    sink_c = jnp.broadcast_to(sink_kg[None, :, :, None, None], sc.shape[:-1] + (1,))
    pc = softmax_with_sink([sc], sink_c).astype(vc.dtype)
    oc = jnp.einsum('bkgqc,bckd->bqkgd', pc, vc).reshape(b_, n_ctx, A_WIDTH) * jax.nn.silu(gc)
    xc_new = xc + gate_c[..., None, :] * (oc @ w_out)
    return x_new, xc_new


def log_decay(h, wa1, wa2, ba):
    z = (h @ wa1) @ wa2 + ba
    return heads(jax.nn.log_sigmoid(z.astype(jnp.float32)) / GATE_TEMP, B_HEADS)


def gla_chunked(q, k, v, log_a, s0):
    b_, t_, h_, _ = q.shape
    nc = t_ // CHUNK
    rs = lambda t: t.reshape(b_, nc, CHUNK, h_, t.shape[-1]).astype(jnp.float32)
    qc, kc, vc, la = rs(q), rs(k), rs(v), rs(log_a)
    cum = jnp.cumsum(la, axis=2)
    total = cum[:, :, -1:]
    q_dec = qc * jnp.exp(cum)
    k_inv = kc * jnp.exp(-cum)
    k_end = kc * jnp.exp(total - cum)
    causal = jnp.tril(jnp.ones((CHUNK, CHUNK), dtype=bool))
    a = jnp.einsum('bnthd,bnshd->bnhts', q_dec, k_inv)
    a = jnp.where(causal, a, 0.0)
    o_intra = jnp.einsum('bnhts,bnshe->bnthe', a, vc)
    kv_chunk = jnp.einsum('bnshd,bnshe->bnhde', k_end, vc)
    decay_chunk = jnp.exp(total[:, :, 0])

    def step(s, inp):
        qd, kv, dec = inp
        o_inter = jnp.einsum('bthd,bhde->bthe', qd, s)
        return s * dec[..., None] + kv, o_inter

    s_final, o_inter = lax.scan(step, s0.astype(jnp.float32),
                                (jnp.swapaxes(q_dec, 0, 1), jnp.swapaxes(kv_chunk, 0, 1),
                                 jnp.swapaxes(decay_chunk, 0, 1)))
    o = o_intra + jnp.swapaxes(o_inter, 0, 1)
    return o.reshape(b_, t_, h_, v.shape[-1]), s_final


def gla_final_state(k, v, log_a):
    cum = jnp.cumsum(log_a.astype(jnp.float32), axis=1)
    kd = k.astype(jnp.float32) * jnp.exp(cum[:, -1:] - cum)
    return jnp.einsum('bthd,bthe->bhde', kd, v.astype(jnp.float32))


def gla_output(o, g, head_norm_g, w_out):
    of = o * lax.rsqrt(jnp.mean(o * o, axis=-1, keepdims=True) + EPS)
    of = of.reshape(o.shape[0], o.shape[1], B_V_WIDTH) * head_norm_g.astype(jnp.float32)
    return (of.astype(g.dtype) * jax.nn.silu(g)) @ w_out


def gla_layer(x, xc, c, c_ctx, norm_g, w_ada, b_ada, w_in, wa1_f, wa2_f, ba_f,
              wa1_b, wa2_b, ba_b, head_norm_g, w_out, last):
    flip = lambda t: jnp.flip(t, axis=1)
    kscale = B_KEY_DIM ** -0.5
    b_ = x.shape[0]

    shift, scl, gate = modulation(c, w_ada, b_ada, 3)
    h = modulate(rmsnorm(x, norm_g), shift, scl)
    q, k, v, g = jnp.split(h @ w_in, B_SPLITS, axis=-1)
    q = heads(q, B_HEADS) * kscale
    k = heads(k, B_HEADS)
    v = heads(v, B_HEADS)
    la_f = log_decay(h, wa1_f, wa2_f, ba_f)
    la_b = log_decay(h, wa1_b, wa2_b, ba_b)

    if last:
        shift_c, scl_c = modulation(c_ctx, w_ada, b_ada, 2)
        hc = modulate(rmsnorm(xc, norm_g), shift_c, scl_c)
        kc, vc = jnp.split(hc @ w_in[:, B_K_WIDTH:2 * B_K_WIDTH + B_V_WIDTH], [B_K_WIDTH], axis=-1)
        kc, vc = heads(kc, B_HEADS), heads(vc, B_HEADS)
        s_f = gla_final_state(kc, vc, log_decay(hc, wa1_f, wa2_f, ba_f))
        s_b = gla_final_state(flip(kc), flip(vc), flip(log_decay(hc, wa1_b, wa2_b, ba_b)))
        xc_new = None
    else:
        shift_c, scl_c, gate_c = modulation(c_ctx, w_ada, b_ada, 3)
        hc = modulate(rmsnorm(xc, norm_g), shift_c, scl_c)
        qc, kc, vc, gc = jnp.split(hc @ w_in, B_SPLITS, axis=-1)
        qc = heads(qc, B_HEADS) * kscale
        kc, vc = heads(kc, B_HEADS), heads(vc, B_HEADS)
        zeros = jnp.zeros((b_, B_HEADS, B_KEY_DIM, B_VAL_DIM), jnp.float32)
        oc_f, s_f = gla_chunked(qc, kc, vc, log_decay(hc, wa1_f, wa2_f, ba_f), zeros)
        oc_b, s_b = gla_chunked(flip(qc), flip(kc), flip(vc),
                                flip(log_decay(hc, wa1_b, wa2_b, ba_b)), zeros)
        oc = gla_output(oc_f + flip(oc_b), gc, head_norm_g, w_out)
        xc_new = xc + gate_c[..., None, :] * oc

    o_f, _ = gla_chunked(q, k, v, la_f, s_f)
    o_b, _ = gla_chunked(flip(q), flip(k), flip(v), flip(la_b), s_b)
    o = gla_output(o_f + flip(o_b), g, head_norm_g, w_out)
    x_new = x + gate[:, None, :] * o
    return x_new, xc_new


def setup_inputs(seed: int = 0) -> dict:
    key = jax.random.key(seed)
    ks = iter(jax.random.split(key, 32))
    D = D_MODEL
    nrm = lambda shape, s: jax.random.normal(next(ks), shape, jnp.float32) * s
    return {
        'x': nrm((BATCH, SEQ, D), 1.0),
        'c': nrm((BATCH, D), 1.0),
        'ctx': nrm((BATCH, CTX_LEN, D), 1.0),
        'c_ctx': nrm((D,), 1.0),
        'l0_norm_g': 1.0 + nrm((D,), 0.02),
        'l0_w_ada': nrm((D, 3 * D), D ** -0.5),
        'l0_b_ada': nrm((3 * D,), 0.02),
        'l0_w_in': nrm((D, A_IN_WIDTH), D ** -0.5),
        'l0_sink': nrm((A_HEADS,), 0.5),
        'l0_w_out': nrm((A_WIDTH, D), A_WIDTH ** -0.5),
        'l1_norm_g': 1.0 + nrm((D,), 0.02),
        'l1_w_ada': nrm((D, 3 * D), D ** -0.5),
        'l1_b_ada': nrm((3 * D,), 0.02),
        'l1_w_in': nrm((D, B_IN_WIDTH), D ** -0.5),
        'l1_wa1_f': nrm((D, GATE_RANK), D ** -0.5),
        'l1_wa2_f': nrm((GATE_RANK, B_K_WIDTH), GATE_RANK ** -0.5),
        'l1_ba_f': nrm((B_K_WIDTH,), 0.1),
        'l1_wa1_b': nrm((D, GATE_RANK), D ** -0.5),
        'l1_wa2_b': nrm((GATE_RANK, B_K_WIDTH), GATE_RANK ** -0.5),
        'l1_ba_b': nrm((B_K_WIDTH,), 0.1),
        'l1_head_norm_g': 1.0 + nrm((B_V_WIDTH,), 0.02),
        'l1_w_out': nrm((B_V_WIDTH, D), B_V_WIDTH ** -0.5),
        'final_norm_g': 1.0 + nrm((D,), 0.02),
    }


def reference(x, c, ctx, c_ctx,
              l0_norm_g, l0_w_ada, l0_b_ada, l0_w_in, l0_sink, l0_w_out,
              l1_norm_g, l1_w_ada, l1_b_ada, l1_w_in, l1_wa1_f, l1_wa2_f, l1_ba_f,
              l1_wa1_b, l1_wa2_b, l1_ba_b, l1_head_norm_g, l1_w_out,
              final_norm_g):
    mixers = (attn_layer, gla_layer)
    layer_params = (
        (l0_norm_g, l0_w_ada, l0_b_ada, l0_w_in, l0_sink, l0_w_out),
        (l1_norm_g, l1_w_ada, l1_b_ada, l1_w_in, l1_wa1_f, l1_wa2_f, l1_ba_f,
         l1_wa1_b, l1_wa2_b, l1_ba_b, l1_head_norm_g, l1_w_out),
    )
    xc = ctx
    for i in range(DEPTH):
        fn = mixers[i % N_MIXERS]
        x, xc = fn(x, xc, c, c_ctx, *layer_params[i], last=(i == DEPTH - 1))
    return rmsnorm(x, final_norm_g)
```

```python
import contextlib
import numpy as np
import concourse.bass as bass
import concourse.mybir as mybir
from concourse.bass_utils import run_bass_kernel_spmd

F32 = mybir.dt.float32
BF16 = mybir.dt.bfloat16
AF = mybir.ActivationFunctionType
ALU = mybir.AluOpType

D = 1024
S = 2048
CTX = 256
T = S + CTX
NT = T // 128
EPS = 1e-6
GROUPS = [(0, 512), (512, 512), (1024, 512), (1536, 512), (2048, 256)]


class Buf:
    __slots__ = ("name", "w", "r")

    def __init__(self, name):
        self.name = name
        self.w = None
        self.r = []


class Op:
    __slots__ = ("eng", "fn", "deps", "kind", "sem", "val", "needed")

    def __init__(self, eng, fn, kind):
        self.eng = eng
        self.fn = fn
        self.kind = kind
        self.deps = []
        self.sem = None
        self.val = 0
        self.needed = False


class Em:
    ENGS = ("pe", "act", "dve", "pool", "sp")

    def __init__(self, nc, n_dma_sems=32):
        self.nc = nc
        self.ops = {e: [] for e in self.ENGS}
        self.n_dma_sems = n_dma_sems
        self.dma_last = [None] * n_dma_sems
        self.dma_cnt = [0] * n_dma_sems
        self.dma_rr = 0
        self.n_sw = 0
        self.pending_barrier = {}

    def buf(self, name="b"):
        return Buf(name)

    def bufs(self, name, n):
        return [Buf(f"{name}{i}") for i in range(n)]

    def _track(self, op, reads, writes):
        deps = op.deps
        for b in reads:
            if b.w is not None:
                deps.append(b.w)
        for b in writes:
            if b.w is not None:
                deps.append(b.w)
            deps.extend(b.r)
        for b in reads:
            b.r.append(op)
        for b in writes:
            b.w = op
            b.r = []

    def op(self, eng, fn, reads=(), writes=()):
        o = Op(eng, fn, "c")
        if self.pending_barrier.get(eng):
            o.deps.extend(self.pending_barrier.pop(eng))
        self._track(o, reads, writes)
        self.ops[eng].append(o)
        return o

    def dma(self, eng, out, in_, reads=(), writes=(), **kw):
        o = Op(eng, (lambda e, out=out, in_=in_, kw=kw: e.dma_start(out=out, in_=in_, **kw)), "d")
        if eng == "pool":
            k = self.n_dma_sems + self.n_sw
            self.n_sw += 1
            o.sem = ("d", k)
            o.val = 16
        else:
            k = self.dma_rr
            self.dma_rr = (self.dma_rr + 1) % self.n_dma_sems
            if self.dma_last[k] is not None:
                o.deps.append(self.dma_last[k])
            self.dma_cnt[k] += 1
            o.sem = ("d", k)
            o.val = 16 * self.dma_cnt[k]
            self.dma_last[k] = o
        if self.pending_barrier.get(eng):
            o.deps.extend(self.pending_barrier.pop(eng))
        self._track(o, reads, writes)
        self.ops[eng].append(o)
        return o

    def barrier(self):
        lasts = []
        for e in self.ENGS:
            if self.ops[e]:
                lasts.append(self.ops[e][-1])
        for d in self.dma_last:
            if d is not None:
                lasts.append(d)
        self.pending_barrier = {e: list(lasts) for e in self.ENGS}

    def emit(self, stack, final_waits=()):
        nc = self.nc
        esem = {e: stack.enter_context(nc.semaphore(f"s_{e}")) for e in ("pe", "act", "dve", "pool")}
        dsem = [stack.enter_context(nc.semaphore(f"d_{k}")) for k in range(self.n_dma_sems + self.n_sw)]
        for e in self.ENGS:
            for o in self.ops[e]:
                for d in o.deps:
                    if d.kind == "c":
                        if d.eng == "pe" and o.eng == "pe" and o.kind == "c":
                            continue
                        d.needed = True
        for o in final_waits:
            if o.kind == "c":
                o.needed = True
        for e in ("pe", "act", "dve", "pool"):
            c = 0
            for o in self.ops[e]:
                if o.kind == "c" and o.needed:
                    c += 1
                    o.sem = ("e", e)
                    o.val = c

        def semh(s):
            return esem[s[1]] if s[0] == "e" else dsem[s[1]]

        engmap = {"pe": "tensor", "act": "scalar", "dve": "vector", "pool": "gpsimd", "sp": "sync"}
        stats = {}
        block = stack.enter_context(nc.Block())

        def make(e):
            def body(eng):
                seen = {}
                nw = 0
                for o in self.ops[e]:
                    need = {}
                    for d in o.deps:
                        if d.kind == "c" and d.eng == "pe" and e == "pe" and o.kind == "c":
                            continue
                        if d.sem is None:
                            continue
                        if need.get(d.sem, 0) < d.val:
                            need[d.sem] = d.val
                    for s, v in need.items():
                        if seen.get(s, 0) < v:
                            eng.wait_ge(semh(s), v)
                            seen[s] = v
                            nw += 1
                    ins = o.fn(eng)
                    if o.kind == "d":
                        ins.then_inc(semh(o.sem), 16)
                    elif o.needed:
                        ins.then_inc(semh(o.sem), 1)
                if e == "sp":
                    need = {}
                    for d in final_waits:
                        if need.get(d.sem, 0) < d.val:
                            need[d.sem] = d.val
                    for s, v in need.items():
                        eng.wait_ge(semh(s), v)
                stats[e] = (len(self.ops[e]), nw)
            return body

        for e in self.ENGS:
            getattr(block, engmap[e])(make(e))
        return stats


class Arena:
    def __init__(self, big, nfl):
        self.big = big
        self.n = nfl
        self.off = 0
        self.peak = 0

    def _alloc(self, nfl):
        a = self.off
        self.off += nfl
        self.peak = max(self.peak, self.off)
        assert self.off <= self.n, f"arena overflow {self.off} > {self.n}"
        return a

    @staticmethod
    def _shape(ap, shape):
        if len(shape) == 1:
            return ap
        if len(shape) == 2:
            return ap.rearrange("p (a b) -> p a b", a=shape[0])
        if len(shape) == 3:
            return ap.rearrange("p (a b c) -> p a b c", a=shape[0], b=shape[1])
        raise ValueError

    def f32(self, *shape):
        n = int(np.prod(shape))
        a = self._alloc(n)
        return self._shape(self.big[:, a:a + n], shape)

    def bf(self, *shape):
        n = int(np.prod(shape))
        nfl = (n + 1) // 2
        a = self._alloc(nfl)
        return self._shape(self.big[:, a:a + nfl].bitcast(BF16)[:, 0:n], shape)

    def mark(self):
        return self.off

    def release(self, m):
        self.off = m


def build_program(stage=2, stop=99):
    nc = bass.Bass("TRN2", target_bir_lowering=False)

    def din(name, shape):
        return nc.dram_tensor(name, list(shape), F32, kind="ExternalInput").ap()

    xs_d = din("xs", [128, 8, T])
    cc_d = din("cc", [128, 8, 2])
    ropec_d = din("ropec", [128, S])
    ropes_d = din("ropes", [128, S])
    perm_d = din("perm", [128, 128])
    ident_d = din("ident", [128, 128])
    mask_d = din("mask", [128, 256])
    rmask_d = din("rmask", [128, 512])
    sink_d = din("sink", [128, 16])
    vec0_d = din("vec0", [128, 32])
    vec1_d = din("vec1", [128, 56])
    wada0_d = din("wada0", [128, 8, 3072])
    win0_d = din("win0", [128, 8, 2304])
    wout0_d = din("wout0", [128, 8, 1024])
    wada1_d = din("wada1", [128, 8, 3072])
    win1_d = din("win1", [128, 8, 3072])
    wout1_d = din("wout1", [128, 8, 1024])
    wa1_d = din("wa1", [128, 8, 32])
    wa2_d = din("wa2", [32, 2, 512])
    y_d = nc.dram_tensor("y", [128, 8, S], F32, kind="ExternalOutput").ap()

    em = Em(nc)
    NFL = 53000
    with contextlib.ExitStack() as st:
        big = st.enter_context(nc.sbuf_tensor("big", [128, NFL], F32))
        ar = Arena(big, NFL)
        NPS = 6
        PS = [st.enter_context(nc.psum_tensor(f"ps{i}", [128, 512], F32)) for i in range(NPS)]
        PSD = [st.enter_context(nc.psum_tensor(f"psd{i}", [128, 512], F32)) for i in range(2)]
        PSDb = em.bufs("psd", 2)
        PSB = None
        PSb = em.bufs("ps", NPS)
        PSBb = [em.buf("psbA"), em.buf("psbB")]
        ps_rr = [0]
        psb_rr = [0]

        ring_n = [NPS + 2]
        PSall = PS + PSD
        PSallb = PSb + PSDb

        def ps_next():
            i = ps_rr[0] % ring_n[0]
            ps_rr[0] = (i + 1) % ring_n[0]
            return PSall[i], PSallb[i]

        def psb_next():
            i = psb_rr[0]
            psb_rr[0] = (i + 1) % 2
            return PSB[:, i * 512:(i + 1) * 512], PSBb[i]

        XS = ar.f32(8, T)
        XSb = [[em.buf(f"xs{j}_{g}") for g in range(5)] for j in range(8)]
        RSTD = ar.f32(T)
        RSTDb = em.bufs("rstd", 5)
        cc = ar.f32(8, 2)
        scb = ar.bf(8, 2)
        MOD = ar.f32(24, 2)
        GM = ar.f32(8, 2)
        MOD0, GM0 = MOD, GM
        MOD1 = ar.f32(24, 2)
        GM1 = ar.f32(8, 2)
        modstage = ar.f32(48)
        vec0 = ar.f32(32)
        vec1 = ar.f32(56)
        ones_bf = ar.bf(128)
        ident_bf = ar.bf(128)
        mask = ar.f32(256)
        sinkexp = ar.f32(16)
        Bc = {k: em.buf(k) for k in ["cc", "scb", "MOD", "GM", "vec0", "vec1", "ones", "ident", "mask",
                                     "sinkexp", "perm", "rmask", "nba", "MOD1", "GM1", "modstage"]}
        mle = mask[:, 0:128]
        mge = mask[:, 128:256]

        for g, (t0, w) in enumerate(GROUPS):
            for j in range(8):
                em.dma("sp", XS[:, j, t0:t0 + w], xs_d[:, j, t0:t0 + w], writes=[XSb[j][g]])
        em.dma("sp", cc, cc_d, writes=[Bc["cc"]])
        em.dma("sp", vec0, vec0_d, writes=[Bc["vec0"]])
        em.dma("sp", vec1, vec1_d, writes=[Bc["vec1"]])
        em.dma("sp", mask, mask_d, writes=[Bc["mask"]])
        em.dma("sp", sinkexp, sink_d, writes=[Bc["sinkexp"]])
        em.dma("pool", ident_bf, ident_d, writes=[Bc["ident"]])
        em.op("pool", lambda e: e.memset(ones_bf, 1.0), writes=[Bc["ones"]])
        em.op("act", lambda e: e.activation(out=sinkexp, in_=sinkexp, func=AF.Exp),
              reads=[Bc["sinkexp"]], writes=[Bc["sinkexp"]])
        em.op("act", lambda e: e.activation(out=scb, in_=cc, func=AF.Silu), reads=[Bc["cc"]], writes=[Bc["scb"]])

        class Modulation:
            NP = 24

            def __init__(self, wada_d, vec, vecb, ring32, ringbf, MOD, GM, MODb, GMb, alias_bufs, tag):
                self.a = (wada_d, vec, vecb, ring32, ringbf, MOD, GM, MODb, GMb)
                self.alias = list(alias_bufs)
                self.b32 = em.bufs("mr32" + tag, 2)
                self.bbf = em.bufs("mrbf" + tag, 2)
                self.first = [True, True]

            def dma(self, pc):
                wada_d, vec, vecb, ring32, ringbf, MOD, GM, MODb, GMb = self.a
                slot = pc % 2
                extra = self.alias if self.first[slot] else []
                self.first[slot] = False
                em.dma("sp", ring32[slot], wada_d[:, :, pc * 128:(pc + 1) * 128], writes=[self.b32[slot]] + extra)

            def mm(self, pc):
                wada_d, vec, vecb, ring32, ringbf, MOD, GM, MODb, GMb = self.a
                slot = pc % 2
                em.op("dve", lambda e: e.tensor_copy(out=ringbf[slot], in_=ring32[slot]),
                      reads=[self.b32[slot]], writes=[self.bbf[slot]])
                pm, pmb = ps_next()
                for kc in range(8):
                    em.op("pe", lambda e, kc=kc: e.matmul(pm[:, 0:2], lhsT=ringbf[slot][:, kc, :], rhs=scb[:, kc, :],
                                                          start=(kc == 0), stop=(kc == 7)),
                          reads=[self.bbf[slot], Bc["scb"]], writes=[pmb])
                em.op("act", lambda e: e.activation(out=modstage[:, pc * 2:(pc + 1) * 2], in_=pm[:, 0:2], func=AF.Copy),
                      reads=[pmb], writes=[Bc["modstage"]])

            def finish(self):
                wada_d, vec, vecb, ring32, ringbf, MOD, GM, MODb, GMb = self.a
                pmv = modstage.rearrange("p (a b) -> p a b", a=24)
                for s in range(2):
                    em.op("dve", lambda e, s=s: e.tensor_tensor(out=MOD[:, :, s], in0=pmv[:, :, s], in1=vec[:, 8:32],
                                                                op=ALU.add),
                          reads=[Bc["modstage"], vecb], writes=[MODb])
                for s in range(2):
                    em.op("dve", lambda e, s=s: e.scalar_tensor_tensor(out=GM[:, :, s], in0=MOD[:, 8:16, s],
                                                                       scalar=1.0, in1=vec[:, 0:8], op0=ALU.add,
                                                                       op1=ALU.mult),
                          reads=[MODb, vecb] + self.b32 + self.bbf, writes=[GMb] + self.alias)

            def run_all(self):
                self.dma(0)
                self.dma(1)
                for pc in range(self.NP):
                    self.mm(pc)
                    if pc + 2 < self.NP:
                        self.dma(pc + 2)
                self.finish()

        def norm_stats(g, sqr, sqrb, tmp, tmpb, src=None, srcb=None, nchunk=8, inv_n=1.0 / D, out=None, outb=None):
            t0, w = GROUPS[g]
            pss, pssb = ps_next()
            for j in range(nchunk):
                sl = j % 2
                if src is None:
                    sap, sb = XS[:, j, t0:t0 + w], XSb[j][g]
                else:
                    sap, sb = src[:, j, 0:w], srcb
                em.op("act", lambda e, sl=sl, sap=sap: e.activation(out=sqr[sl][:, 0:w], in_=sap, func=AF.Square),
                      reads=[sb], writes=[sqrb[sl]])
                em.op("pe", lambda e, sl=sl, j=j: e.matmul(pss[:, 0:w], lhsT=ones_bf, rhs=sqr[sl][:, 0:w],
                                                          start=(j == 0), stop=(j == nchunk - 1)),
                      reads=[sqrb[sl], Bc["ones"]], writes=[pssb])
            em.op("act", lambda e: e.activation(out=tmp[:, 0:w], in_=pss[:, 0:w], func=AF.Ln, bias=EPS_AP,
                                                scale=inv_n),
                  reads=[pssb, Bc["eps"]], writes=[tmpb])
            if out is None:
                oap, ob = RSTD[:, t0:t0 + w], RSTDb[g]
            else:
                oap, ob = out[:, 0:w], outb
            em.op("act", lambda e: e.activation(out=oap, in_=tmp[:, 0:w], func=AF.Exp, scale=-0.5),
                  reads=[tmpb], writes=[ob])

        def make_h(g, Hdst, Hb, tmpr, tmprb):
            t0, w = GROUPS[g]
            s = 1 if g == 4 else 0
            for j in range(8):
                sl = j % 2
                em.op("dve", lambda e, j=j, sl=sl: e.tensor_tensor(out=tmpr[sl][:, 0:w], in0=XS[:, j, t0:t0 + w],
                                                                   in1=RSTD[:, t0:t0 + w], op=ALU.mult),
                      reads=[XSb[j][g], RSTDb[g]], writes=[tmprb[sl]])
                em.op("act", lambda e, j=j, sl=sl: e.activation(out=Hdst[:, j, 0:w], in_=tmpr[sl][:, 0:w],
                                                                func=AF.Identity, bias=MOD[:, j, s:s + 1],
                                                                scale=GM[:, j, s:s + 1]),
                      reads=[tmprb[sl], Bc["MOD"], Bc["GM"]], writes=(Hb if isinstance(Hb, list) else [Hb]))

        epsT = ar.f32(2)
        Bc["eps"] = em.buf("eps")
        EPS_AP = epsT[:, 0:1]
        NEGHALF = epsT[:, 1:2]
        em.op("pool", lambda e: e.memset(epsT[:, 0:1], EPS), writes=[Bc["eps"]])
        em.op("pool", lambda e: e.memset(epsT[:, 1:2], -0.5), writes=[Bc["eps"]])

        layer_mark = ar.mark()

        W0 = ar.bf(8, 2304)
        W0b = em.buf("W0")
        WO0 = ar.bf(8, 1024)
        WO0b = em.buf("WO0")
        KT0 = ar.bf(T)
        KT1 = ar.bf(T)
        KTb = em.bufs("kt", NT)
        VA = ar.bf(NT, 2, 66)
        VAb = em.bufs("va", NT)
        Hg = ar.bf(8, 512)
        Hgb = em.buf("Hg")
        alias_mark = ar.mark()
        QTg = ar.bf(8, 512)
        QTgb = em.bufs("qtg", 8)
        Gg = ar.bf(4, 1024)
        Ggb = em.bufs("gg", 4)
        alias_end = ar.mark()
        ar.release(alias_mark)
        mring = [ar.f32(8, 128) for _ in range(2)]
        mringbf = [ar.bf(8, 128) for _ in range(2)]
        assert ar.mark() <= alias_end
        ar.release(alias_end)
        rope_mark = ar.mark()
        ropeC = [ar.f32(512) for _ in range(2)]
        ropeS = [ar.f32(512) for _ in range(2)]
        ropeb = em.bufs("rope", 2)
        perm = ar.f32(128)
        qtmp = [ar.f32(512), ar.f32(512)]
        qtmpb = em.bufs("qtmp", 2)
        _t0 = ar.f32(512)
        t1r = [_t0, _t0]
        _tb = em.buf("t1r")
        t1rb = [_tb, _tb]
        t2r0 = ar.f32(512)
        t2r = [t2r0, t2r0]
        _save = ar.mark()
        ar.release(rope_mark)
        mring1 = [ar.f32(8, 128) for _ in range(2)]
        ar._alloc(128)
        mring1bf = [ar.bf(8, 128) for _ in range(2)]
        assert ar.mark() <= _save
        ar.release(_save)
        t2rb0 = em.buf("t2r")
        t2rb = [t2rb0, t2rb0]
        tmpr = [ar.f32(512) for _ in range(2)]
        tmprb = em.bufs("tmpr", 2)
        sqr = [ar.bf(512) for _ in range(2)]
        sqrb = em.bufs("sqr", 2)
        rden = [ar.f32(8) for _ in range(2)]
        rdenb = em.bufs("rden", 2)
        PTset = []
        PTb = []
        for sset in range(2):
            PTset.append([ar.bf(512) for _ in range(5)])
            PTb.append(em.bufs(f"pt{sset}_", 5))

        em.dma("sp", perm, perm_d, writes=[Bc["perm"]])
        for g in range(5):
            norm_stats(g, sqr, sqrb, tmpr[0], tmprb[0])
        Modulation(wada0_d, vec0, Bc["vec0"], mring, mringbf, MOD, GM, Bc["MOD"], Bc["GM"],
                   QTgb + Ggb[0:2], "0").run_all()
        for kc in range(8):
            for hf in range(2):
                em.dma("pool", W0[:, kc, hf * 1152:(hf + 1) * 1152], win0_d[:, kc, hf * 1152:(hf + 1) * 1152],
                       writes=[W0b])
        for kc in range(8):
            em.dma("pool", WO0[:, kc, :], wout0_d[:, kc, :], writes=[WO0b])
        em.op("pool", lambda e: e.memset(VA[:, :, :, 64:65], 1.0), writes=VAb)
        em.op("pool", lambda e: e.memset(KT0[64:128, :], 0.0), writes=KTb)
        em.op("pool", lambda e: e.memset(KT1[0:64, :], 0.0), writes=KTb)

        def load_rope(g, sl):
            t0, w = GROUPS[g]
            em.dma("sp", ropeC[sl][:, 0:w], ropec_d[:, t0:t0 + w], writes=[ropeb[sl]])
            em.dma("sp", ropeS[sl][:, 0:w], ropes_d[:, t0:t0 + w], writes=[ropeb[sl]])

        rope_ctr = [0]

        def proj_fm(dst, dstb, wcols, g, rope_sl, Hs=None, Hsb=None, split=None):
            t0, w = GROUPS[g]
            Hs = Hg if Hs is None else Hs
            Hsb = [Hgb] if Hsb is None else Hsb
            pq, pqb = ps_next()
            for kc in range(8):
                em.op("pe", lambda e, kc=kc: e.matmul(pq[:, 0:w], lhsT=W0[:, kc, wcols:wcols + 128],
                                                      rhs=Hs[:, kc, 0:w], start=(kc == 0), stop=(kc == 7)),
                      reads=[W0b] + Hsb, writes=[pqb])
            if g == 4:
                if split is None:
                    em.op("act", lambda e: e.activation(out=dst, in_=pq[:, 0:w], func=AF.Copy), reads=[pqb],
                          writes=dstb)
                else:
                    for (lo, hi, dd) in ((0, 64, split[0]), (64, 128, split[1])):
                        em.op("act", lambda e, lo=lo, hi=hi, dd=dd: e.activation(
                            out=dd[lo:hi, t0:t0 + w], in_=pq[lo:hi, 0:w], func=AF.Copy), reads=[pqb], writes=dstb)
                return
            i = rope_ctr[0] % 2
            rope_ctr[0] += 1
            em.op("act", lambda e: e.activation(out=qtmp[i][:, 0:w], in_=pq[:, 0:w], func=AF.Copy),
                  reads=[pqb], writes=[qtmpb[i]])
            yield
            pr, prb = ps_next()
            em.op("pe", lambda e: e.matmul(pr[:, 0:w], lhsT=perm, rhs=qtmp[i][:, 0:w], start=True, stop=True),
                  reads=[qtmpb[i], Bc["perm"]], writes=[prb])
            em.op("pool", lambda e: e.tensor_tensor(out=t1r[i][:, 0:w], in0=qtmp[i][:, 0:w],
                                                    in1=ropeC[rope_sl][:, 0:w], op=ALU.mult),
                  reads=[qtmpb[i], ropeb[rope_sl]], writes=[t1rb[i]])
            em.op("dve", lambda e: e.tensor_tensor(out=t2r[i][:, 0:w], in0=pr[:, 0:w], in1=ropeS[rope_sl][:, 0:w],
                                                   op=ALU.mult),
                  reads=[prb, ropeb[rope_sl]], writes=[t2rb[i]])
            if split is None:
                em.op("pool", lambda e: e.tensor_tensor(out=dst, in0=t1r[i][:, 0:w], in1=t2r[i][:, 0:w], op=ALU.add),
                      reads=[t1rb[i], t2rb[i]], writes=dstb)
            else:
                for (lo, hi, dd) in ((0, 64, split[0]), (64, 128, split[1])):
                    em.op("pool", lambda e, lo=lo, hi=hi, dd=dd: e.tensor_tensor(
                        out=dd[lo:hi, t0:t0 + w], in0=t1r[i][lo:hi, 0:w], in1=t2r[i][lo:hi, 0:w], op=ALU.add),
                        reads=[t1rb[i], t2rb[i]], writes=dstb)

        def vproj(g, tt, Hs, Hsb):
            t0, w = GROUPS[g]
            ti = t0 // 128 + tt
            pv, pvb = ps_next()
            for kc in range(8):
                em.op("pe", lambda e, kc=kc: e.matmul(pv[:, 0:128], lhsT=Hs[:, kc, tt * 128:(tt + 1) * 128],
                                                      rhs=W0[:, kc, 1152:1280], start=(kc == 0), stop=(kc == 7)),
                      reads=[W0b] + Hsb, writes=[pvb])
            em.op("dve", lambda e: e.tensor_copy(
                out=VA[:, ti, :, 0:64], in_=pv[:, 0:128].rearrange("p (a b) -> p a b", a=2)),
                reads=[pvb], writes=[VAb[ti]])

        def pass_a(g):
            t0, w = GROUPS[g]
            if g % 2 == 0:
                Hs, Hsb = Hg, [Hgb]
            else:
                Hs, Hsb = QTg, QTgb
            make_h(g, Hs, Hsb, tmpr, tmprb)
            if g < 4:
                load_rope(g, g % 2)
            for _ in proj_fm(None, KTb[t0 // 128:(t0 + w) // 128], 1024, g, g % 2, Hs, Hsb, split=(KT0, KT1)):
                pass
            for tt in range(w // 128):
                vproj(g, tt, Hs, Hsb)

        for g in range(5 if stop >= 2 else 0):
            pass_a(g)

        head_ctr = [0]
        mask_ctr = [0]

        def gproj(g, tt, half):
            pg, pgb = ps_next()
            for kc in range(8):
                em.op("pe", lambda e, kc=kc: e.matmul(
                    pg[:, :], lhsT=Hg[:, kc, tt * 128:(tt + 1) * 128],
                    rhs=W0[:, kc, 1280 + half * 512:1280 + (half + 1) * 512], start=(kc == 0), stop=(kc == 7)),
                    reads=[W0b, Hgb], writes=[pgb])
            em.op("act", lambda e: e.activation(out=Gg[:, tt, half * 512:(half + 1) * 512], in_=pg[:, :],
                                                func=AF.Silu),
                  reads=[pgb], writes=[Ggb[tt]])

        def attn_head(g, j, hp):
            t0, w = GROUPS[g]
            nb = w // 128
            n0 = t0 // 128
            h = j + 8 * hp
            r0 = 64 * hp
            sset = head_ctr[0] % 2
            head_ctr[0] += 1
            PTs, PTbs = PTset[sset], PTb[sset]
            raw = [(16, 0, w), (17, 0, w)]
            if g < 4:
                for m in range(max(0, n0 - 1), min(15, n0 + nb) + 1):
                    na = max(m - 1, n0)
                    nb_ = min(m + 1, n0 + nb - 1)
                    raw.append((m, (na - n0) * 128, (nb_ - n0 + 1) * 128))
            bins = []
            for (m, qa, qb) in sorted(raw, key=lambda t: -(t[2] - t[1])):
                wd = qb - qa
                for bn in bins:
                    if bn["used"] + wd <= 512:
                        bn["items"].append((m, qa, qb, bn["used"]))
                        bn["used"] += wd
                        break
                else:
                    bins.append({"used": wd, "items": [(m, qa, qb, 0)]})
            assert len(bins) <= 5
            KTp = KT0 if hp == 0 else KT1
            tiles = []
            for sl, bn in enumerate(bins):
                pss, pssb = ps_next()
                for (m, qa, qb, off) in bn["items"]:
                    em.op("pe", lambda e, m=m, qa=qa, qb=qb, off=off, pss=pss: e.matmul(
                        pss[:, off:off + qb - qa], lhsT=KTp[:, m * 128:(m + 1) * 128],
                        rhs=QTg[:, j, qa:qb], start=True, stop=True),
                        reads=[KTb[m], QTgb[j]], writes=[pssb])
                    tiles.append((sl, m, qa, qb, off))
                em.op("act", lambda e, sl=sl, used=bn["used"], pss=pss: e.activation(
                    out=PTs[sl][:, 0:used], in_=pss[:, 0:used], func=AF.Exp, scale=0.125),
                    reads=[pssb], writes=[PTbs[sl]])
                for (m, qa, qb, off) in bn["items"]:
                    if m >= 16:
                        continue
                    for n in range(n0 + qa // 128, n0 + qb // 128):
                        c0 = off + (n - n0) * 128 - qa
                        if n == m - 1:
                            mk = mle
                        elif n == m + 1:
                            mk = mge
                        else:
                            continue
                        mask_ctr[0] += 1
                        em.op("pool" if mask_ctr[0] % 2 else "dve", lambda e, sl=sl, c0=c0, mk=mk: e.tensor_tensor(
                            out=PTs[sl][:, c0:c0 + 128], in0=PTs[sl][:, c0:c0 + 128], in1=mk, op=ALU.mult),
                            reads=[PTbs[sl], Bc["mask"]], writes=[PTbs[sl]])
            return (g, j, hp, tiles, PTs, PTbs)

        def attn_pv(ctx):
            g, j, hp, tiles, PTs, PTbs = ctx
            t0, w = GROUPS[g]
            nb = w // 128
            h = j + 8 * hp
            po, pob = ps_next()
            pov = po[:, 0:4 * 65].rearrange("p (a b) -> p a b", a=4)
            for bi in range(nb):
                use = [(sl, m, qa, off) for (sl, m, qa, qb, off) in tiles if qa <= bi * 128 < qb]
                for ui, (sl, m, qa, off) in enumerate(use):
                    c0 = off + bi * 128 - qa
                    em.op("pe", lambda e, sl=sl, m=m, c0=c0, bi=bi, ui=ui, nu=len(use):
                          e.matmul(pov[:, bi, :], lhsT=PTs[sl][:, c0:c0 + 128], rhs=VA[:, m, hp, 0:65],
                                   start=(ui == 0), stop=(ui == nu - 1)),
                          reads=[PTbs[sl], VAb[m]], writes=[pob])
            ri = head_ctr[0] % 2
            em.op("dve", lambda e: e.tensor_scalar(
                out=rden[ri][:, 0:nb], in0=pov[:, 0:nb, 64], scalar1=sinkexp[:, h:h + 1], scalar2=None,
                op0=ALU.add),
                reads=[pob, Bc["sinkexp"]], writes=[rdenb[ri]])
            em.op("dve", lambda e: e.reciprocal(out=rden[ri][:, 4:4 + nb], in_=rden[ri][:, 0:nb]),
                  reads=[rdenb[ri]], writes=[rdenb[ri]])
            for bi in range(nb):
                em.op("dve", lambda e, bi=bi: e.scalar_tensor_tensor(
                    out=Gg[:, bi, h * 64:(h + 1) * 64], in0=pov[:, bi, 0:64], scalar=rden[ri][:, 4 + bi:5 + bi],
                    in1=Gg[:, bi, h * 64:(h + 1) * 64], op0=ALU.mult, op1=ALU.mult),
                    reads=[pob, rdenb[ri], Ggb[bi]], writes=[Ggb[bi]])

        def tr_chunk(g, k):
            t0, w = GROUPS[g]
            nb = w // 128
            pt, ptb = ps_next()
            for tt in range(nb):
                em.op("pe", lambda e, tt=tt: e.matmul(
                    pt[:, tt * 128:(tt + 1) * 128], lhsT=Gg[:, tt, k * 128:(k + 1) * 128], rhs=ident_bf,
                    start=True, stop=True),
                    reads=[Ggb[tt], Bc["ident"]], writes=[ptb])
            em.op("act", lambda e: e.activation(out=Hg[:, k, 0:w], in_=pt[:, 0:w], func=AF.Copy),
                  reads=[ptb], writes=[Hgb])

        def oproj(g, c):
            t0, w = GROUPS[g]
            s = 1 if g == 4 else 0
            pp, ppb = ps_next()
            for kc in range(8):
                em.op("pe", lambda e, kc=kc: e.matmul(
                    pp[:, 0:w], lhsT=WO0[:, kc, c * 128:(c + 1) * 128], rhs=Hg[:, kc, 0:w],
                    start=(kc == 0), stop=(kc == 7)),
                    reads=[WO0b, Hgb], writes=[ppb])
            em.op("dve", lambda e: e.scalar_tensor_tensor(
                out=XS[:, c, t0:t0 + w], in0=pp[:, 0:w], scalar=MOD[:, 16 + c, s:s + 1], in1=XS[:, c, t0:t0 + w],
                op0=ALU.mult, op1=ALU.add),
                reads=[ppb, Bc["MOD"], XSb[c][g]], writes=[XSb[c][g]])

        def pass_b(g):
            t0, w = GROUPS[g]
            nb = w // 128
            make_h(g, Hg, Hgb, tmpr, tmprb)
            if g < 4:
                load_rope(g, g % 2)
            pgens = [proj_fm(QTg[:, j, 0:w], [QTgb[j]], j * 128, g, g % 2) for j in range(8)]
            next(pgens[0], None)
            for j in range(8):
                if j + 1 < 8:
                    next(pgens[j + 1], None)
                for _ in pgens[j]:
                    pass
            mod1 = None
            if g == 3 and stage >= 1:
                mod1 = Modulation(wada1_d, vec1, Bc["vec1"], mring1, mring1bf, MOD1, GM1, Bc["MOD1"], Bc["GM1"],
                                  [ropeb[0], ropeb[1], qtmpb[0], qtmpb[1]], "1")
                mod1.dma(0)
                mod1.dma(1)
            for tt in range(nb):
                for half in range(2):
                    gproj(g, tt, half)
            hi = 0
            prev = None
            for j in range(8):
                for hp in range(2):
                    cur = attn_head(g, j, hp)
                    if prev is not None:
                        attn_pv(prev)
                    prev = cur
                    if mod1 is not None and hi < 12:
                        for pc in (2 * hi, 2 * hi + 1):
                            mod1.mm(pc)
                            if pc + 2 < 24:
                                mod1.dma(pc + 2)
                    hi += 1
            attn_pv(prev)
            if mod1 is not None:
                mod1.finish()
            for k in range(8):
                tr_chunk(g, k)
            for c in range(8):
                oproj(g, c)

        for g in range(5 if stop >= 3 else 0):
            pass_b(g)

        em.barrier()
        ar.release(layer_mark)

        if stage >= 1:
            build_layer1(nc, em, ar, locals())

        outs = []
        if stage == 0:
            for j in range(8):
                for g in range(4):
                    t0, w = GROUPS[g]
                    outs.append(em.dma("sp", y_d[:, j, t0:t0 + w], XS[:, j, t0:t0 + w], reads=[XSb[j][g]]))
        else:
            outs = L1_OUTS
        stats = em.emit(st, final_waits=outs)
        build_program.stats = (stats, ar.peak)
    return nc


L1_OUTS = []


def build_layer1(nc, em, ar, L):
    del L1_OUTS[:]
    XS, XSb, RSTD = L["XS"], L["XSb"], L["RSTD"]
    MOD, GM, Bc, vec1 = L["MOD1"], L["GM1"], L["Bc"], L["vec1"]
    ones_bf, ident_bf, mle, mge = L["ones_bf"], L["ident_bf"], L["mle"], L["mge"]
    ps_next = L["ps_next"]
    PSD, PSDb = L["PSD"], L["PSDb"]
    L["ring_n"][0] = 8
    EPS_AP = L["EPS_AP"]
    NEGHALF = L["NEGHALF"]
    win1_d, wout1_d, wa1_d, wa2_d, rmask_d, y_d = (L["win1_d"], L["wout1_d"], L["wa1_d"], L["wa2_d"],
                                                  L["rmask_d"], L["y_d"])
    KSCALE = 128.0 ** -0.5
    C16 = 1.0 / 16.0

    RS = RSTD[:, 0:512]
    RSb = em.buf("RS")
    RS2 = [RSTD[:, 0:512], RSTD[:, 1664:2176]]
    RS2b = [RSb, em.buf("RSx")]
    RTf = RSTD[:, 512:1664].bitcast(BF16)
    H = ar.bf(8, T)
    Hb = em.bufs("H", 5)
    Wqkv = ar.bf(8, 512)
    Wqkvb = em.buf("Wqkv")
    Wg = ar.bf(8, 256)
    Wgb = em.buf("Wg")
    WOh = ar.bf(2, 1024)
    WOhb = em.buf("WOh")
    WA1 = ar.bf(8, 32)
    NBA = ar.f32(8)
    rmask = ar.f32(512)
    Bw = {k: em.buf(k) for k in ["WA1", "WA2", "RT", "NBA", "rmask"]}
    XC = [XS[:, j, S:T] for j in range(8)]
    R = [[XC[0], XC[1]], [XC[2], XC[3]]]
    Sfb = [XC[4][:, 0:128].bitcast(BF16), XC[4][:, 128:256].bitcast(BF16)]
    AT = [[XC[5][:, 0:64].bitcast(BF16), XC[5][:, 64:128].bitcast(BF16)],
          [XC[5][:, 128:192].bitcast(BF16), XC[5][:, 192:256].bitcast(BF16)]]
    ATpair = [XC[5][:, 0:128].bitcast(BF16), XC[5][:, 128:256].bitcast(BF16)]
    mask2 = L["mask"]
    WA2 = [XC[6].bitcast(BF16), XC[7].bitcast(BF16)]
    head_mark = ar.mark()
    QD = [ar.bf(S), ar.bf(S)]
    KI = [ar.bf(T), ar.bf(T)]
    KItm = [ar.bf(NT, 128), ar.bf(NT, 128)]
    DEC = ar.f32(2, NT)
    V = ar.bf(NT, 256)
    SBST = ar.bf(16, 256)
    tA = [ar.f32(512), ar.f32(512)]
    tB = [ar.f32(512), ar.f32(512)]
    TOT = [ar.f32(4), ar.f32(4)]
    Og = [ar.f32(2, 512), ar.f32(2, 512)]
    RO = ar.f32(512)
    sq2 = [ar.bf(512), ar.bf(512)]
    _tx = ar.f32(512)
    tX = [_tx, _tx]
    OGT = ar.bf(2, 512)
    tmpr = [Og[0][:, 0, :], Og[1][:, 0, :]]
    sqr = sq2
    tmps = tX[0]
    names = ["R00", "R01", "R10", "R11", "Sfb0", "Sfb1",
             "AT00", "AT01", "AT10", "AT11", "tA0", "tA1", "tB0", "tB1", "TOT0", "TOT1",
             "Og0", "Og1", "RO", "sq0", "sq1", "tX0", "OGT"]
    Bh = {k: em.buf(k) for k in names}
    Bh["tm0"], Bh["tm1"], Bh["sr0"], Bh["sr1"], Bh["tms"] = Bh["Og0"], Bh["Og1"], Bh["sq0"], Bh["sq1"], Bh["tX0"]
    RTb = em.bufs("rt", 5)
    DECb = [[em.buf("dec%d_%d" % (d, g)) for g in range(5)] for d in range(2)]
    Vb = em.bufs("V", 5)
    KTb = [em.bufs("ktm0_", 5), em.bufs("ktm1_", 5)]
    KIb = [em.bufs("ki0_", 5), em.bufs("ki1_", 5)]
    QDb = [em.bufs("qd0_", 4), em.bufs("qd1_", 4)]
    SBb = em.bufs("sbst", 4)
    STOP = L["stop"]

    em.dma("pool", WA1, wa1_d, writes=[Bw["WA1"]])
    em.dma("sp", rmask, rmask_d, writes=[Bw["rmask"]])
    em.op("dve", lambda e: e.tensor_scalar(out=NBA, in0=vec1[:, 48:56], scalar1=-1.0, scalar2=None, op0=ALU.mult),
          reads=[Bc["vec1"]], writes=[Bw["NBA"]])

    def load_qkv(h):
        for (c_lo, n, dst_lo) in ((h * 128, 128, 0), (512 + h * 128, 128, 128), (1024 + h * 256, 256, 256)):
            em.dma("pool", Wqkv[:, :, dst_lo:dst_lo + n], win1_d[:, :, c_lo:c_lo + n], writes=[Wqkvb])

    def load_g(h):
        em.dma("pool", Wg, win1_d[:, :, 2048 + h * 256:2048 + (h + 1) * 256], writes=[Wgb])
        em.dma("pool", WOh, wout1_d[:, 2 * h:2 * h + 2, :], writes=[WOhb])

    load_qkv(0)
    load_g(0)

    def stats(src_fn, nchunk, inv_n, out_ap, outb, w, sq, sqb, tmp, tmpb):
        pss, pssb = ps_next()
        for j in range(nchunk):
            sl = j % 2
            sap, sb = src_fn(j)
            em.op("act", lambda e, sl=sl, sap=sap: e.activation(out=sq[sl][:, 0:w], in_=sap, func=AF.Square),
                  reads=[sb], writes=[sqb[sl]])
            em.op("pe", lambda e, sl=sl, j=j: e.matmul(pss[:, 0:w], lhsT=ones_bf, rhs=sq[sl][:, 0:w],
                                                      start=(j == 0), stop=(j == nchunk - 1)),
                  reads=[sqb[sl], Bc["ones"]], writes=[pssb])
        em.op("act", lambda e: e.activation(out=tmp[:, 0:w], in_=pss[:, 0:w], func=AF.Ln, bias=EPS_AP, scale=inv_n),
              reads=[pssb, Bc["eps"]], writes=[tmpb])
        em.op("act", lambda e: e.activation(out=out_ap, in_=tmp[:, 0:w], func=AF.Exp, scale=-0.5),
              reads=[tmpb], writes=[outb])

    def setup_group(g, ri):
        t0, w = GROUPS[g]
        s = 1 if g == 4 else 0
        RSg, RSgb = RS2[ri], RS2b[ri]
        for j in range(8):
            sl = j % 2
            em.op("dve", lambda e, j=j, sl=sl: e.tensor_tensor(out=tmpr[sl][:, 0:w], in0=XS[:, j, t0:t0 + w],
                                                               in1=RSg[:, 0:w], op=ALU.mult),
                  reads=[XSb[j][g], RSgb], writes=[Bh["tm%d" % sl]])
            em.op("act", lambda e, j=j, sl=sl: e.activation(out=H[:, j, t0:t0 + w], in_=tmpr[sl][:, 0:w],
                                                            func=AF.Identity, bias=MOD[:, j, s:s + 1],
                                                            scale=GM[:, j, s:s + 1]),
                  reads=[Bh["tm%d" % sl], Bc["MOD1"], Bc["GM1"]], writes=[Hb[g]])
        pr, prb = ps_next()
        for kc in range(8):
            em.op("pe", lambda e, kc=kc: e.matmul(pr[0:32, 0:w], lhsT=WA1[:, kc, 0:32], rhs=H[:, kc, t0:t0 + w],
                                                  start=(kc == 0), stop=(kc == 7)),
                  reads=[Bw["WA1"], Hb[g]], writes=[prb])
        em.op("act", lambda e: e.activation(out=RTf[0:32, t0:t0 + w], in_=pr[0:32, 0:w], func=AF.Copy),
              reads=[prb], writes=[RTb[g]])

    def setup_stats(g, ri):
        t0, w = GROUPS[g]
        stats(lambda j: (XS[:, j, t0:t0 + w], XSb[j][g]), 8, 1.0 / D, RS2[ri][:, 0:w], RS2b[ri], w,
              sqr, [Bh["sr0"], Bh["sr1"]], tmps, Bh["tms"])

    def prep_dir(h, g, d, P):
        t0, w = GROUPS[g]
        nch = w // 128
        c0 = t0 // 128
        lat = g < 4
        a_, b_, tot = tA[d], tB[d], TOT[d]
        ab, bb, totb = Bh["tA%d" % d], Bh["tB%d" % d], Bh["TOT%d" % d]
        pz, pzb = ps_next()
        em.op("pe", lambda e: e.matmul(pz[:, 0:w], lhsT=WA2[d][0:32, h * 128:(h + 1) * 128],
                                       rhs=RTf[0:32, t0:t0 + w], start=True, stop=True),
              reads=[Bw["WA2"], RTb[g]], writes=[pzb])
        em.op("act", lambda e: e.activation(out=a_[:, 0:w], in_=pz[:, 0:w], func=AF.Exp, scale=-1.0,
                                            bias=NBA[:, d * 4 + h:d * 4 + h + 1]),
              reads=[pzb, Bw["NBA"]], writes=[ab])
        yield
        em.op("act", lambda e: e.activation(out=a_[:, 0:w], in_=a_[:, 0:w], func=AF.Ln, bias=1.0),
              reads=[ab], writes=[ab])
        yield
        em.op("dve", lambda e: e.tensor_tensor_scan(out=b_[:, 0:w], data0=rmask[:, 0:w], data1=a_[:, 0:w],
                                                    initial=0.0, op0=ALU.mult, op1=ALU.add),
              reads=[ab, Bw["rmask"]], writes=[bb])
        yield
        em.op("act", lambda e: e.activation(out=DEC[:, d, c0:c0 + nch], in_=b_[:, 127:w:128], func=AF.Exp, scale=-C16),
              reads=[bb], writes=[DECb[d][g]])
        yield
        if d == 0:
            sc1, sc2 = -C16, C16
        else:
            em.op("dve", lambda e: e.tensor_copy(out=tot[:, 0:nch], in_=b_[:, 127:w:128]), reads=[bb], writes=[totb])
            b3 = b_[:, 0:w].rearrange("p (c t) -> p c t", t=128)
            em.op("dve", lambda e: e.tensor_tensor(out=b3, in0=b3,
                                                   in1=tot[:, 0:nch].unsqueeze(2).to_broadcast([128, nch, 128]),
                                                   op=ALU.subtract),
                  reads=[bb, totb], writes=[bb])
            em.op("dve", lambda e: e.tensor_tensor(out=b_[:, 0:w], in0=b_[:, 0:w], in1=a_[:, 0:w], op=ALU.subtract),
                  reads=[bb, ab], writes=[bb])
            sc1, sc2 = C16, -C16
        yield
        if lat:
            pq, pqb = P["pq"]
            em.op("act", lambda e: e.activation(out=a_[:, 0:w], in_=b_[:, 0:w], func=AF.Exp, scale=sc1),
                  reads=[bb], writes=[ab])
            em.op("dve", lambda e: e.scalar_tensor_tensor(
                out=QD[d][:, t0:t0 + w], in0=pq[:, 0:w], scalar=KSCALE, in1=a_[:, 0:w], op0=ALU.mult, op1=ALU.mult),
                reads=[pqb, ab], writes=[QDb[d][g]])
        yield
        pk, pkb = P["pk"]
        em.op("act", lambda e: e.activation(out=b_[:, 0:w], in_=b_[:, 0:w], func=AF.Exp, scale=sc2),
              reads=[bb], writes=[bb])
        em.op("dve", lambda e: e.tensor_tensor(out=KI[d][:, t0:t0 + w], in0=pk[:, 0:w], in1=b_[:, 0:w], op=ALU.mult),
              reads=[pkb, bb], writes=[KIb[d][g]])
        yield
        pt, ptb = ps_next()
        for c in range(nch):
            em.op("pe", lambda e, c=c: e.matmul(pt[:, c * 128:(c + 1) * 128],
                                                lhsT=KI[d][:, t0 + c * 128:t0 + (c + 1) * 128], rhs=ident_bf,
                                                start=True, stop=True),
                  reads=[KIb[d][g], Bc["ident"]], writes=[ptb])
        em.op("act", lambda e: e.activation(
            out=KItm[d][:, c0:c0 + nch, :], in_=pt[:, 0:w].rearrange("p (c t) -> p c t", t=128), func=AF.Copy),
            reads=[ptb], writes=[KTb[d][g]])

    def prep_group(h, g):
        t0, w = GROUPS[g]
        nch = w // 128
        lat = g < 4
        P = {}
        gens = [prep_dir(h, g, d, P) for d in range(2)]
        alive = [True, True]
        rnd = 0
        while any(alive):
            rnd += 1
            if rnd == 6 and lat:
                pq, pqb = ps_next()
                for kc in range(8):
                    em.op("pe", lambda e, kc=kc, pq=pq: e.matmul(pq[:, 0:w], lhsT=Wqkv[:, kc, 0:128],
                                                                 rhs=H[:, kc, t0:t0 + w], start=(kc == 0),
                                                                 stop=(kc == 7)),
                          reads=[Wqkvb, Hb[g]], writes=[pqb])
                P["pq"] = (pq, pqb)
            if rnd == 7:
                pk, pkb = ps_next()
                for kc in range(8):
                    em.op("pe", lambda e, kc=kc, pk=pk: e.matmul(pk[:, 0:w], lhsT=Wqkv[:, kc, 128:256],
                                                                 rhs=H[:, kc, t0:t0 + w], start=(kc == 0),
                                                                 stop=(kc == 7)),
                          reads=[Wqkvb, Hb[g]], writes=[pkb])
                P["pk"] = (pk, pkb)
            for d in range(2):
                if alive[d]:
                    try:
                        next(gens[d])
                    except StopIteration:
                        alive[d] = False
            yield
        for tt in range(nch):
            vproj(h, g, tt)
            yield

    def vproj(h, g, tt):
        t0, w = GROUPS[g]
        ti = t0 // 128 + tt
        pv, pvb = ps_next()
        for kc in range(8):
            em.op("pe", lambda e, kc=kc: e.matmul(pv[:, 0:256], lhsT=H[:, kc, ti * 128:(ti + 1) * 128],
                                                  rhs=Wqkv[:, kc, 256:512], start=(kc == 0), stop=(kc == 7)),
                  reads=[Wqkvb, Hb[g]], writes=[pvb])
        em.op("act", lambda e: e.activation(out=V[:, ti, :], in_=pv[:, 0:256], func=AF.Copy),
              reads=[pvb], writes=[Vb[g]])

    rstate = {0: [0, None, None], 1: [0, None, None]}

    def chain_reset(pair, d):
        rstate[pair] = [0, None, d]

    def chain_step(pair, c):
        cur, prev, d = rstate[pair]
        nxt = 1 - cur
        g = c // 4
        pkv, pkvb = ps_next()
        em.op("pe", lambda e: e.matmul(pkv[:, 0:256], lhsT=KItm[d][:, c, :], rhs=V[:, c, :], start=True, stop=True),
              reads=[KTb[d][g], Vb[g]], writes=[pkvb])
        if prev is None:
            em.op("dve", lambda e: e.tensor_copy(out=R[pair][nxt], in_=pkv[:, 0:256]),
                  reads=[pkvb], writes=[Bh["R%d%d" % (pair, nxt)]])
        else:
            em.op("dve", lambda e: e.scalar_tensor_tensor(out=R[pair][nxt], in0=R[pair][cur],
                                                          scalar=DEC[:, d, prev:prev + 1],
                                                          in1=pkv[:, 0:256], op0=ALU.mult, op1=ALU.add),
                  reads=[Bh["R%d%d" % (pair, cur)], DECb[d][prev // 4], pkvb], writes=[Bh["R%d%d" % (pair, nxt)]])
        rstate[pair][0] = nxt
        rstate[pair][1] = c

    def state_bf16(pair, out_ap, outb):
        cur, prev, d = rstate[pair]
        em.op("act", lambda e: e.activation(out=out_ap, in_=R[pair][cur], func=AF.Identity,
                                            scale=DEC[:, d, prev:prev + 1]),
              reads=[Bh["R%d%d" % (pair, cur)], DECb[d][prev // 4]], writes=[outb])

    def out_a(h, c):
        ai = c % 2
        g = c // 4
        cs = slice(c * 128, (c + 1) * 128)
        pa, pab = ps_next()
        for d in range(2):
            em.op("pe", lambda e, d=d: e.matmul(pa[:, d * 128:(d + 1) * 128], lhsT=KI[d][:, cs], rhs=QD[d][:, cs],
                                                start=True, stop=True),
                  reads=[KIb[d][g], QDb[d][g]], writes=[pab])
        em.op("dve", lambda e: e.tensor_tensor(out=ATpair[ai], in0=pa[:, 0:256], in1=mask2, op=ALU.mult),
              reads=[pab, Bc["mask"]], writes=[Bh["AT%d0" % ai], Bh["AT%d1" % ai]])

    def out_b(h, c, ds, do_chain):
        pair = h % 2
        dc = 1 - ds
        ai = c % 2
        g = c // 4
        cs = slice(c * 128, (c + 1) * 128)
        state_bf16(pair, Sfb[ai], Bh["Sfb%d" % ai])
        if do_chain:
            chain_step(pair, c)
        pos = [ps_next(), ps_next()]
        for ec in range(2):
            es = slice(ec * 128, (ec + 1) * 128)
            po, pob = pos[ec]
            em.op("pe", lambda e, es=es, po=po: e.matmul(po[:, 0:128], lhsT=V[:, c, es], rhs=AT[ai][0], start=True,
                                                         stop=False),
                  reads=[Vb[g], Bh["AT%d0" % ai]], writes=[pob])
            em.op("pe", lambda e, es=es, po=po: e.matmul(po[:, 0:128], lhsT=V[:, c, es], rhs=AT[ai][1], start=False,
                                                         stop=False),
                  reads=[Vb[g], Bh["AT%d1" % ai]], writes=[pob])
            em.op("pe", lambda e, es=es, po=po: e.matmul(po[:, 0:128], lhsT=SBST[:, c, es], rhs=QD[dc][:, cs],
                                                         start=False, stop=False),
                  reads=[SBb[g], QDb[dc][g]], writes=[pob])
        cc = c % 4
        oi = g % 2
        for ec in range(2):
            es = slice(ec * 128, (ec + 1) * 128)
            po, pob = pos[ec]
            em.op("pe", lambda e, es=es, po=po: e.matmul(po[:, 0:128], lhsT=Sfb[ai][:, es], rhs=QD[ds][:, cs],
                                                         start=False, stop=True),
                  reads=[Bh["Sfb%d" % ai], QDb[ds][g]], writes=[pob])
            em.op("act", lambda e, ec=ec, po=po: e.activation(out=Og[oi][:, ec, cc * 128:(cc + 1) * 128],
                                                              in_=po[:, 0:128], func=AF.Copy),
                  reads=[pob], writes=[Bh["Og%d" % oi]])

    def epilogue12(h, g):
        t0, w = GROUPS[g]
        oi = g % 2
        Ogg, Oggb = Og[oi], Bh["Og%d" % oi]
        stats(lambda j: (Ogg[:, j, :], Oggb), 2, 1.0 / 256.0, RO, Bh["RO"], 512,
              sq2, [Bh["sq0"], Bh["sq1"]], tX[0], Bh["tX0"])
        for ec in range(2):
            pg, pgb = ps_next()
            for kc in range(8):
                em.op("pe", lambda e, kc=kc, ec=ec, pg=pg: e.matmul(
                    pg[:, :], lhsT=Wg[:, kc, ec * 128:(ec + 1) * 128], rhs=H[:, kc, t0:t0 + w],
                    start=(kc == 0), stop=(kc == 7)),
                    reads=[Wgb, Hb[g]], writes=[pgb])
            em.op("dve", lambda e, ec=ec: e.scalar_tensor_tensor(
                out=Ogg[:, ec, :], in0=Ogg[:, ec, :], scalar=vec1[:, 32 + 2 * h + ec:33 + 2 * h + ec], in1=RO,
                op0=ALU.mult, op1=ALU.mult),
                reads=[Oggb, Bc["vec1"], Bh["RO"]], writes=[Oggb])
            em.op("act", lambda e, pg=pg, ec=ec: e.activation(out=tX[0], in_=pg[:, :], func=AF.Silu), reads=[pgb],
                  writes=[Bh["tX0"]])
            em.op("pool", lambda e, ec=ec: e.tensor_tensor(out=OGT[:, ec, :], in0=Ogg[:, ec, :], in1=tX[0],
                                                           op=ALU.mult),
                  reads=[Oggb, Bh["tX0"]], writes=[Bh["OGT"]])

    def epilogue3(h, g):
        t0, w = GROUPS[g]
        for cch in range(8):
            pp, ppb = ps_next()
            for ec in range(2):
                em.op("pe", lambda e, ec=ec, cch=cch, pp=pp: e.matmul(
                    pp[:, :], lhsT=WOh[:, ec, cch * 128:(cch + 1) * 128], rhs=OGT[:, ec, :],
                    start=(ec == 0), stop=(ec == 1)),
                    reads=[WOhb, Bh["OGT"]], writes=[ppb])
            em.op("dve", lambda e, cch=cch, pp=pp: e.scalar_tensor_tensor(
                out=XS[:, cch, t0:t0 + w], in0=pp[:, :], scalar=MOD[:, 16 + cch, 0:1], in1=XS[:, cch, t0:t0 + w],
                op0=ALU.mult, op1=ALU.add),
                reads=[ppb, Bc["MOD1"], XSb[cch][g]], writes=[XSb[cch][g]])

    def final_group(g):
        t0, w = GROUPS[g]
        stats(lambda j: (XS[:, j, t0:t0 + w], XSb[j][g]), 8, 1.0 / D, RS2[g % 2][:, 0:w], RS2b[g % 2], w,
              sq2, [Bh["sq0"], Bh["sq1"]], tX[0], Bh["tX0"])
        for j in range(8):
            em.op("dve", lambda e, j=j: e.scalar_tensor_tensor(
                out=XS[:, j, t0:t0 + w], in0=XS[:, j, t0:t0 + w], scalar=vec1[:, 40 + j:41 + j],
                in1=RS2[g % 2][:, 0:w], op0=ALU.mult, op1=ALU.mult),
                reads=[XSb[j][g], Bc["vec1"], RS2b[g % 2]], writes=[XSb[j][g]])
            L1_OUTS.append(em.dma("sp", y_d[:, j, t0:t0 + w], XS[:, j, t0:t0 + w], reads=[XSb[j][g]]))

    def chunks_of(g, d):
        if g == 4:
            return [16, 17] if d == 0 else [17, 16]
        cs = list(range(4 * g, 4 * g + 4))
        return cs if d == 0 else cs[::-1]

    def stored_chain_group(h, g):
        pair = h % 2
        dc = h % 2
        step = 1 if dc == 0 else -1
        for c in chunks_of(g, dc):
            chain_step(pair, c)
            if g == 4:
                nxt = (0 if dc == 0 else 15) if c == chunks_of(4, dc)[-1] else None
            else:
                nxt = c + step
            if nxt is not None and 0 <= nxt <= 15:
                state_bf16(pair, SBST[:, nxt, :], SBb[nxt // 4])
            yield

    def sweep_group(h, g, next_first):
        pair = h % 2
        ds = 1 - (h % 2)
        cs = chunks_of(g, ds)
        for i, c in enumerate(cs):
            nxt = cs[i + 1] if i + 1 < len(cs) else next_first
            if nxt is not None:
                out_a(h, nxt)
            last_overall = (c == (0 if ds == 1 else 15))
            out_b(h, c, ds, not last_overall)
            yield
            if i == 0 and pending_ep3:
                gg = pending_ep3.pop()
                epilogue3(h, gg)
                if h == 3:
                    final_group(gg)
                yield
        epilogue12(h, g)
        pending_ep3.append(g)
        yield

    pending_ep3 = []

    def drive(gens_a, gens_b, ratio=2):
        A = list(gens_a)
        B = list(gens_b)
        if STOP == 77:
            for g_ in A:
                for _ in g_:
                    pass
            em.barrier()
            for g_ in B:
                for _ in g_:
                    pass
            em.barrier()
            return
        ia = ib = 0
        while ia < len(A) or ib < len(B):
            for _ in range(ratio):
                if ib < len(B):
                    try:
                        next(B[ib])
                    except StopIteration:
                        ib += 1
            if ia < len(A):
                try:
                    next(A[ia])
                except StopIteration:
                    ia += 1

    def prep_order(h):
        return [4, 0, 1, 2, 3] if h % 2 == 0 else [4, 3, 2, 1, 0]

    chain_reset(0, 0)
    po0 = prep_order(0)
    setup_stats(po0[0], 0)
    for i, g in enumerate(po0):
        if i + 1 < 5:
            setup_stats(po0[i + 1], (i + 1) % 2)
        setup_group(g, i % 2)
        if i == 0:
            for d in range(2):
                em.dma("pool", WA2[d][0:32, :], wa2_d[:, d, :], writes=[Bw["WA2"], XSb[6][4], XSb[7][4]])
        if i >= 1:
            for _ in prep_group(0, po0[i - 1]):
                pass
        if i >= 2:
            for _ in stored_chain_group(0, po0[i - 2]):
                pass
    for _ in prep_group(0, po0[4]):
        pass
    for _ in stored_chain_group(0, po0[3]):
        pass
    for _ in stored_chain_group(0, po0[4]):
        pass

    for h in range(4):
        pair = h % 2
        ds = 1 - (h % 2)
        if h < 3:
            load_qkv(h + 1)
        chain_reset(pair, ds)
        for c in chunks_of(4, ds):
            chain_step(pair, c)
        sg = [3, 2, 1, 0] if ds == 1 else [0, 1, 2, 3]
        out_a(h, chunks_of(sg[0], ds)[0])
        if h < 3:
            chain_reset(1 - pair, (h + 1) % 2)
            pg = prep_order(h + 1)
            assert pg[1:] == sg
        for k in range(5):
            A = []
            if k < 4:
                nf = chunks_of(sg[k + 1], ds)[0] if k + 1 < 4 else None
                A = [sweep_group(h, sg[k], nf)]
            B = []
            if h < 3:
                B.append(prep_group(h + 1, pg[k]))
                if k > 0:
                    B.append(stored_chain_group(h + 1, pg[k - 1]))
            drive(A, B)
        if h < 3:
            for _ in stored_chain_group(h + 1, pg[4]):
                pass
        while pending_ep3:
            gg = pending_ep3.pop()
            epilogue3(h, gg)
            if h == 3:
                final_group(gg)
        if h < 3:
            load_g(h + 1)


def _kc_layout(w):
    n = w.shape[1]
    return np.ascontiguousarray(w.reshape(8, 128, n).transpose(1, 0, 2))


def _vec_layout(v):
    return np.ascontiguousarray(v.reshape(-1, 128).T)


def _wa2_layout(wf, wb):
    o = np.zeros((32, 2, 512), np.float32)
    o[0:16, 0, :] = wf
    o[16:32, 1, :] = wb
    return o


def _rope_tables():
    pos = np.arange(S)
    row = (pos // 64).astype(np.float64)
    col = (pos % 64).astype(np.float64)
    inv_freq = 10000.0 ** (-np.arange(16, dtype=np.float64) / 16.0)
    C = np.zeros((128, S), np.float32)
    Sn = np.zeros((128, S), np.float32)
    for r in range(128):
        d = r % 64
        a = d // 32
        f = d % 16
        ang = (row if a == 0 else col) * inv_freq[f]
        C[r] = np.cos(ang).astype(np.float32)
        Sn[r] = np.sin(ang).astype(np.float32)
    P = np.zeros((128, 128), np.float32)
    for m in range(128):
        d = m % 64
        jj = (d % 32) // 16
        if jj == 0:
            P[m + 16, m] = -1.0
        else:
            P[m - 16, m] = 1.0
    return C, Sn, P


def prepare_inputs(x, c, ctx, c_ctx, l0_norm_g, l0_w_ada, l0_b_ada, l0_w_in, l0_sink, l0_w_out,
                   l1_norm_g, l1_w_ada, l1_b_ada, l1_w_in, l1_wa1_f, l1_wa2_f, l1_ba_f,
                   l1_wa1_b, l1_wa2_b, l1_ba_b, l1_head_norm_g, l1_w_out, final_norm_g):
    f = lambda a: np.asarray(a, dtype=np.float32)
    x, c, ctx, c_ctx = f(x), f(c), f(ctx), f(c_ctx)
    C, Sn, P = _rope_tables()
    ii = np.arange(128)
    mask = np.concatenate([(ii[:, None] <= ii[None, :]), (ii[:, None] >= ii[None, :])], axis=1).astype(np.float32)
    rmask = np.ones((128, 512), np.float32)
    rmask[:, ::128] = 0.0
    qcols = []
    for j in range(8):
        for r in range(128):
            hh = j if r < 64 else 8 + j
            qcols.append(hh * 64 + (r % 64))
    w_in0 = f(l0_w_in)
    cols = np.concatenate([np.array(qcols), np.arange(1024, 1152), np.arange(1152, 1280), np.arange(1280, 2304)])
    shared = {
        "ropec": C, "ropes": Sn, "perm": P, "ident": np.eye(128, dtype=np.float32), "mask": mask, "rmask": rmask,
        "sink": np.ascontiguousarray(np.broadcast_to(f(l0_sink)[None, :], (128, 16))),
        "vec0": np.concatenate([_vec_layout(f(l0_norm_g)), _vec_layout(f(l0_b_ada))], axis=1),
        "vec1": np.concatenate([_vec_layout(f(l1_norm_g)), _vec_layout(f(l1_b_ada)), _vec_layout(f(l1_head_norm_g)),
                                _vec_layout(f(final_norm_g)), _vec_layout(f(l1_ba_f)), _vec_layout(f(l1_ba_b))],
                               axis=1),
        "wada0": _kc_layout(f(l0_w_ada)), "win0": _kc_layout(w_in0[:, cols]), "wout0": _kc_layout(f(l0_w_out)),
        "wada1": _kc_layout(f(l1_w_ada)), "win1": _kc_layout(f(l1_w_in)), "wout1": _kc_layout(f(l1_w_out)),
        "wa1": _kc_layout(np.concatenate([f(l1_wa1_f), f(l1_wa1_b)], axis=1)),
        "wa2": _wa2_layout(f(l1_wa2_f), f(l1_wa2_b)),
    }
    in_maps = []
    for b in range(8):
        cat = np.concatenate([x[b], ctx[b]], axis=0)
        xs = np.ascontiguousarray(cat.T.reshape(8, 128, T).transpose(1, 0, 2))
        ccb = np.ascontiguousarray(np.stack([_vec_layout(c[b]), _vec_layout(c_ctx)], axis=2))
        m = {"xs": xs, "cc": ccb}
        m.update(shared)
        in_maps.append(m)
    return in_maps


_NC_CACHE = {}


def kernel(**inputs):
    in_maps = prepare_inputs(**inputs)
    if "nc" not in _NC_CACHE:
        _NC_CACHE["nc"] = build_program(stage=2)
    nc = _NC_CACHE["nc"]
    res = run_bass_kernel_spmd(nc, in_maps, core_ids=list(range(8)))
    out = np.empty((8, S, D), np.float32)
    for b in range(8):
        y = res.results[b]["y"]
        out[b] = y.transpose(2, 1, 0).reshape(S, D)
    return out
```

```python
import contextlib
import numpy as np
import concourse.bass as bass
import concourse.mybir as mybir
from concourse.bass_utils import run_bass_kernel_spmd

F32 = mybir.dt.float32
BF16 = mybir.dt.bfloat16
AF = mybir.ActivationFunctionType
ALU = mybir.AluOpType

D = 1024
S = 2048
CTX = 256
T = S + CTX
NT = T // 128
EPS = 1e-6
GROUPS = [(0, 512), (512, 512), (1024, 512), (1536, 512), (2048, 256)]


class Buf:
    __slots__ = ("name", "w", "r")

    def __init__(self, name):
        self.name = name
        self.w = None
        self.r = []


class Op:
    __slots__ = ("eng", "fn", "deps", "kind", "sem", "val", "needed")

    def __init__(self, eng, fn, kind):
        self.eng = eng
        self.fn = fn
        self.kind = kind
        self.deps = []
        self.sem = None
        self.val = 0
        self.needed = False


class Em:
    ENGS = ("pe", "act", "dve", "pool", "sp")

    def __init__(self, nc, n_dma_sems=32):
        self.nc = nc
        self.ops = {e: [] for e in self.ENGS}
        self.n_dma_sems = n_dma_sems
        self.dma_last = [None] * n_dma_sems
        self.dma_cnt = [0] * n_dma_sems
        self.dma_rr = 0
        self.n_sw = 0
        self.pending_barrier = {}

    def buf(self, name="b"):
        return Buf(name)

    def bufs(self, name, n):
        return [Buf(f"{name}{i}") for i in range(n)]

    def _track(self, op, reads, writes):
        deps = op.deps
        for b in reads:
            if b.w is not None:
                deps.append(b.w)
        for b in writes:
            if b.w is not None:
                deps.append(b.w)
            deps.extend(b.r)
        for b in reads:
            b.r.append(op)
        for b in writes:
            b.w = op
            b.r = []

    def op(self, eng, fn, reads=(), writes=()):
        o = Op(eng, fn, "c")
        if self.pending_barrier.get(eng):
            o.deps.extend(self.pending_barrier.pop(eng))
        self._track(o, reads, writes)
        self.ops[eng].append(o)
        return o

    def dma(self, eng, out, in_, reads=(), writes=(), **kw):
        o = Op(eng, (lambda e, out=out, in_=in_, kw=kw: e.dma_start(out=out, in_=in_, **kw)), "d")
        if eng == "pool":
            k = self.n_dma_sems + self.n_sw
            self.n_sw += 1
            o.sem = ("d", k)
            o.val = 16
        else:
            k = self.dma_rr
            self.dma_rr = (self.dma_rr + 1) % self.n_dma_sems
            if self.dma_last[k] is not None:
                o.deps.append(self.dma_last[k])
            self.dma_cnt[k] += 1
            o.sem = ("d", k)
            o.val = 16 * self.dma_cnt[k]
            self.dma_last[k] = o
        if self.pending_barrier.get(eng):
            o.deps.extend(self.pending_barrier.pop(eng))
        self._track(o, reads, writes)
        self.ops[eng].append(o)
        return o

    def barrier(self):
        lasts = []
        for e in self.ENGS:
            if self.ops[e]:
                lasts.append(self.ops[e][-1])
        for d in self.dma_last:
            if d is not None:
                lasts.append(d)
        self.pending_barrier = {e: list(lasts) for e in self.ENGS}

    def emit(self, stack, final_waits=()):
        nc = self.nc
        esem = {e: stack.enter_context(nc.semaphore(f"s_{e}")) for e in ("pe", "act", "dve", "pool")}
        dsem = [stack.enter_context(nc.semaphore(f"d_{k}")) for k in range(self.n_dma_sems + self.n_sw)]
        for e in self.ENGS:
            for o in self.ops[e]:
                for d in o.deps:
                    if d.kind == "c":
                        if d.eng == "pe" and o.eng == "pe" and o.kind == "c":
                            continue
                        d.needed = True
        for o in final_waits:
            if o.kind == "c":
                o.needed = True
        for e in ("pe", "act", "dve", "pool"):
            c = 0
            for o in self.ops[e]:
                if o.kind == "c" and o.needed:
                    c += 1
                    o.sem = ("e", e)
                    o.val = c

        def semh(s):
            return esem[s[1]] if s[0] == "e" else dsem[s[1]]

        engmap = {"pe": "tensor", "act": "scalar", "dve": "vector", "pool": "gpsimd", "sp": "sync"}
        stats = {}
        block = stack.enter_context(nc.Block())

        def make(e):
            def body(eng):
                seen = {}
                nw = 0
                for o in self.ops[e]:
                    need = {}
                    for d in o.deps:
                        if d.kind == "c" and d.eng == "pe" and e == "pe" and o.kind == "c":
                            continue
                        if d.sem is None:
                            continue
                        if need.get(d.sem, 0) < d.val:
                            need[d.sem] = d.val
                    for s, v in need.items():
                        if seen.get(s, 0) < v:
                            eng.wait_ge(semh(s), v)
                            seen[s] = v
                            nw += 1
                    ins = o.fn(eng)
                    if o.kind == "d":
                        ins.then_inc(semh(o.sem), 16)
                    elif o.needed:
                        ins.then_inc(semh(o.sem), 1)
                if e == "sp":
                    need = {}
                    for d in final_waits:
                        if need.get(d.sem, 0) < d.val:
                            need[d.sem] = d.val
                    for s, v in need.items():
                        eng.wait_ge(semh(s), v)
                stats[e] = (len(self.ops[e]), nw)
            return body

        for e in self.ENGS:
            getattr(block, engmap[e])(make(e))
        return stats


class Arena:
    def __init__(self, big, nfl):
        self.big = big
        self.n = nfl
        self.off = 0
        self.peak = 0

    def _alloc(self, nfl):
        a = self.off
        self.off += nfl
        self.peak = max(self.peak, self.off)
        assert self.off <= self.n, f"arena overflow {self.off} > {self.n}"
        return a

    @staticmethod
    def _shape(ap, shape):
        if len(shape) == 1:
            return ap
        if len(shape) == 2:
            return ap.rearrange("p (a b) -> p a b", a=shape[0])
        if len(shape) == 3:
            return ap.rearrange("p (a b c) -> p a b c", a=shape[0], b=shape[1])
        raise ValueError

    def f32(self, *shape):
        n = int(np.prod(shape))
        a = self._alloc(n)
        return self._shape(self.big[:, a:a + n], shape)

    def bf(self, *shape):
        n = int(np.prod(shape))
        nfl = (n + 1) // 2
        a = self._alloc(nfl)
        return self._shape(self.big[:, a:a + nfl].bitcast(BF16)[:, 0:n], shape)

    def mark(self):
        return self.off

    def release(self, m):
        self.off = m


def build_program(stage=2, stop=99):
    nc = bass.Bass("TRN2", target_bir_lowering=False)

    def din(name, shape):
        return nc.dram_tensor(name, list(shape), F32, kind="ExternalInput").ap()

    xs_d = din("xs", [128, 8, T])
    cc_d = din("cc", [128, 8, 2])
    ropec_d = din("ropec", [128, S])
    ropes_d = din("ropes", [128, S])
    perm_d = din("perm", [128, 128])
    ident_d = din("ident", [128, 128])
    mask_d = din("mask", [128, 256])
    rmask_d = din("rmask", [128, 512])
    sink_d = din("sink", [128, 16])
    vec0_d = din("vec0", [128, 32])
    vec1_d = din("vec1", [128, 56])
    wada0_d = din("wada0", [128, 8, 3072])
    win0_d = din("win0", [128, 8, 2304])
    wout0_d = din("wout0", [128, 8, 1024])
    wada1_d = din("wada1", [128, 8, 3072])
    win1_d = din("win1", [128, 8, 3072])
    wout1_d = din("wout1", [128, 8, 1024])
    wa1_d = din("wa1", [128, 8, 32])
    wa2_d = din("wa2", [32, 2, 512])
    y_d = nc.dram_tensor("y", [128, 8, S], F32, kind="ExternalOutput").ap()

    em = Em(nc)
    NFL = 53000
    with contextlib.ExitStack() as st:
        big = st.enter_context(nc.sbuf_tensor("big", [128, NFL], F32))
        ar = Arena(big, NFL)
        NPS = 6
        PS = [st.enter_context(nc.psum_tensor(f"ps{i}", [128, 512], F32)) for i in range(NPS)]
        PSD = [st.enter_context(nc.psum_tensor(f"psd{i}", [128, 512], F32)) for i in range(2)]
        PSDb = em.bufs("psd", 2)
        PSB = None
        PSb = em.bufs("ps", NPS)
        PSBb = [em.buf("psbA"), em.buf("psbB")]
        ps_rr = [0]
        psb_rr = [0]

        ring_n = [NPS + 2]
        PSall = PS + PSD
        PSallb = PSb + PSDb

        def ps_next():
            i = ps_rr[0] % ring_n[0]
            ps_rr[0] = (i + 1) % ring_n[0]
            return PSall[i], PSallb[i]

        def psb_next():
            i = psb_rr[0]
            psb_rr[0] = (i + 1) % 2
            return PSB[:, i * 512:(i + 1) * 512], PSBb[i]

        XS = ar.f32(8, T)
        XSb = [[em.buf(f"xs{j}_{g}") for g in range(5)] for j in range(8)]
        RSTD = ar.f32(T)
        RSTDb = em.bufs("rstd", 5)
        cc = ar.f32(8, 2)
        scb = ar.bf(8, 2)
        MOD = ar.f32(24, 2)
        GM = ar.f32(8, 2)
        MOD0, GM0 = MOD, GM
        MOD1 = ar.f32(24, 2)
        GM1 = ar.f32(8, 2)
        modstage = ar.f32(48)
        vec0 = ar.f32(32)
        vec1 = ar.f32(56)
        ones_bf = ar.bf(128)
        ident_bf = ar.bf(128)
        mask = ar.f32(256)
        sinkexp = ar.f32(16)
        Bc = {k: em.buf(k) for k in ["cc", "scb", "MOD", "GM", "vec0", "vec1", "ones", "ident", "mask",
                                     "sinkexp", "perm", "rmask", "nba", "MOD1", "GM1", "modstage"]}
        mle = mask[:, 0:128]
        mge = mask[:, 128:256]

        for g, (t0, w) in enumerate(GROUPS):
            for j in range(8):
                em.dma("sp", XS[:, j, t0:t0 + w], xs_d[:, j, t0:t0 + w], writes=[XSb[j][g]])
        em.dma("sp", cc, cc_d, writes=[Bc["cc"]])
        em.dma("sp", vec0, vec0_d, writes=[Bc["vec0"]])
        em.dma("sp", vec1, vec1_d, writes=[Bc["vec1"]])
        em.dma("sp", mask, mask_d, writes=[Bc["mask"]])
        em.dma("sp", sinkexp, sink_d, writes=[Bc["sinkexp"]])
        em.dma("pool", ident_bf, ident_d, writes=[Bc["ident"]])
        em.op("pool", lambda e: e.memset(ones_bf, 1.0), writes=[Bc["ones"]])
        em.op("act", lambda e: e.activation(out=sinkexp, in_=sinkexp, func=AF.Exp),
              reads=[Bc["sinkexp"]], writes=[Bc["sinkexp"]])
        em.op("act", lambda e: e.activation(out=scb, in_=cc, func=AF.Silu), reads=[Bc["cc"]], writes=[Bc["scb"]])

        class Modulation:
            NP = 24

            def __init__(self, wada_d, vec, vecb, ring32, ringbf, MOD, GM, MODb, GMb, alias_bufs, tag):
                self.a = (wada_d, vec, vecb, ring32, ringbf, MOD, GM, MODb, GMb)
                self.alias = list(alias_bufs)
                self.b32 = em.bufs("mr32" + tag, 2)
                self.bbf = em.bufs("mrbf" + tag, 2)
                self.first = [True, True]

            def dma(self, pc):
                wada_d, vec, vecb, ring32, ringbf, MOD, GM, MODb, GMb = self.a
                slot = pc % 2
                extra = self.alias if self.first[slot] else []
                self.first[slot] = False
                em.dma("sp", ring32[slot], wada_d[:, :, pc * 128:(pc + 1) * 128], writes=[self.b32[slot]] + extra)

            def mm(self, pc):
                wada_d, vec, vecb, ring32, ringbf, MOD, GM, MODb, GMb = self.a
                slot = pc % 2
                em.op("dve", lambda e: e.tensor_copy(out=ringbf[slot], in_=ring32[slot]),
                      reads=[self.b32[slot]], writes=[self.bbf[slot]])
                pm, pmb = ps_next()
                for kc in range(8):
                    em.op("pe", lambda e, kc=kc: e.matmul(pm[:, 0:2], lhsT=ringbf[slot][:, kc, :], rhs=scb[:, kc, :],
                                                          start=(kc == 0), stop=(kc == 7)),
                          reads=[self.bbf[slot], Bc["scb"]], writes=[pmb])
                em.op("act", lambda e: e.activation(out=modstage[:, pc * 2:(pc + 1) * 2], in_=pm[:, 0:2], func=AF.Copy),
                      reads=[pmb], writes=[Bc["modstage"]])

            def finish(self):
                wada_d, vec, vecb, ring32, ringbf, MOD, GM, MODb, GMb = self.a
                pmv = modstage.rearrange("p (a b) -> p a b", a=24)
                for s in range(2):
                    em.op("dve", lambda e, s=s: e.tensor_tensor(out=MOD[:, :, s], in0=pmv[:, :, s], in1=vec[:, 8:32],
                                                                op=ALU.add),
                          reads=[Bc["modstage"], vecb], writes=[MODb])
                for s in range(2):
                    em.op("dve", lambda e, s=s: e.scalar_tensor_tensor(out=GM[:, :, s], in0=MOD[:, 8:16, s],
                                                                       scalar=1.0, in1=vec[:, 0:8], op0=ALU.add,
                                                                       op1=ALU.mult),
                          reads=[MODb, vecb] + self.b32 + self.bbf, writes=[GMb] + self.alias)

            def run_all(self):
                self.dma(0)
                self.dma(1)
                for pc in range(self.NP):
                    self.mm(pc)
                    if pc + 2 < self.NP:
                        self.dma(pc + 2)
                self.finish()

        def norm_stats(g, sqr, sqrb, tmp, tmpb, src=None, srcb=None, nchunk=8, inv_n=1.0 / D, out=None, outb=None):
            t0, w = GROUPS[g]
            pss, pssb = ps_next()
            for j in range(nchunk):
                sl = j % 2
                if src is None:
                    sap, sb = XS[:, j, t0:t0 + w], XSb[j][g]
                else:
                    sap, sb = src[:, j, 0:w], srcb
                em.op("act", lambda e, sl=sl, sap=sap: e.activation(out=sqr[sl][:, 0:w], in_=sap, func=AF.Square),
                      reads=[sb], writes=[sqrb[sl]])
                em.op("pe", lambda e, sl=sl, j=j: e.matmul(pss[:, 0:w], lhsT=ones_bf, rhs=sqr[sl][:, 0:w],
                                                          start=(j == 0), stop=(j == nchunk - 1)),
                      reads=[sqrb[sl], Bc["ones"]], writes=[pssb])
            em.op("act", lambda e: e.activation(out=tmp[:, 0:w], in_=pss[:, 0:w], func=AF.Ln, bias=EPS_AP,
                                                scale=inv_n),
                  reads=[pssb, Bc["eps"]], writes=[tmpb])
            if out is None:
                oap, ob = RSTD[:, t0:t0 + w], RSTDb[g]
            else:
                oap, ob = out[:, 0:w], outb
            em.op("act", lambda e: e.activation(out=oap, in_=tmp[:, 0:w], func=AF.Exp, scale=-0.5),
                  reads=[tmpb], writes=[ob])

        def make_h(g, Hdst, Hb, tmpr, tmprb):
            t0, w = GROUPS[g]
            s = 1 if g == 4 else 0
            for j in range(8):
                sl = j % 2
                em.op("dve", lambda e, j=j, sl=sl: e.tensor_tensor(out=tmpr[sl][:, 0:w], in0=XS[:, j, t0:t0 + w],
                                                                   in1=RSTD[:, t0:t0 + w], op=ALU.mult),
                      reads=[XSb[j][g], RSTDb[g]], writes=[tmprb[sl]])
                em.op("act", lambda e, j=j, sl=sl: e.activation(out=Hdst[:, j, 0:w], in_=tmpr[sl][:, 0:w],
                                                                func=AF.Identity, bias=MOD[:, j, s:s + 1],
                                                                scale=GM[:, j, s:s + 1]),
                      reads=[tmprb[sl], Bc["MOD"], Bc["GM"]], writes=(Hb if isinstance(Hb, list) else [Hb]))

        epsT = ar.f32(2)
        Bc["eps"] = em.buf("eps")
        EPS_AP = epsT[:, 0:1]
        NEGHALF = epsT[:, 1:2]
        em.op("pool", lambda e: e.memset(epsT[:, 0:1], EPS), writes=[Bc["eps"]])
        em.op("pool", lambda e: e.memset(epsT[:, 1:2], -0.5), writes=[Bc["eps"]])

        layer_mark = ar.mark()

        W0 = ar.bf(8, 2304)
        W0b = em.buf("W0")
        WO0 = ar.bf(8, 1024)
        WO0b = em.buf("WO0")
        KT0 = ar.bf(T)
        KT1 = ar.bf(T)
        KTb = em.bufs("kt", NT)
        VA = ar.bf(NT, 2, 66)
        VAb = em.bufs("va", NT)
        Hg = ar.bf(8, 512)
        Hgb = em.buf("Hg")
        alias_mark = ar.mark()
        QTg = ar.bf(8, 512)
        QTgb = em.bufs("qtg", 8)
        Gg = ar.bf(4, 1024)
        Ggb = em.bufs("gg", 4)
        alias_end = ar.mark()
        ar.release(alias_mark)
        mring = [ar.f32(8, 128) for _ in range(2)]
        mringbf = [ar.bf(8, 128) for _ in range(2)]
        assert ar.mark() <= alias_end
        ar.release(alias_end)
        rope_mark = ar.mark()
        ropeC = [ar.f32(512) for _ in range(2)]
        ropeS = [ar.f32(512) for _ in range(2)]
        ropeb = em.bufs("rope", 2)
        perm = ar.f32(128)
        qtmp = [ar.f32(512), ar.f32(512)]
        qtmpb = em.bufs("qtmp", 2)
        _t0 = ar.f32(512)
        t1r = [_t0, _t0]
        _tb = em.buf("t1r")
        t1rb = [_tb, _tb]
        t2r0 = ar.f32(512)
        t2r = [t2r0, t2r0]
        _save = ar.mark()
        ar.release(rope_mark)
        mring1 = [ar.f32(8, 128) for _ in range(2)]
        ar._alloc(128)
        mring1bf = [ar.bf(8, 128) for _ in range(2)]
        assert ar.mark() <= _save
        ar.release(_save)
        t2rb0 = em.buf("t2r")
        t2rb = [t2rb0, t2rb0]
        tmpr = [ar.f32(512) for _ in range(2)]
        tmprb = em.bufs("tmpr", 2)
        sqr = [ar.bf(512) for _ in range(2)]
        sqrb = em.bufs("sqr", 2)
        rden = [ar.f32(8) for _ in range(2)]
        rdenb = em.bufs("rden", 2)
        PTset = []
        PTb = []
        for sset in range(2):
            PTset.append([ar.bf(512) for _ in range(5)])
            PTb.append(em.bufs(f"pt{sset}_", 5))

        em.dma("sp", perm, perm_d, writes=[Bc["perm"]])
        for g in range(5):
            norm_stats(g, sqr, sqrb, tmpr[0], tmprb[0])
        Modulation(wada0_d, vec0, Bc["vec0"], mring, mringbf, MOD, GM, Bc["MOD"], Bc["GM"],
                   QTgb + Ggb[0:2], "0").run_all()
        for kc in range(8):
            for hf in range(2):
                em.dma("pool", W0[:, kc, hf * 1152:(hf + 1) * 1152], win0_d[:, kc, hf * 1152:(hf + 1) * 1152],
                       writes=[W0b])
        for kc in range(8):
            em.dma("pool", WO0[:, kc, :], wout0_d[:, kc, :], writes=[WO0b])
        em.op("pool", lambda e: e.memset(VA[:, :, :, 64:65], 1.0), writes=VAb)
        em.op("pool", lambda e: e.memset(KT0[64:128, :], 0.0), writes=KTb)
        em.op("pool", lambda e: e.memset(KT1[0:64, :], 0.0), writes=KTb)

        def load_rope(g, sl):
            t0, w = GROUPS[g]
            em.dma("sp", ropeC[sl][:, 0:w], ropec_d[:, t0:t0 + w], writes=[ropeb[sl]])
            em.dma("sp", ropeS[sl][:, 0:w], ropes_d[:, t0:t0 + w], writes=[ropeb[sl]])

        rope_ctr = [0]

        def proj_fm(dst, dstb, wcols, g, rope_sl, Hs=None, Hsb=None, split=None):
            t0, w = GROUPS[g]
            Hs = Hg if Hs is None else Hs
            Hsb = [Hgb] if Hsb is None else Hsb
            pq, pqb = ps_next()
            for kc in range(8):
                em.op("pe", lambda e, kc=kc: e.matmul(pq[:, 0:w], lhsT=W0[:, kc, wcols:wcols + 128],
                                                      rhs=Hs[:, kc, 0:w], start=(kc == 0), stop=(kc == 7)),
                      reads=[W0b] + Hsb, writes=[pqb])
            if g == 4:
                if split is None:
                    em.op("act", lambda e: e.activation(out=dst, in_=pq[:, 0:w], func=AF.Copy), reads=[pqb],
                          writes=dstb)
                else:
                    for (lo, hi, dd) in ((0, 64, split[0]), (64, 128, split[1])):
                        em.op("act", lambda e, lo=lo, hi=hi, dd=dd: e.activation(
                            out=dd[lo:hi, t0:t0 + w], in_=pq[lo:hi, 0:w], func=AF.Copy), reads=[pqb], writes=dstb)
                return
            i = rope_ctr[0] % 2
            rope_ctr[0] += 1
            em.op("act", lambda e: e.activation(out=qtmp[i][:, 0:w], in_=pq[:, 0:w], func=AF.Copy),
                  reads=[pqb], writes=[qtmpb[i]])
            yield
            pr, prb = ps_next()
            em.op("pe", lambda e: e.matmul(pr[:, 0:w], lhsT=perm, rhs=qtmp[i][:, 0:w], start=True, stop=True),
                  reads=[qtmpb[i], Bc["perm"]], writes=[prb])
            em.op("pool", lambda e: e.tensor_tensor(out=t1r[i][:, 0:w], in0=qtmp[i][:, 0:w],
                                                    in1=ropeC[rope_sl][:, 0:w], op=ALU.mult),
                  reads=[qtmpb[i], ropeb[rope_sl]], writes=[t1rb[i]])
            em.op("dve", lambda e: e.tensor_tensor(out=t2r[i][:, 0:w], in0=pr[:, 0:w], in1=ropeS[rope_sl][:, 0:w],
                                                   op=ALU.mult),
                  reads=[prb, ropeb[rope_sl]], writes=[t2rb[i]])
            if split is None:
                em.op("pool", lambda e: e.tensor_tensor(out=dst, in0=t1r[i][:, 0:w], in1=t2r[i][:, 0:w], op=ALU.add),
                      reads=[t1rb[i], t2rb[i]], writes=dstb)
            else:
                for (lo, hi, dd) in ((0, 64, split[0]), (64, 128, split[1])):
                    em.op("pool", lambda e, lo=lo, hi=hi, dd=dd: e.tensor_tensor(
                        out=dd[lo:hi, t0:t0 + w], in0=t1r[i][lo:hi, 0:w], in1=t2r[i][lo:hi, 0:w], op=ALU.add),
                        reads=[t1rb[i], t2rb[i]], writes=dstb)

        def vproj(g, tt, Hs, Hsb):
            t0, w = GROUPS[g]
            ti = t0 // 128 + tt
            pv, pvb = ps_next()
            for kc in range(8):
                em.op("pe", lambda e, kc=kc: e.matmul(pv[:, 0:128], lhsT=Hs[:, kc, tt * 128:(tt + 1) * 128],
                                                      rhs=W0[:, kc, 1152:1280], start=(kc == 0), stop=(kc == 7)),
                      reads=[W0b] + Hsb, writes=[pvb])
            em.op("dve", lambda e: e.tensor_copy(
                out=VA[:, ti, :, 0:64], in_=pv[:, 0:128].rearrange("p (a b) -> p a b", a=2)),
                reads=[pvb], writes=[VAb[ti]])

        def pass_a(g):
            t0, w = GROUPS[g]
            if g % 2 == 0:
                Hs, Hsb = Hg, [Hgb]
            else:
                Hs, Hsb = QTg, QTgb
            make_h(g, Hs, Hsb, tmpr, tmprb)
            if g < 4:
                load_rope(g, g % 2)
            kgen = proj_fm(None, KTb[t0 // 128:(t0 + w) // 128], 1024, g, g % 2, Hs, Hsb, split=(KT0, KT1))
            next(kgen, None)
            for tt in range(w // 128):
                vproj(g, tt, Hs, Hsb)
            for _ in kgen:
                pass

        for g in range(5 if stop >= 2 else 0):
            pass_a(g)

        head_ctr = [0]
        mask_ctr = [0]

        def gproj(g, tt, half):
            pg, pgb = ps_next()
            for kc in range(8):
                em.op("pe", lambda e, kc=kc: e.matmul(
                    pg[:, :], lhsT=Hg[:, kc, tt * 128:(tt + 1) * 128],
                    rhs=W0[:, kc, 1280 + half * 512:1280 + (half + 1) * 512], start=(kc == 0), stop=(kc == 7)),
                    reads=[W0b, Hgb], writes=[pgb])
            em.op("act", lambda e: e.activation(out=Gg[:, tt, half * 512:(half + 1) * 512], in_=pg[:, :],
                                                func=AF.Silu),
                  reads=[pgb], writes=[Ggb[tt]])

        def attn_head(g, j, hp):
            t0, w = GROUPS[g]
            nb = w // 128
            n0 = t0 // 128
            h = j + 8 * hp
            r0 = 64 * hp
            sset = head_ctr[0] % 2
            head_ctr[0] += 1
            PTs, PTbs = PTset[sset], PTb[sset]
            raw = [(16, 0, w), (17, 0, w)]
            if g < 4:
                for m in range(max(0, n0 - 1), min(15, n0 + nb) + 1):
                    na = max(m - 1, n0)
                    nb_ = min(m + 1, n0 + nb - 1)
                    raw.append((m, (na - n0) * 128, (nb_ - n0 + 1) * 128))
            bins = []
            for (m, qa, qb) in sorted(raw, key=lambda t: -(t[2] - t[1])):
                wd = qb - qa
                for bn in bins:
                    if bn["used"] + wd <= 512:
                        bn["items"].append((m, qa, qb, bn["used"]))
                        bn["used"] += wd
                        break
                else:
                    bins.append({"used": wd, "items": [(m, qa, qb, 0)]})
            assert len(bins) <= 5
            KTp = KT0 if hp == 0 else KT1
            tiles = []
            for sl, bn in enumerate(bins):
                pss, pssb = ps_next()
                for (m, qa, qb, off) in bn["items"]:
                    em.op("pe", lambda e, m=m, qa=qa, qb=qb, off=off, pss=pss: e.matmul(
                        pss[:, off:off + qb - qa], lhsT=KTp[:, m * 128:(m + 1) * 128],
                        rhs=QTg[:, j, qa:qb], start=True, stop=True),
                        reads=[KTb[m], QTgb[j]], writes=[pssb])
                    tiles.append((sl, m, qa, qb, off))
                em.op("act", lambda e, sl=sl, used=bn["used"], pss=pss: e.activation(
                    out=PTs[sl][:, 0:used], in_=pss[:, 0:used], func=AF.Exp, scale=0.125),
                    reads=[pssb], writes=[PTbs[sl]])
                for (m, qa, qb, off) in bn["items"]:
                    if m >= 16:
                        continue
                    for n in range(n0 + qa // 128, n0 + qb // 128):
                        c0 = off + (n - n0) * 128 - qa
                        if n == m - 1:
                            mk = mle
                        elif n == m + 1:
                            mk = mge
                        else:
                            continue
                        mask_ctr[0] += 1
                        em.op("pool" if mask_ctr[0] % 2 else "dve", lambda e, sl=sl, c0=c0, mk=mk: e.tensor_tensor(
                            out=PTs[sl][:, c0:c0 + 128], in0=PTs[sl][:, c0:c0 + 128], in1=mk, op=ALU.mult),
                            reads=[PTbs[sl], Bc["mask"]], writes=[PTbs[sl]])
            return (g, j, hp, tiles, PTs, PTbs)

        def attn_pv(ctx):
            g, j, hp, tiles, PTs, PTbs = ctx
            t0, w = GROUPS[g]
            nb = w // 128
            h = j + 8 * hp
            po, pob = ps_next()
            pov = po[:, 0:4 * 65].rearrange("p (a b) -> p a b", a=4)
            for bi in range(nb):
                use = [(sl, m, qa, off) for (sl, m, qa, qb, off) in tiles if qa <= bi * 128 < qb]
                for ui, (sl, m, qa, off) in enumerate(use):
                    c0 = off + bi * 128 - qa
                    em.op("pe", lambda e, sl=sl, m=m, c0=c0, bi=bi, ui=ui, nu=len(use):
                          e.matmul(pov[:, bi, :], lhsT=PTs[sl][:, c0:c0 + 128], rhs=VA[:, m, hp, 0:65],
                                   start=(ui == 0), stop=(ui == nu - 1)),
                          reads=[PTbs[sl], VAb[m]], writes=[pob])
            ri = head_ctr[0] % 2
            em.op("dve", lambda e: e.tensor_scalar(
                out=rden[ri][:, 0:nb], in0=pov[:, 0:nb, 64], scalar1=sinkexp[:, h:h + 1], scalar2=None,
                op0=ALU.add),
                reads=[pob, Bc["sinkexp"]], writes=[rdenb[ri]])
            em.op("dve", lambda e: e.reciprocal(out=rden[ri][:, 4:4 + nb], in_=rden[ri][:, 0:nb]),
                  reads=[rdenb[ri]], writes=[rdenb[ri]])
            for bi in range(nb):
                em.op("dve", lambda e, bi=bi: e.scalar_tensor_tensor(
                    out=Gg[:, bi, h * 64:(h + 1) * 64], in0=pov[:, bi, 0:64], scalar=rden[ri][:, 4 + bi:5 + bi],
                    in1=Gg[:, bi, h * 64:(h + 1) * 64], op0=ALU.mult, op1=ALU.mult),
                    reads=[pob, rdenb[ri], Ggb[bi]], writes=[Ggb[bi]])

        def tr_chunk(g, k):
            t0, w = GROUPS[g]
            nb = w // 128
            pt, ptb = ps_next()
            for tt in range(nb):
                em.op("pe", lambda e, tt=tt: e.matmul(
                    pt[:, tt * 128:(tt + 1) * 128], lhsT=Gg[:, tt, k * 128:(k + 1) * 128], rhs=ident_bf,
                    start=True, stop=True),
                    reads=[Ggb[tt], Bc["ident"]], writes=[ptb])
            em.op("act", lambda e: e.activation(out=Hg[:, k, 0:w], in_=pt[:, 0:w], func=AF.Copy),
                  reads=[ptb], writes=[Hgb])

        def oproj(g, c):
            t0, w = GROUPS[g]
            s = 1 if g == 4 else 0
            pp, ppb = ps_next()
            for kc in range(8):
                em.op("pe", lambda e, kc=kc: e.matmul(
                    pp[:, 0:w], lhsT=WO0[:, kc, c * 128:(c + 1) * 128], rhs=Hg[:, kc, 0:w],
                    start=(kc == 0), stop=(kc == 7)),
                    reads=[WO0b, Hgb], writes=[ppb])
            em.op("dve", lambda e: e.scalar_tensor_tensor(
                out=XS[:, c, t0:t0 + w], in0=pp[:, 0:w], scalar=MOD[:, 16 + c, s:s + 1], in1=XS[:, c, t0:t0 + w],
                op0=ALU.mult, op1=ALU.add),
                reads=[ppb, Bc["MOD"], XSb[c][g]], writes=[XSb[c][g]])

        def pass_b(g):
            t0, w = GROUPS[g]
            nb = w // 128
            make_h(g, Hg, Hgb, tmpr, tmprb)
            if g < 4:
                load_rope(g, g % 2)
            pgens = [proj_fm(QTg[:, j, 0:w], [QTgb[j]], j * 128, g, g % 2) for j in range(8)]
            next(pgens[0], None)
            for j in range(8):
                if j + 1 < 8:
                    next(pgens[j + 1], None)
                for _ in pgens[j]:
                    pass
            mod1 = None
            if g == 3 and stage >= 1:
                mod1 = Modulation(wada1_d, vec1, Bc["vec1"], mring1, mring1bf, MOD1, GM1, Bc["MOD1"], Bc["GM1"],
                                  [ropeb[0], ropeb[1], qtmpb[0], qtmpb[1]], "1")
                mod1.dma(0)
                mod1.dma(1)
            for tt in range(nb):
                for half in range(2):
                    gproj(g, tt, half)
            hi = 0
            prev = None
            for j in range(8):
                for hp in range(2):
                    cur = attn_head(g, j, hp)
                    if prev is not None:
                        attn_pv(prev)
                    prev = cur
                    if mod1 is not None and hi < 12:
                        for pc in (2 * hi, 2 * hi + 1):
                            mod1.mm(pc)
                            if pc + 2 < 24:
                                mod1.dma(pc + 2)
                    hi += 1
            attn_pv(prev)
            if mod1 is not None:
                mod1.finish()
            for k in range(8):
                tr_chunk(g, k)
            for c in range(8):
                oproj(g, c)

        for g in range(5 if stop >= 3 else 0):
            pass_b(g)

        em.barrier()
        ar.release(layer_mark)

        if stage >= 1:
            build_layer1(nc, em, ar, locals())

        outs = []
        if stage == 0:
            for j in range(8):
                for g in range(4):
                    t0, w = GROUPS[g]
                    outs.append(em.dma("sp", y_d[:, j, t0:t0 + w], XS[:, j, t0:t0 + w], reads=[XSb[j][g]]))
        else:
            outs = L1_OUTS
        stats = em.emit(st, final_waits=outs)
        build_program.stats = (stats, ar.peak)
    return nc


L1_OUTS = []


def build_layer1(nc, em, ar, L):
    del L1_OUTS[:]
    XS, XSb, RSTD = L["XS"], L["XSb"], L["RSTD"]
    MOD, GM, Bc, vec1 = L["MOD1"], L["GM1"], L["Bc"], L["vec1"]
    ones_bf, ident_bf, mle, mge = L["ones_bf"], L["ident_bf"], L["mle"], L["mge"]
    ps_next = L["ps_next"]
    PSD, PSDb = L["PSD"], L["PSDb"]
    L["ring_n"][0] = 8
    EPS_AP = L["EPS_AP"]
    NEGHALF = L["NEGHALF"]
    win1_d, wout1_d, wa1_d, wa2_d, rmask_d, y_d = (L["win1_d"], L["wout1_d"], L["wa1_d"], L["wa2_d"],
                                                  L["rmask_d"], L["y_d"])
    KSCALE = 128.0 ** -0.5
    C16 = 1.0 / 16.0

    RS = RSTD[:, 0:512]
    RSb = em.buf("RS")
    RS2 = [RSTD[:, 0:512], RSTD[:, 1664:2176]]
    RS2b = [RSb, em.buf("RSx")]
    RTf = RSTD[:, 512:1664].bitcast(BF16)
    H = ar.bf(8, T)
    Hb = em.bufs("H", 5)
    Wqkv = ar.bf(8, 512)
    Wqkvb = em.buf("Wqkv")
    Wg = ar.bf(8, 256)
    Wgb = em.buf("Wg")
    WOh = ar.bf(2, 1024)
    WOhb = em.buf("WOh")
    WA1 = ar.bf(8, 32)
    NBA = ar.f32(8)
    rmask = ar.f32(512)
    Bw = {k: em.buf(k) for k in ["WA1", "WA2", "RT", "NBA", "rmask"]}
    XC = [XS[:, j, S:T] for j in range(8)]
    R = [[XC[0], XC[1]], [XC[2], XC[3]]]
    Sfb = [XC[4][:, 0:128].bitcast(BF16), XC[4][:, 128:256].bitcast(BF16)]
    AT = [[XC[5][:, 0:64].bitcast(BF16), XC[5][:, 64:128].bitcast(BF16)],
          [XC[5][:, 128:192].bitcast(BF16), XC[5][:, 192:256].bitcast(BF16)]]
    ATpair = [XC[5][:, 0:128].bitcast(BF16), XC[5][:, 128:256].bitcast(BF16)]
    mask2 = L["mask"]
    WA2 = [XC[6].bitcast(BF16), XC[7].bitcast(BF16)]
    head_mark = ar.mark()
    QD = [ar.bf(S), ar.bf(S)]
    KI = [ar.bf(T), ar.bf(T)]
    KItm = [ar.bf(NT, 128), ar.bf(NT, 128)]
    DEC = ar.f32(2, NT)
    V = ar.bf(NT, 256)
    SBST = ar.bf(16, 256)
    tA = [ar.f32(512), ar.f32(512)]
    tB = [ar.f32(512), ar.f32(512)]
    TOT = [ar.f32(4), ar.f32(4)]
    Og = [ar.f32(2, 512), ar.f32(2, 512)]
    RO = ar.f32(512)
    sq2 = [ar.bf(512), ar.bf(512)]
    _tx = ar.f32(512)
    tX = [_tx, _tx]
    OGT = ar.bf(2, 512)
    tmpr = [Og[0][:, 0, :], Og[1][:, 0, :]]
    sqr = sq2
    tmps = tX[0]
    names = ["R00", "R01", "R10", "R11", "Sfb0", "Sfb1",
             "AT00", "AT01", "AT10", "AT11", "tA0", "tA1", "tB0", "tB1", "TOT0", "TOT1",
             "Og0", "Og1", "RO", "sq0", "sq1", "tX0", "OGT"]
    Bh = {k: em.buf(k) for k in names}
    Bh["tm0"], Bh["tm1"], Bh["sr0"], Bh["sr1"], Bh["tms"] = Bh["Og0"], Bh["Og1"], Bh["sq0"], Bh["sq1"], Bh["tX0"]
    RTb = em.bufs("rt", 5)
    DECb = [[em.buf("dec%d_%d" % (d, g)) for g in range(5)] for d in range(2)]
    Vb = em.bufs("V", 5)
    KTb = [em.bufs("ktm0_", 5), em.bufs("ktm1_", 5)]
    KIb = [em.bufs("ki0_", 5), em.bufs("ki1_", 5)]
    QDb = [em.bufs("qd0_", 4), em.bufs("qd1_", 4)]
    SBb = em.bufs("sbst", 4)
    STOP = L["stop"]

    em.dma("pool", WA1, wa1_d, writes=[Bw["WA1"]])
    em.dma("sp", rmask, rmask_d, writes=[Bw["rmask"]])
    em.op("dve", lambda e: e.tensor_scalar(out=NBA, in0=vec1[:, 48:56], scalar1=-1.0, scalar2=None, op0=ALU.mult),
          reads=[Bc["vec1"]], writes=[Bw["NBA"]])

    def load_qkv(h):
        for (c_lo, n, dst_lo) in ((h * 128, 128, 0), (512 + h * 128, 128, 128), (1024 + h * 256, 256, 256)):
            em.dma("pool", Wqkv[:, :, dst_lo:dst_lo + n], win1_d[:, :, c_lo:c_lo + n], writes=[Wqkvb])

    def load_g(h):
        em.dma("pool", Wg, win1_d[:, :, 2048 + h * 256:2048 + (h + 1) * 256], writes=[Wgb])
        em.dma("pool", WOh, wout1_d[:, 2 * h:2 * h + 2, :], writes=[WOhb])

    load_qkv(0)
    load_g(0)

    def stats(src_fn, nchunk, inv_n, out_ap, outb, w, sq, sqb, tmp, tmpb):
        pss, pssb = ps_next()
        for j in range(nchunk):
            sl = j % 2
            sap, sb = src_fn(j)
            em.op("act", lambda e, sl=sl, sap=sap: e.activation(out=sq[sl][:, 0:w], in_=sap, func=AF.Square),
                  reads=[sb], writes=[sqb[sl]])
            em.op("pe", lambda e, sl=sl, j=j: e.matmul(pss[:, 0:w], lhsT=ones_bf, rhs=sq[sl][:, 0:w],
                                                      start=(j == 0), stop=(j == nchunk - 1)),
                  reads=[sqb[sl], Bc["ones"]], writes=[pssb])
        em.op("act", lambda e: e.activation(out=tmp[:, 0:w], in_=pss[:, 0:w], func=AF.Ln, bias=EPS_AP, scale=inv_n),
              reads=[pssb, Bc["eps"]], writes=[tmpb])
        em.op("act", lambda e: e.activation(out=out_ap, in_=tmp[:, 0:w], func=AF.Exp, scale=-0.5),
              reads=[tmpb], writes=[outb])

    def setup_group(g, ri):
        t0, w = GROUPS[g]
        s = 1 if g == 4 else 0
        RSg, RSgb = RS2[ri], RS2b[ri]
        for j in range(8):
            sl = j % 2
            em.op("dve", lambda e, j=j, sl=sl: e.tensor_tensor(out=tmpr[sl][:, 0:w], in0=XS[:, j, t0:t0 + w],
                                                               in1=RSg[:, 0:w], op=ALU.mult),
                  reads=[XSb[j][g], RSgb], writes=[Bh["tm%d" % sl]])
            em.op("act", lambda e, j=j, sl=sl: e.activation(out=H[:, j, t0:t0 + w], in_=tmpr[sl][:, 0:w],
                                                            func=AF.Identity, bias=MOD[:, j, s:s + 1],
                                                            scale=GM[:, j, s:s + 1]),
                  reads=[Bh["tm%d" % sl], Bc["MOD1"], Bc["GM1"]], writes=[Hb[g]])
        pr, prb = ps_next()
        for kc in range(8):
            em.op("pe", lambda e, kc=kc: e.matmul(pr[0:32, 0:w], lhsT=WA1[:, kc, 0:32], rhs=H[:, kc, t0:t0 + w],
                                                  start=(kc == 0), stop=(kc == 7)),
                  reads=[Bw["WA1"], Hb[g]], writes=[prb])
        em.op("act", lambda e: e.activation(out=RTf[0:32, t0:t0 + w], in_=pr[0:32, 0:w], func=AF.Copy),
              reads=[prb], writes=[RTb[g]])

    def setup_stats(g, ri):
        t0, w = GROUPS[g]
        stats(lambda j: (XS[:, j, t0:t0 + w], XSb[j][g]), 8, 1.0 / D, RS2[ri][:, 0:w], RS2b[ri], w,
              sqr, [Bh["sr0"], Bh["sr1"]], tmps, Bh["tms"])

    def prep_dir(h, g, d, P):
        t0, w = GROUPS[g]
        nch = w // 128
        c0 = t0 // 128
        lat = g < 4
        a_, b_, tot = tA[d], tB[d], TOT[d]
        ab, bb, totb = Bh["tA%d" % d], Bh["tB%d" % d], Bh["TOT%d" % d]
        pz, pzb = ps_next()
        em.op("pe", lambda e: e.matmul(pz[:, 0:w], lhsT=WA2[d][0:32, h * 128:(h + 1) * 128],
                                       rhs=RTf[0:32, t0:t0 + w], start=True, stop=True),
              reads=[Bw["WA2"], RTb[g]], writes=[pzb])
        em.op("act", lambda e: e.activation(out=a_[:, 0:w], in_=pz[:, 0:w], func=AF.Exp, scale=-1.0,
                                            bias=NBA[:, d * 4 + h:d * 4 + h + 1]),
              reads=[pzb, Bw["NBA"]], writes=[ab])
        yield
        em.op("act", lambda e: e.activation(out=a_[:, 0:w], in_=a_[:, 0:w], func=AF.Ln, bias=1.0),
              reads=[ab], writes=[ab])
        yield
        em.op("dve", lambda e: e.tensor_tensor_scan(out=b_[:, 0:w], data0=rmask[:, 0:w], data1=a_[:, 0:w],
                                                    initial=0.0, op0=ALU.mult, op1=ALU.add),
              reads=[ab, Bw["rmask"]], writes=[bb])
        yield
        em.op("act", lambda e: e.activation(out=DEC[:, d, c0:c0 + nch], in_=b_[:, 127:w:128], func=AF.Exp, scale=-C16),
              reads=[bb], writes=[DECb[d][g]])
        yield
        if d == 0:
            sc1, sc2 = -C16, C16
        else:
            em.op("dve", lambda e: e.tensor_copy(out=tot[:, 0:nch], in_=b_[:, 127:w:128]), reads=[bb], writes=[totb])
            b3 = b_[:, 0:w].rearrange("p (c t) -> p c t", t=128)
            em.op("dve", lambda e: e.tensor_tensor(out=b3, in0=b3,
                                                   in1=tot[:, 0:nch].unsqueeze(2).to_broadcast([128, nch, 128]),
                                                   op=ALU.subtract),
                  reads=[bb, totb], writes=[bb])
            em.op("dve", lambda e: e.tensor_tensor(out=b_[:, 0:w], in0=b_[:, 0:w], in1=a_[:, 0:w], op=ALU.subtract),
                  reads=[bb, ab], writes=[bb])
            sc1, sc2 = C16, -C16
        yield
        if lat:
            pq, pqb = P["pq"]
            em.op("act", lambda e: e.activation(out=a_[:, 0:w], in_=b_[:, 0:w], func=AF.Exp, scale=sc1),
                  reads=[bb], writes=[ab])
            em.op("dve", lambda e: e.scalar_tensor_tensor(
                out=QD[d][:, t0:t0 + w], in0=pq[:, 0:w], scalar=KSCALE, in1=a_[:, 0:w], op0=ALU.mult, op1=ALU.mult),
                reads=[pqb, ab], writes=[QDb[d][g]])
        yield
        pk, pkb = P["pk"]
        em.op("act", lambda e: e.activation(out=b_[:, 0:w], in_=b_[:, 0:w], func=AF.Exp, scale=sc2),
              reads=[bb], writes=[bb])
        em.op("dve", lambda e: e.tensor_tensor(out=KI[d][:, t0:t0 + w], in0=pk[:, 0:w], in1=b_[:, 0:w], op=ALU.mult),
              reads=[pkb, bb], writes=[KIb[d][g]])
        yield
        pt, ptb = ps_next()
        for c in range(nch):
            em.op("pe", lambda e, c=c: e.matmul(pt[:, c * 128:(c + 1) * 128],
                                                lhsT=KI[d][:, t0 + c * 128:t0 + (c + 1) * 128], rhs=ident_bf,
                                                start=True, stop=True),
                  reads=[KIb[d][g], Bc["ident"]], writes=[ptb])
        em.op("act", lambda e: e.activation(
            out=KItm[d][:, c0:c0 + nch, :], in_=pt[:, 0:w].rearrange("p (c t) -> p c t", t=128), func=AF.Copy),
            reads=[ptb], writes=[KTb[d][g]])

    def prep_group(h, g):
        t0, w = GROUPS[g]
        nch = w // 128
        lat = g < 4
        P = {}
        gens = [prep_dir(h, g, d, P) for d in range(2)]
        alive = [True, True]
        rnd = 0
        while any(alive):
            rnd += 1
            if rnd == 6 and lat:
                pq, pqb = ps_next()
                for kc in range(8):
                    em.op("pe", lambda e, kc=kc, pq=pq: e.matmul(pq[:, 0:w], lhsT=Wqkv[:, kc, 0:128],
                                                                 rhs=H[:, kc, t0:t0 + w], start=(kc == 0),
                                                                 stop=(kc == 7)),
                          reads=[Wqkvb, Hb[g]], writes=[pqb])
                P["pq"] = (pq, pqb)
            if rnd == 7:
                pk, pkb = ps_next()
                for kc in range(8):
                    em.op("pe", lambda e, kc=kc, pk=pk: e.matmul(pk[:, 0:w], lhsT=Wqkv[:, kc, 128:256],
                                                                 rhs=H[:, kc, t0:t0 + w], start=(kc == 0),
                                                                 stop=(kc == 7)),
                          reads=[Wqkvb, Hb[g]], writes=[pkb])
                P["pk"] = (pk, pkb)
            for d in range(2):
                if alive[d]:
                    try:
                        next(gens[d])
                    except StopIteration:
                        alive[d] = False
            yield
        for tt in range(nch):
            vproj(h, g, tt)
            yield

    def vproj(h, g, tt):
        t0, w = GROUPS[g]
        ti = t0 // 128 + tt
        pv, pvb = ps_next()
        for kc in range(8):
            em.op("pe", lambda e, kc=kc: e.matmul(pv[:, 0:256], lhsT=H[:, kc, ti * 128:(ti + 1) * 128],
                                                  rhs=Wqkv[:, kc, 256:512], start=(kc == 0), stop=(kc == 7)),
                  reads=[Wqkvb, Hb[g]], writes=[pvb])
        em.op("act", lambda e: e.activation(out=V[:, ti, :], in_=pv[:, 0:256], func=AF.Copy),
              reads=[pvb], writes=[Vb[g]])

    rstate = {0: [0, None, None], 1: [0, None, None]}

    def chain_reset(pair, d):
        rstate[pair] = [0, None, d]

    def chain_step(pair, c):
        cur, prev, d = rstate[pair]
        nxt = 1 - cur
        g = c // 4
        pkv, pkvb = ps_next()
        em.op("pe", lambda e: e.matmul(pkv[:, 0:256], lhsT=KItm[d][:, c, :], rhs=V[:, c, :], start=True, stop=True),
              reads=[KTb[d][g], Vb[g]], writes=[pkvb])
        if prev is None:
            em.op("dve", lambda e: e.tensor_copy(out=R[pair][nxt], in_=pkv[:, 0:256]),
                  reads=[pkvb], writes=[Bh["R%d%d" % (pair, nxt)]])
        else:
            em.op("dve", lambda e: e.scalar_tensor_tensor(out=R[pair][nxt], in0=R[pair][cur],
                                                          scalar=DEC[:, d, prev:prev + 1],
                                                          in1=pkv[:, 0:256], op0=ALU.mult, op1=ALU.add),
                  reads=[Bh["R%d%d" % (pair, cur)], DECb[d][prev // 4], pkvb], writes=[Bh["R%d%d" % (pair, nxt)]])
        rstate[pair][0] = nxt
        rstate[pair][1] = c

    def state_bf16(pair, out_ap, outb, on_dve=False):
        cur, prev, d = rstate[pair]
        if on_dve:
            em.op("dve", lambda e: e.tensor_scalar(out=out_ap, in0=R[pair][cur], scalar1=DEC[:, d, prev:prev + 1],
                                                   scalar2=None, op0=ALU.mult),
                  reads=[Bh["R%d%d" % (pair, cur)], DECb[d][prev // 4]], writes=[outb])
            return
        em.op("act", lambda e: e.activation(out=out_ap, in_=R[pair][cur], func=AF.Identity,
                                            scale=DEC[:, d, prev:prev + 1]),
              reads=[Bh["R%d%d" % (pair, cur)], DECb[d][prev // 4]], writes=[outb])

    def out_a(h, c):
        ai = c % 2
        g = c // 4
        cs = slice(c * 128, (c + 1) * 128)
        pa, pab = ps_next()
        for d in range(2):
            em.op("pe", lambda e, d=d: e.matmul(pa[:, d * 128:(d + 1) * 128], lhsT=KI[d][:, cs], rhs=QD[d][:, cs],
                                                start=True, stop=True),
                  reads=[KIb[d][g], QDb[d][g]], writes=[pab])
        em.op("dve", lambda e: e.tensor_tensor(out=ATpair[ai], in0=pa[:, 0:256], in1=mask2, op=ALU.mult),
              reads=[pab, Bc["mask"]], writes=[Bh["AT%d0" % ai], Bh["AT%d1" % ai]])

    def out_b(h, c, ds, do_chain):
        pair = h % 2
        dc = 1 - ds
        ai = c % 2
        g = c // 4
        cs = slice(c * 128, (c + 1) * 128)
        state_bf16(pair, Sfb[ai], Bh["Sfb%d" % ai], on_dve=True)
        if do_chain:
            chain_step(pair, c)
        pos = [ps_next(), ps_next()]
        for ec in range(2):
            es = slice(ec * 128, (ec + 1) * 128)
            po, pob = pos[ec]
            em.op("pe", lambda e, es=es, po=po: e.matmul(po[:, 0:128], lhsT=V[:, c, es], rhs=AT[ai][0], start=True,
                                                         stop=False),
                  reads=[Vb[g], Bh["AT%d0" % ai]], writes=[pob])
            em.op("pe", lambda e, es=es, po=po: e.matmul(po[:, 0:128], lhsT=V[:, c, es], rhs=AT[ai][1], start=False,
                                                         stop=False),
                  reads=[Vb[g], Bh["AT%d1" % ai]], writes=[pob])
            em.op("pe", lambda e, es=es, po=po: e.matmul(po[:, 0:128], lhsT=SBST[:, c, es], rhs=QD[dc][:, cs],
                                                         start=False, stop=False),
                  reads=[SBb[g], QDb[dc][g]], writes=[pob])
        cc = c % 4
        oi = g % 2
        for ec in range(2):
            es = slice(ec * 128, (ec + 1) * 128)
            po, pob = pos[ec]
            em.op("pe", lambda e, es=es, po=po: e.matmul(po[:, 0:128], lhsT=Sfb[ai][:, es], rhs=QD[ds][:, cs],
                                                         start=False, stop=True),
                  reads=[Bh["Sfb%d" % ai], QDb[ds][g]], writes=[pob])
            em.op("act", lambda e, ec=ec, po=po: e.activation(out=Og[oi][:, ec, cc * 128:(cc + 1) * 128],
                                                              in_=po[:, 0:128], func=AF.Copy),
                  reads=[pob], writes=[Bh["Og%d" % oi]])

    def epilogue12(h, g):
        t0, w = GROUPS[g]
        oi = g % 2
        Ogg, Oggb = Og[oi], Bh["Og%d" % oi]
        stats(lambda j: (Ogg[:, j, :], Oggb), 2, 1.0 / 256.0, RO, Bh["RO"], 512,
              sq2, [Bh["sq0"], Bh["sq1"]], tX[0], Bh["tX0"])
        for ec in range(2):
            pg, pgb = ps_next()
            for kc in range(8):
                em.op("pe", lambda e, kc=kc, ec=ec, pg=pg: e.matmul(
                    pg[:, :], lhsT=Wg[:, kc, ec * 128:(ec + 1) * 128], rhs=H[:, kc, t0:t0 + w],
                    start=(kc == 0), stop=(kc == 7)),
                    reads=[Wgb, Hb[g]], writes=[pgb])
            em.op("dve", lambda e, ec=ec: e.scalar_tensor_tensor(
                out=Ogg[:, ec, :], in0=Ogg[:, ec, :], scalar=vec1[:, 32 + 2 * h + ec:33 + 2 * h + ec], in1=RO,
                op0=ALU.mult, op1=ALU.mult),
                reads=[Oggb, Bc["vec1"], Bh["RO"]], writes=[Oggb])
            em.op("act", lambda e, pg=pg, ec=ec: e.activation(out=tX[0], in_=pg[:, :], func=AF.Silu), reads=[pgb],
                  writes=[Bh["tX0"]])
            em.op("pool", lambda e, ec=ec: e.tensor_tensor(out=OGT[:, ec, :], in0=Ogg[:, ec, :], in1=tX[0],
                                                           op=ALU.mult),
                  reads=[Oggb, Bh["tX0"]], writes=[Bh["OGT"]])

    def epilogue3(h, g):
        t0, w = GROUPS[g]
        for cch in range(8):
            pp, ppb = ps_next()
            for ec in range(2):
                em.op("pe", lambda e, ec=ec, cch=cch, pp=pp: e.matmul(
                    pp[:, :], lhsT=WOh[:, ec, cch * 128:(cch + 1) * 128], rhs=OGT[:, ec, :],
                    start=(ec == 0), stop=(ec == 1)),
                    reads=[WOhb, Bh["OGT"]], writes=[ppb])
            em.op("dve", lambda e, cch=cch, pp=pp: e.scalar_tensor_tensor(
                out=XS[:, cch, t0:t0 + w], in0=pp[:, :], scalar=MOD[:, 16 + cch, 0:1], in1=XS[:, cch, t0:t0 + w],
                op0=ALU.mult, op1=ALU.add),
                reads=[ppb, Bc["MOD1"], XSb[cch][g]], writes=[XSb[cch][g]])

    def final_group(g):
        t0, w = GROUPS[g]
        stats(lambda j: (XS[:, j, t0:t0 + w], XSb[j][g]), 8, 1.0 / D, RS2[g % 2][:, 0:w], RS2b[g % 2], w,
              sq2, [Bh["sq0"], Bh["sq1"]], tX[0], Bh["tX0"])
        for j in range(8):
            em.op("dve", lambda e, j=j: e.scalar_tensor_tensor(
                out=XS[:, j, t0:t0 + w], in0=XS[:, j, t0:t0 + w], scalar=vec1[:, 40 + j:41 + j],
                in1=RS2[g % 2][:, 0:w], op0=ALU.mult, op1=ALU.mult),
                reads=[XSb[j][g], Bc["vec1"], RS2b[g % 2]], writes=[XSb[j][g]])
            L1_OUTS.append(em.dma("sp", y_d[:, j, t0:t0 + w], XS[:, j, t0:t0 + w], reads=[XSb[j][g]]))

    def chunks_of(g, d):
        if g == 4:
            return [16, 17] if d == 0 else [17, 16]
        cs = list(range(4 * g, 4 * g + 4))
        return cs if d == 0 else cs[::-1]

    def stored_chain_group(h, g):
        pair = h % 2
        dc = h % 2
        step = 1 if dc == 0 else -1
        for c in chunks_of(g, dc):
            chain_step(pair, c)
            if g == 4:
                nxt = (0 if dc == 0 else 15) if c == chunks_of(4, dc)[-1] else None
            else:
                nxt = c + step
            if nxt is not None and 0 <= nxt <= 15:
                state_bf16(pair, SBST[:, nxt, :], SBb[nxt // 4])
            yield

    def sweep_group(h, g, next_first):
        pair = h % 2
        ds = 1 - (h % 2)
        cs = chunks_of(g, ds)
        for i, c in enumerate(cs):
            nxt = cs[i + 1] if i + 1 < len(cs) else next_first
            if nxt is not None:
                out_a(h, nxt)
            last_overall = (c == (0 if ds == 1 else 15))
            out_b(h, c, ds, not last_overall)
            yield
            if i == 0 and pending_ep3:
                gg = pending_ep3.pop()
                epilogue3(h, gg)
                if h == 3:
                    final_group(gg)
                yield
        epilogue12(h, g)
        pending_ep3.append(g)
        yield

    pending_ep3 = []

    def drive(gens_a, gens_b, ratio=2):
        A = list(gens_a)
        B = list(gens_b)
        if STOP == 77:
            for g_ in A:
                for _ in g_:
                    pass
            em.barrier()
            for g_ in B:
                for _ in g_:
                    pass
            em.barrier()
            return
        ia = ib = 0
        while ia < len(A) or ib < len(B):
            if ia < len(A):
                try:
                    next(A[ia])
                except StopIteration:
                    ia += 1
            for _ in range(ratio):
                if ib < len(B):
                    try:
                        next(B[ib])
                    except StopIteration:
                        ib += 1

    def prep_order(h):
        return [4, 0, 1, 2, 3] if h % 2 == 0 else [4, 3, 2, 1, 0]

    chain_reset(0, 0)
    po0 = prep_order(0)
    setup_stats(po0[0], 0)
    for i, g in enumerate(po0):
        if i + 1 < 5:
            setup_stats(po0[i + 1], (i + 1) % 2)
        setup_group(g, i % 2)
        if i == 0:
            for d in range(2):
                em.dma("pool", WA2[d][0:32, :], wa2_d[:, d, :], writes=[Bw["WA2"], XSb[6][4], XSb[7][4]])
        if i >= 1:
            for _ in prep_group(0, po0[i - 1]):
                pass
        if i >= 2:
            for _ in stored_chain_group(0, po0[i - 2]):
                pass
    for _ in prep_group(0, po0[4]):
        pass
    for _ in stored_chain_group(0, po0[3]):
        pass
    for _ in stored_chain_group(0, po0[4]):
        pass

    for h in range(4):
        pair = h % 2
        ds = 1 - (h % 2)
        if h < 3:
            load_qkv(h + 1)
        chain_reset(pair, ds)
        for c in chunks_of(4, ds):
            chain_step(pair, c)
        sg = [3, 2, 1, 0] if ds == 1 else [0, 1, 2, 3]
        out_a(h, chunks_of(sg[0], ds)[0])
        if h < 3:
            chain_reset(1 - pair, (h + 1) % 2)
            pg = prep_order(h + 1)
            assert pg[1:] == sg
        for k in range(5):
            A = []
            if k < 4:
                nf = chunks_of(sg[k + 1], ds)[0] if k + 1 < 4 else None
                A = [sweep_group(h, sg[k], nf)]
            B = []
            if h < 3:
                B.append(prep_group(h + 1, pg[k]))
                if k > 0:
                    B.append(stored_chain_group(h + 1, pg[k - 1]))
            drive(A, B)
        if h < 3:
            for _ in stored_chain_group(h + 1, pg[4]):
                pass
        while pending_ep3:
            gg = pending_ep3.pop()
            epilogue3(h, gg)
            if h == 3:
                final_group(gg)
        if h < 3:
            load_g(h + 1)


def _kc_layout(w):
    n = w.shape[1]
    return np.ascontiguousarray(w.reshape(8, 128, n).transpose(1, 0, 2))


def _vec_layout(v):
    return np.ascontiguousarray(v.reshape(-1, 128).T)


def _wa2_layout(wf, wb):
    o = np.zeros((32, 2, 512), np.float32)
    o[0:16, 0, :] = wf
    o[16:32, 1, :] = wb
    return o


def _rope_tables():
    pos = np.arange(S)
    row = (pos // 64).astype(np.float64)
    col = (pos % 64).astype(np.float64)
    inv_freq = 10000.0 ** (-np.arange(16, dtype=np.float64) / 16.0)
    C = np.zeros((128, S), np.float32)
    Sn = np.zeros((128, S), np.float32)
    for r in range(128):
        d = r % 64
        a = d // 32
        f = d % 16
        ang = (row if a == 0 else col) * inv_freq[f]
        C[r] = np.cos(ang).astype(np.float32)
        Sn[r] = np.sin(ang).astype(np.float32)
    P = np.zeros((128, 128), np.float32)
    for m in range(128):
        d = m % 64
        jj = (d % 32) // 16
        if jj == 0:
            P[m + 16, m] = -1.0
        else:
            P[m - 16, m] = 1.0
    return C, Sn, P


def prepare_inputs(x, c, ctx, c_ctx, l0_norm_g, l0_w_ada, l0_b_ada, l0_w_in, l0_sink, l0_w_out,
                   l1_norm_g, l1_w_ada, l1_b_ada, l1_w_in, l1_wa1_f, l1_wa2_f, l1_ba_f,
                   l1_wa1_b, l1_wa2_b, l1_ba_b, l1_head_norm_g, l1_w_out, final_norm_g):
    f = lambda a: np.asarray(a, dtype=np.float32)
    x, c, ctx, c_ctx = f(x), f(c), f(ctx), f(c_ctx)
    C, Sn, P = _rope_tables()
    ii = np.arange(128)
    mask = np.concatenate([(ii[:, None] <= ii[None, :]), (ii[:, None] >= ii[None, :])], axis=1).astype(np.float32)
    rmask = np.ones((128, 512), np.float32)
    rmask[:, ::128] = 0.0
    qcols = []
    for j in range(8):
        for r in range(128):
            hh = j if r < 64 else 8 + j
            qcols.append(hh * 64 + (r % 64))
    w_in0 = f(l0_w_in)
    cols = np.concatenate([np.array(qcols), np.arange(1024, 1152), np.arange(1152, 1280), np.arange(1280, 2304)])
    shared = {
        "ropec": C, "ropes": Sn, "perm": P, "ident": np.eye(128, dtype=np.float32), "mask": mask, "rmask": rmask,
        "sink": np.ascontiguousarray(np.broadcast_to(f(l0_sink)[None, :], (128, 16))),
        "vec0": np.concatenate([_vec_layout(f(l0_norm_g)), _vec_layout(f(l0_b_ada))], axis=1),
        "vec1": np.concatenate([_vec_layout(f(l1_norm_g)), _vec_layout(f(l1_b_ada)), _vec_layout(f(l1_head_norm_g)),
                                _vec_layout(f(final_norm_g)), _vec_layout(f(l1_ba_f)), _vec_layout(f(l1_ba_b))],
                               axis=1),
        "wada0": _kc_layout(f(l0_w_ada)), "win0": _kc_layout(w_in0[:, cols]), "wout0": _kc_layout(f(l0_w_out)),
        "wada1": _kc_layout(f(l1_w_ada)), "win1": _kc_layout(f(l1_w_in)), "wout1": _kc_layout(f(l1_w_out)),
        "wa1": _kc_layout(np.concatenate([f(l1_wa1_f), f(l1_wa1_b)], axis=1)),
        "wa2": _wa2_layout(f(l1_wa2_f), f(l1_wa2_b)),
    }
    in_maps = []
    for b in range(8):
        cat = np.concatenate([x[b], ctx[b]], axis=0)
        xs = np.ascontiguousarray(cat.T.reshape(8, 128, T).transpose(1, 0, 2))
        ccb = np.ascontiguousarray(np.stack([_vec_layout(c[b]), _vec_layout(c_ctx)], axis=2))
        m = {"xs": xs, "cc": ccb}
        m.update(shared)
        in_maps.append(m)
    return in_maps


_NC_CACHE = {}


def kernel(**inputs):
    in_maps = prepare_inputs(**inputs)
    if "nc" not in _NC_CACHE:
        _NC_CACHE["nc"] = build_program(stage=2)
    nc = _NC_CACHE["nc"]
    res = run_bass_kernel_spmd(nc, in_maps, core_ids=list(range(8)))
    out = np.empty((8, S, D), np.float32)
    for b in range(8):
        y = res.results[b]["y"]
        out[b] = y.transpose(2, 1, 0).reshape(S, D)
    return out
```

```python
import contextlib
import numpy as np
import concourse.bass as bass
import concourse.mybir as mybir
from concourse.bass_utils import run_bass_kernel_spmd

F32 = mybir.dt.float32
BF16 = mybir.dt.bfloat16
AF = mybir.ActivationFunctionType
ALU = mybir.AluOpType

D = 1024
S = 2048
CTX = 256
T = S + CTX
NT = T // 128
EPS = 1e-6
GROUPS = [(0, 512), (512, 512), (1024, 512), (1536, 512), (2048, 256)]


class Buf:
    __slots__ = ("name", "w", "r")

    def __init__(self, name):
        self.name = name
        self.w = None
        self.r = []


class Op:
    __slots__ = ("eng", "fn", "deps", "kind", "sem", "val", "needed")

    def __init__(self, eng, fn, kind):
        self.eng = eng
        self.fn = fn
        self.kind = kind
        self.deps = []
        self.sem = None
        self.val = 0
        self.needed = False


class Em:
    ENGS = ("pe", "act", "dve", "pool", "sp")

    def __init__(self, nc, n_dma_sems=32):
        self.nc = nc
        self.ops = {e: [] for e in self.ENGS}
        self.n_dma_sems = n_dma_sems
        self.dma_last = [None] * n_dma_sems
        self.dma_cnt = [0] * n_dma_sems
        self.dma_rr = 0
        self.n_sw = 0
        self.pending_barrier = {}

    def buf(self, name="b"):
        return Buf(name)

    def bufs(self, name, n):
        return [Buf(f"{name}{i}") for i in range(n)]

    def _track(self, op, reads, writes):
        deps = op.deps
        for b in reads:
            if b.w is not None:
                deps.append(b.w)
        for b in writes:
            if b.w is not None:
                deps.append(b.w)
            deps.extend(b.r)
        for b in reads:
            b.r.append(op)
        for b in writes:
            b.w = op
            b.r = []

    def op(self, eng, fn, reads=(), writes=()):
        o = Op(eng, fn, "c")
        if self.pending_barrier.get(eng):
            o.deps.extend(self.pending_barrier.pop(eng))
        self._track(o, reads, writes)
        self.ops[eng].append(o)
        return o

    def dma(self, eng, out, in_, reads=(), writes=(), **kw):
        o = Op(eng, (lambda e, out=out, in_=in_, kw=kw: e.dma_start(out=out, in_=in_, **kw)), "d")
        if eng == "pool":
            k = self.n_dma_sems + self.n_sw
            self.n_sw += 1
            o.sem = ("d", k)
            o.val = 16
        else:
            k = self.dma_rr
            self.dma_rr = (self.dma_rr + 1) % self.n_dma_sems
            if self.dma_last[k] is not None:
                o.deps.append(self.dma_last[k])
            self.dma_cnt[k] += 1
            o.sem = ("d", k)
            o.val = 16 * self.dma_cnt[k]
            self.dma_last[k] = o
        if self.pending_barrier.get(eng):
            o.deps.extend(self.pending_barrier.pop(eng))
        self._track(o, reads, writes)
        self.ops[eng].append(o)
        return o

    def barrier(self):
        lasts = []
        for e in self.ENGS:
            if self.ops[e]:
                lasts.append(self.ops[e][-1])
        for d in self.dma_last:
            if d is not None:
                lasts.append(d)
        self.pending_barrier = {e: list(lasts) for e in self.ENGS}

    def emit(self, stack, final_waits=()):
        nc = self.nc
        esem = {e: stack.enter_context(nc.semaphore(f"s_{e}")) for e in ("pe", "act", "dve", "pool")}
        dsem = [stack.enter_context(nc.semaphore(f"d_{k}")) for k in range(self.n_dma_sems + self.n_sw)]
        for e in self.ENGS:
            for o in self.ops[e]:
                for d in o.deps:
                    if d.kind == "c":
                        if d.eng == "pe" and o.eng == "pe" and o.kind == "c":
                            continue
                        d.needed = True
        for o in final_waits:
            if o.kind == "c":
                o.needed = True
        for e in ("pe", "act", "dve", "pool"):
            c = 0
            for o in self.ops[e]:
                if o.kind == "c" and o.needed:
                    c += 1
                    o.sem = ("e", e)
                    o.val = c

        def semh(s):
            return esem[s[1]] if s[0] == "e" else dsem[s[1]]

        engmap = {"pe": "tensor", "act": "scalar", "dve": "vector", "pool": "gpsimd", "sp": "sync"}
        stats = {}
        block = stack.enter_context(nc.Block())

        def make(e):
            def body(eng):
                seen = {}
                nw = 0
                for o in self.ops[e]:
                    need = {}
                    for d in o.deps:
                        if d.kind == "c" and d.eng == "pe" and e == "pe" and o.kind == "c":
                            continue
                        if d.sem is None:
                            continue
                        if need.get(d.sem, 0) < d.val:
                            need[d.sem] = d.val
                    for s, v in need.items():
                        if seen.get(s, 0) < v:
                            eng.wait_ge(semh(s), v)
                            seen[s] = v
                            nw += 1
                    ins = o.fn(eng)
                    if o.kind == "d":
                        ins.then_inc(semh(o.sem), 16)
                    elif o.needed:
                        ins.then_inc(semh(o.sem), 1)
                if e == "sp":
                    need = {}
                    for d in final_waits:
                        if need.get(d.sem, 0) < d.val:
                            need[d.sem] = d.val
                    for s, v in need.items():
                        eng.wait_ge(semh(s), v)
                stats[e] = (len(self.ops[e]), nw)
            return body

        for e in self.ENGS:
            getattr(block, engmap[e])(make(e))
        return stats


class Arena:
    def __init__(self, big, nfl):
        self.big = big
        self.n = nfl
        self.off = 0
        self.peak = 0

    def _alloc(self, nfl):
        a = self.off
        self.off += nfl
        self.peak = max(self.peak, self.off)
        assert self.off <= self.n, f"arena overflow {self.off} > {self.n}"
        return a

    @staticmethod
    def _shape(ap, shape):
        if len(shape) == 1:
            return ap
        if len(shape) == 2:
            return ap.rearrange("p (a b) -> p a b", a=shape[0])
        if len(shape) == 3:
            return ap.rearrange("p (a b c) -> p a b c", a=shape[0], b=shape[1])
        raise ValueError

    def f32(self, *shape):
        n = int(np.prod(shape))
        a = self._alloc(n)
        return self._shape(self.big[:, a:a + n], shape)

    def bf(self, *shape):
        n = int(np.prod(shape))
        nfl = (n + 1) // 2
        a = self._alloc(nfl)
        return self._shape(self.big[:, a:a + nfl].bitcast(BF16)[:, 0:n], shape)

    def mark(self):
        return self.off

    def release(self, m):
        self.off = m


def build_program(stage=2, stop=99):
    nc = bass.Bass("TRN2", target_bir_lowering=False)

    def din(name, shape):
        return nc.dram_tensor(name, list(shape), F32, kind="ExternalInput").ap()

    xs_d = din("xs", [128, 8, T])
    cc_d = din("cc", [128, 8, 2])
    ropec_d = din("ropec", [128, S])
    ropes_d = din("ropes", [128, S])
    perm_d = din("perm", [128, 128])
    ident_d = din("ident", [128, 128])
    mask_d = din("mask", [128, 256])
    rmask_d = din("rmask", [128, 512])
    sink_d = din("sink", [128, 16])
    vec0_d = din("vec0", [128, 32])
    vec1_d = din("vec1", [128, 56])
    wada0_d = din("wada0", [128, 8, 3072])
    win0_d = din("win0", [128, 8, 2304])
    wout0_d = din("wout0", [128, 8, 1024])
    wada1_d = din("wada1", [128, 8, 3072])
    win1_d = din("win1", [128, 8, 3072])
    wout1_d = din("wout1", [128, 8, 1024])
    wa1_d = din("wa1", [128, 8, 32])
    wa2_d = din("wa2", [32, 2, 512])
    y_d = nc.dram_tensor("y", [128, 8, S], F32, kind="ExternalOutput").ap()

    em = Em(nc)
    NFL = 53000
    with contextlib.ExitStack() as st:
        big = st.enter_context(nc.sbuf_tensor("big", [128, NFL], F32))
        ar = Arena(big, NFL)
        NPS = 6
        PS = [st.enter_context(nc.psum_tensor(f"ps{i}", [128, 512], F32)) for i in range(NPS)]
        PSD = [st.enter_context(nc.psum_tensor(f"psd{i}", [128, 512], F32)) for i in range(2)]
        PSDb = em.bufs("psd", 2)
        PSB = None
        PSb = em.bufs("ps", NPS)
        PSBb = [em.buf("psbA"), em.buf("psbB")]
        ps_rr = [0]
        psb_rr = [0]

        ring_n = [NPS + 2]
        PSall = PS + PSD
        PSallb = PSb + PSDb

        def ps_next():
            i = ps_rr[0] % ring_n[0]
            ps_rr[0] = (i + 1) % ring_n[0]
            return PSall[i], PSallb[i]

        def psb_next():
            i = psb_rr[0]
            psb_rr[0] = (i + 1) % 2
            return PSB[:, i * 512:(i + 1) * 512], PSBb[i]

        XS = ar.f32(8, T)
        XSb = [[em.buf(f"xs{j}_{g}") for g in range(5)] for j in range(8)]
        RSTD = ar.f32(T)
        RSTDb = em.bufs("rstd", 5)
        cc = ar.f32(8, 2)
        scb = ar.bf(8, 2)
        MOD = ar.f32(24, 2)
        GM = ar.f32(8, 2)
        MOD0, GM0 = MOD, GM
        MOD1 = ar.f32(24, 2)
        GM1 = ar.f32(8, 2)
        modstage = ar.f32(48)
        vec0 = ar.f32(32)
        vec1 = ar.f32(56)
        ones_bf = ar.bf(128)
        ident_bf = ar.bf(128)
        mask = ar.f32(256)
        sinkexp = ar.f32(16)
        Bc = {k: em.buf(k) for k in ["cc", "scb", "MOD", "GM", "vec0", "vec1", "ones", "ident", "mask",
                                     "sinkexp", "perm", "rmask", "nba", "MOD1", "GM1", "modstage"]}
        mle = mask[:, 0:128]
        mge = mask[:, 128:256]

        for g, (t0, w) in enumerate(GROUPS):
            for j in range(8):
                em.dma("act", XS[:, j, t0:t0 + w], xs_d[:, j, t0:t0 + w], writes=[XSb[j][g]])
        em.dma("sp", cc, cc_d, writes=[Bc["cc"]])
        em.dma("sp", vec0, vec0_d, writes=[Bc["vec0"]])
        em.dma("sp", vec1, vec1_d, writes=[Bc["vec1"]])
        em.dma("sp", mask, mask_d, writes=[Bc["mask"]])
        em.dma("sp", sinkexp, sink_d, writes=[Bc["sinkexp"]])
        em.dma("pool", ident_bf, ident_d, writes=[Bc["ident"]])
        em.op("pool", lambda e: e.memset(ones_bf, 1.0), writes=[Bc["ones"]])
        em.op("act", lambda e: e.activation(out=sinkexp, in_=sinkexp, func=AF.Exp),
              reads=[Bc["sinkexp"]], writes=[Bc["sinkexp"]])
        em.op("act", lambda e: e.activation(out=scb, in_=cc, func=AF.Silu), reads=[Bc["cc"]], writes=[Bc["scb"]])

        class Modulation:
            NP = 24

            def __init__(self, wada_d, vec, vecb, ring32, ringbf, MOD, GM, MODb, GMb, alias_bufs, tag):
                self.a = (wada_d, vec, vecb, ring32, ringbf, MOD, GM, MODb, GMb)
                self.alias = list(alias_bufs)
                self.b32 = em.bufs("mr32" + tag, 2)
                self.bbf = em.bufs("mrbf" + tag, 2)
                self.first = [True, True]

            def dma(self, pc):
                wada_d, vec, vecb, ring32, ringbf, MOD, GM, MODb, GMb = self.a
                slot = pc % 2
                extra = self.alias if self.first[slot] else []
                self.first[slot] = False
                em.dma("sp", ring32[slot], wada_d[:, :, pc * 128:(pc + 1) * 128], writes=[self.b32[slot]] + extra)

            def mm(self, pc):
                wada_d, vec, vecb, ring32, ringbf, MOD, GM, MODb, GMb = self.a
                slot = pc % 2
                em.op("dve", lambda e: e.tensor_copy(out=ringbf[slot], in_=ring32[slot]),
                      reads=[self.b32[slot]], writes=[self.bbf[slot]])
                pm, pmb = ps_next()
                for kc in range(8):
                    em.op("pe", lambda e, kc=kc: e.matmul(pm[:, 0:2], lhsT=ringbf[slot][:, kc, :], rhs=scb[:, kc, :],
                                                          start=(kc == 0), stop=(kc == 7)),
                          reads=[self.bbf[slot], Bc["scb"]], writes=[pmb])
                em.op("act", lambda e: e.activation(out=modstage[:, pc * 2:(pc + 1) * 2], in_=pm[:, 0:2], func=AF.Copy),
                      reads=[pmb], writes=[Bc["modstage"]])

            def finish(self):
                wada_d, vec, vecb, ring32, ringbf, MOD, GM, MODb, GMb = self.a
                pmv = modstage.rearrange("p (a b) -> p a b", a=24)
                for s in range(2):
                    em.op("dve", lambda e, s=s: e.tensor_tensor(out=MOD[:, :, s], in0=pmv[:, :, s], in1=vec[:, 8:32],
                                                                op=ALU.add),
                          reads=[Bc["modstage"], vecb], writes=[MODb])
                for s in range(2):
                    em.op("dve", lambda e, s=s: e.scalar_tensor_tensor(out=GM[:, :, s], in0=MOD[:, 8:16, s],
                                                                       scalar=1.0, in1=vec[:, 0:8], op0=ALU.add,
                                                                       op1=ALU.mult),
                          reads=[MODb, vecb] + self.b32 + self.bbf, writes=[GMb] + self.alias)

            def run_all(self):
                self.dma(0)
                self.dma(1)
                for pc in range(self.NP):
                    self.mm(pc)
                    if pc + 2 < self.NP:
                        self.dma(pc + 2)
                self.finish()

        def norm_stats(g, sqr, sqrb, tmp, tmpb, src=None, srcb=None, nchunk=8, inv_n=1.0 / D, out=None, outb=None):
            t0, w = GROUPS[g]
            pss, pssb = ps_next()
            for j in range(nchunk):
                sl = j % 2
                if src is None:
                    sap, sb = XS[:, j, t0:t0 + w], XSb[j][g]
                else:
                    sap, sb = src[:, j, 0:w], srcb
                em.op("act", lambda e, sl=sl, sap=sap: e.activation(out=sqr[sl][:, 0:w], in_=sap, func=AF.Square),
                      reads=[sb], writes=[sqrb[sl]])
                em.op("pe", lambda e, sl=sl, j=j: e.matmul(pss[:, 0:w], lhsT=ones_bf, rhs=sqr[sl][:, 0:w],
                                                          start=(j == 0), stop=(j == nchunk - 1)),
                      reads=[sqrb[sl], Bc["ones"]], writes=[pssb])
            em.op("act", lambda e: e.activation(out=tmp[:, 0:w], in_=pss[:, 0:w], func=AF.Ln, bias=EPS_AP,
                                                scale=inv_n),
                  reads=[pssb, Bc["eps"]], writes=[tmpb])
            if out is None:
                oap, ob = RSTD[:, t0:t0 + w], RSTDb[g]
            else:
                oap, ob = out[:, 0:w], outb
            em.op("act", lambda e: e.activation(out=oap, in_=tmp[:, 0:w], func=AF.Exp, scale=-0.5),
                  reads=[tmpb], writes=[ob])

        def make_h(g, Hdst, Hb, tmpr, tmprb):
            t0, w = GROUPS[g]
            s = 1 if g == 4 else 0
            for j in range(8):
                sl = j % 2
                em.op("dve", lambda e, j=j, sl=sl: e.tensor_tensor(out=tmpr[sl][:, 0:w], in0=XS[:, j, t0:t0 + w],
                                                                   in1=RSTD[:, t0:t0 + w], op=ALU.mult),
                      reads=[XSb[j][g], RSTDb[g]], writes=[tmprb[sl]])
                em.op("act", lambda e, j=j, sl=sl: e.activation(out=Hdst[:, j, 0:w], in_=tmpr[sl][:, 0:w],
                                                                func=AF.Identity, bias=MOD[:, j, s:s + 1],
                                                                scale=GM[:, j, s:s + 1]),
                      reads=[tmprb[sl], Bc["MOD"], Bc["GM"]], writes=(Hb if isinstance(Hb, list) else [Hb]))

        epsT = ar.f32(2)
        Bc["eps"] = em.buf("eps")
        EPS_AP = epsT[:, 0:1]
        NEGHALF = epsT[:, 1:2]
        em.op("pool", lambda e: e.memset(epsT[:, 0:1], EPS), writes=[Bc["eps"]])
        em.op("pool", lambda e: e.memset(epsT[:, 1:2], -0.5), writes=[Bc["eps"]])

        layer_mark = ar.mark()

        W0 = ar.bf(8, 2304)
        W0b = em.buf("W0")
        WO0 = ar.bf(8, 1024)
        WO0b = em.buf("WO0")
        KT0 = ar.bf(T)
        KT1 = ar.bf(T)
        KTb = em.bufs("kt", NT)
        VA = ar.bf(NT, 2, 66)
        VAb = em.bufs("va", NT)
        Hg = ar.bf(8, 512)
        Hgb = em.buf("Hg")
        alias_mark = ar.mark()
        QTg = ar.bf(8, 512)
        QTgb = em.bufs("qtg", 8)
        Gg = ar.bf(4, 1024)
        Ggb = em.bufs("gg", 4)
        alias_end = ar.mark()
        ar.release(alias_mark)
        mring = [ar.f32(8, 128) for _ in range(2)]
        mringbf = [ar.bf(8, 128) for _ in range(2)]
        assert ar.mark() <= alias_end
        ar.release(alias_end)
        rope_mark = ar.mark()
        ropeC = [ar.f32(512) for _ in range(2)]
        ropeS = [ar.f32(512) for _ in range(2)]
        ropeb = em.bufs("rope", 2)
        perm = ar.f32(128)
        qtmp = [ar.f32(512), ar.f32(512)]
        qtmpb = em.bufs("qtmp", 2)
        _t0 = ar.f32(512)
        t1r = [_t0, _t0]
        _tb = em.buf("t1r")
        t1rb = [_tb, _tb]
        t2r0 = ar.f32(512)
        t2r = [t2r0, t2r0]
        _save = ar.mark()
        ar.release(rope_mark)
        mring1 = [ar.f32(8, 128) for _ in range(2)]
        ar._alloc(128)
        mring1bf = [ar.bf(8, 128) for _ in range(2)]
        assert ar.mark() <= _save
        ar.release(_save)
        t2rb0 = em.buf("t2r")
        t2rb = [t2rb0, t2rb0]
        tmpr = [ar.f32(512) for _ in range(2)]
        tmprb = em.bufs("tmpr", 2)
        sqr = [ar.bf(512) for _ in range(2)]
        sqrb = em.bufs("sqr", 2)
        rden = [ar.f32(8) for _ in range(2)]
        rdenb = em.bufs("rden", 2)
        PTset = []
        PTb = []
        for sset in range(2):
            PTset.append([ar.bf(512) for _ in range(5)])
            PTb.append(em.bufs(f"pt{sset}_", 5))

        em.dma("sp", perm, perm_d, writes=[Bc["perm"]])
        for g in range(5):
            norm_stats(g, sqr, sqrb, tmpr[0], tmprb[0])
        Modulation(wada0_d, vec0, Bc["vec0"], mring, mringbf, MOD, GM, Bc["MOD"], Bc["GM"],
                   QTgb + Ggb[0:2], "0").run_all()
        for kc in range(8):
            for hf in range(2):
                em.dma("pool", W0[:, kc, hf * 1152:(hf + 1) * 1152], win0_d[:, kc, hf * 1152:(hf + 1) * 1152],
                       writes=[W0b])
        for kc in range(8):
            em.dma("pool", WO0[:, kc, :], wout0_d[:, kc, :], writes=[WO0b])
        em.op("pool", lambda e: e.memset(VA[:, :, :, 64:65], 1.0), writes=VAb)
        em.op("pool", lambda e: e.memset(KT0[64:128, :], 0.0), writes=KTb)
        em.op("pool", lambda e: e.memset(KT1[0:64, :], 0.0), writes=KTb)

        def load_rope(g, sl):
            t0, w = GROUPS[g]
            em.dma("sp", ropeC[sl][:, 0:w], ropec_d[:, t0:t0 + w], writes=[ropeb[sl]])
            em.dma("sp", ropeS[sl][:, 0:w], ropes_d[:, t0:t0 + w], writes=[ropeb[sl]])

        rope_ctr = [0]

        def proj_fm(dst, dstb, wcols, g, rope_sl, Hs=None, Hsb=None, split=None):
            t0, w = GROUPS[g]
            Hs = Hg if Hs is None else Hs
            Hsb = [Hgb] if Hsb is None else Hsb
            pq, pqb = ps_next()
            for kc in range(8):
                em.op("pe", lambda e, kc=kc: e.matmul(pq[:, 0:w], lhsT=W0[:, kc, wcols:wcols + 128],
                                                      rhs=Hs[:, kc, 0:w], start=(kc == 0), stop=(kc == 7)),
                      reads=[W0b] + Hsb, writes=[pqb])
            if g == 4:
                if split is None:
                    em.op("act", lambda e: e.activation(out=dst, in_=pq[:, 0:w], func=AF.Copy), reads=[pqb],
                          writes=dstb)
                else:
                    for (lo, hi, dd) in ((0, 64, split[0]), (64, 128, split[1])):
                        em.op("act", lambda e, lo=lo, hi=hi, dd=dd: e.activation(
                            out=dd[lo:hi, t0:t0 + w], in_=pq[lo:hi, 0:w], func=AF.Copy), reads=[pqb], writes=dstb)
                return
            i = rope_ctr[0] % 2
            rope_ctr[0] += 1
            em.op("act", lambda e: e.activation(out=qtmp[i][:, 0:w], in_=pq[:, 0:w], func=AF.Copy),
                  reads=[pqb], writes=[qtmpb[i]])
            yield
            pr, prb = ps_next()
            em.op("pe", lambda e: e.matmul(pr[:, 0:w], lhsT=perm, rhs=qtmp[i][:, 0:w], start=True, stop=True),
                  reads=[qtmpb[i], Bc["perm"]], writes=[prb])
            em.op("pool", lambda e: e.tensor_tensor(out=t1r[i][:, 0:w], in0=qtmp[i][:, 0:w],
                                                    in1=ropeC[rope_sl][:, 0:w], op=ALU.mult),
                  reads=[qtmpb[i], ropeb[rope_sl]], writes=[t1rb[i]])
            em.op("dve", lambda e: e.tensor_tensor(out=t2r[i][:, 0:w], in0=pr[:, 0:w], in1=ropeS[rope_sl][:, 0:w],
                                                   op=ALU.mult),
                  reads=[prb, ropeb[rope_sl]], writes=[t2rb[i]])
            if split is None:
                em.op("pool", lambda e: e.tensor_tensor(out=dst, in0=t1r[i][:, 0:w], in1=t2r[i][:, 0:w], op=ALU.add),
                      reads=[t1rb[i], t2rb[i]], writes=dstb)
            else:
                for (lo, hi, dd) in ((0, 64, split[0]), (64, 128, split[1])):
                    em.op("pool", lambda e, lo=lo, hi=hi, dd=dd: e.tensor_tensor(
                        out=dd[lo:hi, t0:t0 + w], in0=t1r[i][lo:hi, 0:w], in1=t2r[i][lo:hi, 0:w], op=ALU.add),
                        reads=[t1rb[i], t2rb[i]], writes=dstb)

        def vproj(g, tt, Hs, Hsb):
            t0, w = GROUPS[g]
            ti = t0 // 128 + tt
            pv, pvb = ps_next()
            for kc in range(8):
                em.op("pe", lambda e, kc=kc: e.matmul(pv[:, 0:128], lhsT=Hs[:, kc, tt * 128:(tt + 1) * 128],
                                                      rhs=W0[:, kc, 1152:1280], start=(kc == 0), stop=(kc == 7)),
                      reads=[W0b] + Hsb, writes=[pvb])
            em.op("dve", lambda e: e.tensor_copy(
                out=VA[:, ti, :, 0:64], in_=pv[:, 0:128].rearrange("p (a b) -> p a b", a=2)),
                reads=[pvb], writes=[VAb[ti]])

        def pass_a(g):
            t0, w = GROUPS[g]
            if g % 2 == 0:
                Hs, Hsb = Hg, [Hgb]
            else:
                Hs, Hsb = QTg, QTgb
            make_h(g, Hs, Hsb, tmpr, tmprb)
            if g < 4:
                load_rope(g, g % 2)
            kgen = proj_fm(None, KTb[t0 // 128:(t0 + w) // 128], 1024, g, g % 2, Hs, Hsb, split=(KT0, KT1))
            next(kgen, None)
            for tt in range(w // 128):
                vproj(g, tt, Hs, Hsb)
            for _ in kgen:
                pass

        for g in range(5 if stop >= 2 else 0):
            pass_a(g)

        head_ctr = [0]
        mask_ctr = [0]

        def gproj(g, tt, half):
            pg, pgb = ps_next()
            for kc in range(8):
                em.op("pe", lambda e, kc=kc: e.matmul(
                    pg[:, :], lhsT=Hg[:, kc, tt * 128:(tt + 1) * 128],
                    rhs=W0[:, kc, 1280 + half * 512:1280 + (half + 1) * 512], start=(kc == 0), stop=(kc == 7)),
                    reads=[W0b, Hgb], writes=[pgb])
            em.op("act", lambda e: e.activation(out=Gg[:, tt, half * 512:(half + 1) * 512], in_=pg[:, :],
                                                func=AF.Silu),
                  reads=[pgb], writes=[Ggb[tt]])

        def attn_head(g, j, hp):
            t0, w = GROUPS[g]
            nb = w // 128
            n0 = t0 // 128
            h = j + 8 * hp
            r0 = 64 * hp
            sset = head_ctr[0] % 2
            head_ctr[0] += 1
            PTs, PTbs = PTset[sset], PTb[sset]
            raw = [(16, 0, w), (17, 0, w)]
            if g < 4:
                for m in range(max(0, n0 - 1), min(15, n0 + nb) + 1):
                    na = max(m - 1, n0)
                    nb_ = min(m + 1, n0 + nb - 1)
                    raw.append((m, (na - n0) * 128, (nb_ - n0 + 1) * 128))
            bins = []
            for (m, qa, qb) in sorted(raw, key=lambda t: -(t[2] - t[1])):
                wd = qb - qa
                for bn in bins:
                    if bn["used"] + wd <= 512:
                        bn["items"].append((m, qa, qb, bn["used"]))
                        bn["used"] += wd
                        break
                else:
                    bins.append({"used": wd, "items": [(m, qa, qb, 0)]})
            assert len(bins) <= 5
            KTp = KT0 if hp == 0 else KT1
            tiles = []
            for sl, bn in enumerate(bins):
                pss, pssb = ps_next()
                for (m, qa, qb, off) in bn["items"]:
                    em.op("pe", lambda e, m=m, qa=qa, qb=qb, off=off, pss=pss: e.matmul(
                        pss[:, off:off + qb - qa], lhsT=KTp[:, m * 128:(m + 1) * 128],
                        rhs=QTg[:, j, qa:qb], start=True, stop=True),
                        reads=[KTb[m], QTgb[j]], writes=[pssb])
                    tiles.append((sl, m, qa, qb, off))
                em.op("act", lambda e, sl=sl, used=bn["used"], pss=pss: e.activation(
                    out=PTs[sl][:, 0:used], in_=pss[:, 0:used], func=AF.Exp, scale=0.125),
                    reads=[pssb], writes=[PTbs[sl]])
                for (m, qa, qb, off) in bn["items"]:
                    if m >= 16:
                        continue
                    for n in range(n0 + qa // 128, n0 + qb // 128):
                        c0 = off + (n - n0) * 128 - qa
                        if n == m - 1:
                            mk = mle
                        elif n == m + 1:
                            mk = mge
                        else:
                            continue
                        mask_ctr[0] += 1
                        em.op("pool" if mask_ctr[0] % 2 else "dve", lambda e, sl=sl, c0=c0, mk=mk: e.tensor_tensor(
                            out=PTs[sl][:, c0:c0 + 128], in0=PTs[sl][:, c0:c0 + 128], in1=mk, op=ALU.mult),
                            reads=[PTbs[sl], Bc["mask"]], writes=[PTbs[sl]])
            return (g, j, hp, tiles, PTs, PTbs)

        def attn_pv(ctx):
            g, j, hp, tiles, PTs, PTbs = ctx
            t0, w = GROUPS[g]
            nb = w // 128
            h = j + 8 * hp
            po, pob = ps_next()
            pov = po[:, 0:4 * 65].rearrange("p (a b) -> p a b", a=4)
            for bi in range(nb):
                use = [(sl, m, qa, off) for (sl, m, qa, qb, off) in tiles if qa <= bi * 128 < qb]
                for ui, (sl, m, qa, off) in enumerate(use):
                    c0 = off + bi * 128 - qa
                    em.op("pe", lambda e, sl=sl, m=m, c0=c0, bi=bi, ui=ui, nu=len(use):
                          e.matmul(pov[:, bi, :], lhsT=PTs[sl][:, c0:c0 + 128], rhs=VA[:, m, hp, 0:65],
                                   start=(ui == 0), stop=(ui == nu - 1)),
                          reads=[PTbs[sl], VAb[m]], writes=[pob])
            ri = head_ctr[0] % 2
            em.op("dve", lambda e: e.tensor_scalar(
                out=rden[ri][:, 0:nb], in0=pov[:, 0:nb, 64], scalar1=sinkexp[:, h:h + 1], scalar2=None,
                op0=ALU.add),
                reads=[pob, Bc["sinkexp"]], writes=[rdenb[ri]])
            em.op("dve", lambda e: e.reciprocal(out=rden[ri][:, 4:4 + nb], in_=rden[ri][:, 0:nb]),
                  reads=[rdenb[ri]], writes=[rdenb[ri]])
            for bi in range(nb):
                em.op("dve", lambda e, bi=bi: e.scalar_tensor_tensor(
                    out=Gg[:, bi, h * 64:(h + 1) * 64], in0=pov[:, bi, 0:64], scalar=rden[ri][:, 4 + bi:5 + bi],
                    in1=Gg[:, bi, h * 64:(h + 1) * 64], op0=ALU.mult, op1=ALU.mult),
                    reads=[pob, rdenb[ri], Ggb[bi]], writes=[Ggb[bi]])

        def tr_chunk(g, k):
            t0, w = GROUPS[g]
            nb = w // 128
            pt, ptb = ps_next()
            for tt in range(nb):
                em.op("pe", lambda e, tt=tt: e.matmul(
                    pt[:, tt * 128:(tt + 1) * 128], lhsT=Gg[:, tt, k * 128:(k + 1) * 128], rhs=ident_bf,
                    start=True, stop=True),
                    reads=[Ggb[tt], Bc["ident"]], writes=[ptb])
            em.op("act", lambda e: e.activation(out=Hg[:, k, 0:w], in_=pt[:, 0:w], func=AF.Copy),
                  reads=[ptb], writes=[Hgb])

        def oproj(g, c):
            t0, w = GROUPS[g]
            s = 1 if g == 4 else 0
            pp, ppb = ps_next()
            for kc in range(8):
                em.op("pe", lambda e, kc=kc: e.matmul(
                    pp[:, 0:w], lhsT=WO0[:, kc, c * 128:(c + 1) * 128], rhs=Hg[:, kc, 0:w],
                    start=(kc == 0), stop=(kc == 7)),
                    reads=[WO0b, Hgb], writes=[ppb])
            em.op("dve", lambda e: e.scalar_tensor_tensor(
                out=XS[:, c, t0:t0 + w], in0=pp[:, 0:w], scalar=MOD[:, 16 + c, s:s + 1], in1=XS[:, c, t0:t0 + w],
                op0=ALU.mult, op1=ALU.add),
                reads=[ppb, Bc["MOD"], XSb[c][g]], writes=[XSb[c][g]])

        def pass_b(g):
            t0, w = GROUPS[g]
            nb = w // 128
            make_h(g, Hg, Hgb, tmpr, tmprb)
            if g < 4:
                load_rope(g, g % 2)
            pgens = [proj_fm(QTg[:, j, 0:w], [QTgb[j]], j * 128, g, g % 2) for j in range(8)]
            next(pgens[0], None)
            for j in range(8):
                if j + 1 < 8:
                    next(pgens[j + 1], None)
                for _ in pgens[j]:
                    pass
            mod1 = None
            if g == 3 and stage >= 1:
                mod1 = Modulation(wada1_d, vec1, Bc["vec1"], mring1, mring1bf, MOD1, GM1, Bc["MOD1"], Bc["GM1"],
                                  [ropeb[0], ropeb[1], qtmpb[0], qtmpb[1]], "1")
                mod1.dma(0)
                mod1.dma(1)
            for tt in range(nb):
                for half in range(2):
                    gproj(g, tt, half)
            hi = 0
            prev = None
            for j in range(8):
                for hp in range(2):
                    cur = attn_head(g, j, hp)
                    if prev is not None:
                        attn_pv(prev)
                    prev = cur
                    if mod1 is not None and hi < 12:
                        for pc in (2 * hi, 2 * hi + 1):
                            mod1.mm(pc)
                            if pc + 2 < 24:
                                mod1.dma(pc + 2)
                    hi += 1
            attn_pv(prev)
            if mod1 is not None:
                mod1.finish()
            for k in range(8):
                tr_chunk(g, k)
            for c in range(8):
                oproj(g, c)

        for g in range(5 if stop >= 3 else 0):
            pass_b(g)

        em.barrier()
        ar.release(layer_mark)

        if stage >= 1:
            build_layer1(nc, em, ar, locals())

        outs = []
        if stage == 0:
            for j in range(8):
                for g in range(4):
                    t0, w = GROUPS[g]
                    outs.append(em.dma("sp", y_d[:, j, t0:t0 + w], XS[:, j, t0:t0 + w], reads=[XSb[j][g]]))
        else:
            outs = L1_OUTS
        stats = em.emit(st, final_waits=outs)
        build_program.stats = (stats, ar.peak)
    return nc


L1_OUTS = []


def build_layer1(nc, em, ar, L):
    del L1_OUTS[:]
    XS, XSb, RSTD = L["XS"], L["XSb"], L["RSTD"]
    MOD, GM, Bc, vec1 = L["MOD1"], L["GM1"], L["Bc"], L["vec1"]
    ones_bf, ident_bf, mle, mge = L["ones_bf"], L["ident_bf"], L["mle"], L["mge"]
    ps_next = L["ps_next"]
    PSD, PSDb = L["PSD"], L["PSDb"]
    L["ring_n"][0] = 8
    EPS_AP = L["EPS_AP"]
    NEGHALF = L["NEGHALF"]
    win1_d, wout1_d, wa1_d, wa2_d, rmask_d, y_d = (L["win1_d"], L["wout1_d"], L["wa1_d"], L["wa2_d"],
                                                  L["rmask_d"], L["y_d"])
    KSCALE = 128.0 ** -0.5
    C16 = 1.0 / 16.0

    RS = RSTD[:, 0:512]
    RSb = em.buf("RS")
    RS2 = [RSTD[:, 0:512], RSTD[:, 1664:2176]]
    RS2b = [RSb, em.buf("RSx")]
    RTf = RSTD[:, 512:1664].bitcast(BF16)
    H = ar.bf(8, T)
    Hb = em.bufs("H", 5)
    Wqkv = ar.bf(8, 512)
    Wqkvb = em.buf("Wqkv")
    Wg = ar.bf(8, 256)
    Wgb = em.buf("Wg")
    WOh = ar.bf(2, 1024)
    WOhb = em.buf("WOh")
    WA1 = ar.bf(8, 32)
    NBA = ar.f32(8)
    rmask = ar.f32(512)
    Bw = {k: em.buf(k) for k in ["WA1", "WA2", "RT", "NBA", "rmask"]}
    XC = [XS[:, j, S:T] for j in range(8)]
    R = [[XC[0], XC[1]], [XC[2], XC[3]]]
    Sfb = [XC[4][:, 0:128].bitcast(BF16), XC[4][:, 128:256].bitcast(BF16)]
    AT = [[XC[5][:, 0:64].bitcast(BF16), XC[5][:, 64:128].bitcast(BF16)],
          [XC[5][:, 128:192].bitcast(BF16), XC[5][:, 192:256].bitcast(BF16)]]
    ATpair = [XC[5][:, 0:128].bitcast(BF16), XC[5][:, 128:256].bitcast(BF16)]
    mask2 = L["mask"]
    WA2 = [XC[6].bitcast(BF16), XC[7].bitcast(BF16)]
    head_mark = ar.mark()
    QD = [ar.bf(S), ar.bf(S)]
    KI = [ar.bf(T), ar.bf(T)]
    KItm = [ar.bf(NT, 128), ar.bf(NT, 128)]
    DEC = ar.f32(2, NT)
    V = ar.bf(NT, 256)
    SBST = ar.bf(16, 256)
    tA = [ar.f32(512), ar.f32(512)]
    tB = [ar.f32(512), ar.f32(512)]
    TOT = [ar.f32(4), ar.f32(4)]
    Og = [ar.f32(2, 512), ar.f32(2, 512)]
    RO = ar.f32(512)
    sq2 = [ar.bf(512), ar.bf(512)]
    _tx = ar.f32(512)
    tX = [_tx, _tx]
    OGT = ar.bf(2, 512)
    tmpr = [Og[0][:, 0, :], Og[1][:, 0, :]]
    sqr = sq2
    tmps = tX[0]
    names = ["R00", "R01", "R10", "R11", "Sfb0", "Sfb1",
             "AT00", "AT01", "AT10", "AT11", "tA0", "tA1", "tB0", "tB1", "TOT0", "TOT1",
             "Og0", "Og1", "RO", "sq0", "sq1", "tX0", "OGT"]
    Bh = {k: em.buf(k) for k in names}
    Bh["tm0"], Bh["tm1"], Bh["sr0"], Bh["sr1"], Bh["tms"] = Bh["Og0"], Bh["Og1"], Bh["sq0"], Bh["sq1"], Bh["tX0"]
    RTb = em.bufs("rt", 5)
    DECb = [[em.buf("dec%d_%d" % (d, g)) for g in range(5)] for d in range(2)]
    Vb = em.bufs("V", 5)
    KTb = [em.bufs("ktm0_", 5), em.bufs("ktm1_", 5)]
    KIb = [em.bufs("ki0_", 5), em.bufs("ki1_", 5)]
    QDb = [em.bufs("qd0_", 4), em.bufs("qd1_", 4)]
    SBb = em.bufs("sbst", 4)
    STOP = L["stop"]

    em.dma("pool", WA1, wa1_d, writes=[Bw["WA1"]])
    em.dma("sp", rmask, rmask_d, writes=[Bw["rmask"]])
    em.op("dve", lambda e: e.tensor_scalar(out=NBA, in0=vec1[:, 48:56], scalar1=-1.0, scalar2=None, op0=ALU.mult),
          reads=[Bc["vec1"]], writes=[Bw["NBA"]])

    def load_qkv(h):
        for (c_lo, n, dst_lo) in ((h * 128, 128, 0), (512 + h * 128, 128, 128), (1024 + h * 256, 256, 256)):
            em.dma("pool", Wqkv[:, :, dst_lo:dst_lo + n], win1_d[:, :, c_lo:c_lo + n], writes=[Wqkvb])

    def load_g(h):
        em.dma("pool", Wg, win1_d[:, :, 2048 + h * 256:2048 + (h + 1) * 256], writes=[Wgb])
        em.dma("pool", WOh, wout1_d[:, 2 * h:2 * h + 2, :], writes=[WOhb])

    load_qkv(0)
    load_g(0)

    def stats(src_fn, nchunk, inv_n, out_ap, outb, w, sq, sqb, tmp, tmpb):
        pss, pssb = ps_next()
        for j in range(nchunk):
            sl = j % 2
            sap, sb = src_fn(j)
            em.op("act", lambda e, sl=sl, sap=sap: e.activation(out=sq[sl][:, 0:w], in_=sap, func=AF.Square),
                  reads=[sb], writes=[sqb[sl]])
            em.op("pe", lambda e, sl=sl, j=j: e.matmul(pss[:, 0:w], lhsT=ones_bf, rhs=sq[sl][:, 0:w],
                                                      start=(j == 0), stop=(j == nchunk - 1)),
                  reads=[sqb[sl], Bc["ones"]], writes=[pssb])
        em.op("act", lambda e: e.activation(out=tmp[:, 0:w], in_=pss[:, 0:w], func=AF.Ln, bias=EPS_AP, scale=inv_n),
              reads=[pssb, Bc["eps"]], writes=[tmpb])
        em.op("act", lambda e: e.activation(out=out_ap, in_=tmp[:, 0:w], func=AF.Exp, scale=-0.5),
              reads=[tmpb], writes=[outb])

    def setup_group(g, ri):
        t0, w = GROUPS[g]
        s = 1 if g == 4 else 0
        RSg, RSgb = RS2[ri], RS2b[ri]
        for j in range(8):
            sl = j % 2
            em.op("dve", lambda e, j=j, sl=sl: e.tensor_tensor(out=tmpr[sl][:, 0:w], in0=XS[:, j, t0:t0 + w],
                                                               in1=RSg[:, 0:w], op=ALU.mult),
                  reads=[XSb[j][g], RSgb], writes=[Bh["tm%d" % sl]])
            em.op("act", lambda e, j=j, sl=sl: e.activation(out=H[:, j, t0:t0 + w], in_=tmpr[sl][:, 0:w],
                                                            func=AF.Identity, bias=MOD[:, j, s:s + 1],
                                                            scale=GM[:, j, s:s + 1]),
                  reads=[Bh["tm%d" % sl], Bc["MOD1"], Bc["GM1"]], writes=[Hb[g]])
        pr, prb = ps_next()
        for kc in range(8):
            em.op("pe", lambda e, kc=kc: e.matmul(pr[0:32, 0:w], lhsT=WA1[:, kc, 0:32], rhs=H[:, kc, t0:t0 + w],
                                                  start=(kc == 0), stop=(kc == 7)),
                  reads=[Bw["WA1"], Hb[g]], writes=[prb])
        em.op("act", lambda e: e.activation(out=RTf[0:32, t0:t0 + w], in_=pr[0:32, 0:w], func=AF.Copy),
              reads=[prb], writes=[RTb[g]])

    def setup_stats(g, ri):
        t0, w = GROUPS[g]
        stats(lambda j: (XS[:, j, t0:t0 + w], XSb[j][g]), 8, 1.0 / D, RS2[ri][:, 0:w], RS2b[ri], w,
              sqr, [Bh["sr0"], Bh["sr1"]], tmps, Bh["tms"])

    def prep_dir(h, g, d, P):
        t0, w = GROUPS[g]
        nch = w // 128
        c0 = t0 // 128
        lat = g < 4
        a_, b_, tot = tA[d], tB[d], TOT[d]
        ab, bb, totb = Bh["tA%d" % d], Bh["tB%d" % d], Bh["TOT%d" % d]
        pz, pzb = ps_next()
        em.op("pe", lambda e: e.matmul(pz[:, 0:w], lhsT=WA2[d][0:32, h * 128:(h + 1) * 128],
                                       rhs=RTf[0:32, t0:t0 + w], start=True, stop=True),
              reads=[Bw["WA2"], RTb[g]], writes=[pzb])
        em.op("act", lambda e: e.activation(out=a_[:, 0:w], in_=pz[:, 0:w], func=AF.Exp, scale=-1.0,
                                            bias=NBA[:, d * 4 + h:d * 4 + h + 1]),
              reads=[pzb, Bw["NBA"]], writes=[ab])
        yield
        em.op("act", lambda e: e.activation(out=a_[:, 0:w], in_=a_[:, 0:w], func=AF.Ln, bias=1.0),
              reads=[ab], writes=[ab])
        yield
        em.op("dve", lambda e: e.tensor_tensor_scan(out=b_[:, 0:w], data0=rmask[:, 0:w], data1=a_[:, 0:w],
                                                    initial=0.0, op0=ALU.mult, op1=ALU.add),
              reads=[ab, Bw["rmask"]], writes=[bb])
        yield
        em.op("act", lambda e: e.activation(out=DEC[:, d, c0:c0 + nch], in_=b_[:, 127:w:128], func=AF.Exp, scale=-C16),
              reads=[bb], writes=[DECb[d][g]])
        yield
        if d == 0:
            sc1, sc2 = -C16, C16
        else:
            em.op("dve", lambda e: e.tensor_copy(out=tot[:, 0:nch], in_=b_[:, 127:w:128]), reads=[bb], writes=[totb])
            b3 = b_[:, 0:w].rearrange("p (c t) -> p c t", t=128)
            em.op("dve", lambda e: e.tensor_tensor(out=b3, in0=b3,
                                                   in1=tot[:, 0:nch].unsqueeze(2).to_broadcast([128, nch, 128]),
                                                   op=ALU.subtract),
                  reads=[bb, totb], writes=[bb])
            em.op("dve", lambda e: e.tensor_tensor(out=b_[:, 0:w], in0=b_[:, 0:w], in1=a_[:, 0:w], op=ALU.subtract),
                  reads=[bb, ab], writes=[bb])
            sc1, sc2 = C16, -C16
        yield
        if lat:
            pq, pqb = P["pq"]
            em.op("act", lambda e: e.activation(out=a_[:, 0:w], in_=b_[:, 0:w], func=AF.Exp, scale=sc1),
                  reads=[bb], writes=[ab])
            em.op("dve", lambda e: e.scalar_tensor_tensor(
                out=QD[d][:, t0:t0 + w], in0=pq[:, 0:w], scalar=KSCALE, in1=a_[:, 0:w], op0=ALU.mult, op1=ALU.mult),
                reads=[pqb, ab], writes=[QDb[d][g]])
        yield
        pk, pkb = P["pk"]
        em.op("act", lambda e: e.activation(out=b_[:, 0:w], in_=b_[:, 0:w], func=AF.Exp, scale=sc2),
              reads=[bb], writes=[bb])
        em.op("dve", lambda e: e.tensor_tensor(out=KI[d][:, t0:t0 + w], in0=pk[:, 0:w], in1=b_[:, 0:w], op=ALU.mult),
              reads=[pkb, bb], writes=[KIb[d][g]])
        yield
        pt, ptb = ps_next()
        for c in range(nch):
            em.op("pe", lambda e, c=c: e.matmul(pt[:, c * 128:(c + 1) * 128],
                                                lhsT=KI[d][:, t0 + c * 128:t0 + (c + 1) * 128], rhs=ident_bf,
                                                start=True, stop=True),
                  reads=[KIb[d][g], Bc["ident"]], writes=[ptb])
        em.op("act", lambda e: e.activation(
            out=KItm[d][:, c0:c0 + nch, :], in_=pt[:, 0:w].rearrange("p (c t) -> p c t", t=128), func=AF.Copy),
            reads=[ptb], writes=[KTb[d][g]])

    def prep_group(h, g):
        t0, w = GROUPS[g]
        nch = w // 128
        lat = g < 4
        P = {}
        gens = [prep_dir(h, g, d, P) for d in range(2)]
        alive = [True, True]
        rnd = 0
        while any(alive):
            rnd += 1
            if rnd == 6 and lat:
                pq, pqb = ps_next()
                for kc in range(8):
                    em.op("pe", lambda e, kc=kc, pq=pq: e.matmul(pq[:, 0:w], lhsT=Wqkv[:, kc, 0:128],
                                                                 rhs=H[:, kc, t0:t0 + w], start=(kc == 0),
                                                                 stop=(kc == 7)),
                          reads=[Wqkvb, Hb[g]], writes=[pqb])
                P["pq"] = (pq, pqb)
            if rnd == 7:
                pk, pkb = ps_next()
                for kc in range(8):
                    em.op("pe", lambda e, kc=kc, pk=pk: e.matmul(pk[:, 0:w], lhsT=Wqkv[:, kc, 128:256],
                                                                 rhs=H[:, kc, t0:t0 + w], start=(kc == 0),
                                                                 stop=(kc == 7)),
                          reads=[Wqkvb, Hb[g]], writes=[pkb])
                P["pk"] = (pk, pkb)
            for d in range(2):
                if alive[d]:
                    try:
                        next(gens[d])
                    except StopIteration:
                        alive[d] = False
            yield
        for tt in range(nch):
            vproj(h, g, tt)
            yield

    def vproj(h, g, tt):
        t0, w = GROUPS[g]
        ti = t0 // 128 + tt
        pv, pvb = ps_next()
        for kc in range(8):
            em.op("pe", lambda e, kc=kc: e.matmul(pv[:, 0:256], lhsT=H[:, kc, ti * 128:(ti + 1) * 128],
                                                  rhs=Wqkv[:, kc, 256:512], start=(kc == 0), stop=(kc == 7)),
                  reads=[Wqkvb, Hb[g]], writes=[pvb])
        em.op("act", lambda e: e.activation(out=V[:, ti, :], in_=pv[:, 0:256], func=AF.Copy),
              reads=[pvb], writes=[Vb[g]])

    rstate = {0: [0, None, None], 1: [0, None, None]}

    def chain_reset(pair, d):
        rstate[pair] = [0, None, d]

    def chain_step(pair, c):
        cur, prev, d = rstate[pair]
        nxt = 1 - cur
        g = c // 4
        pkv, pkvb = ps_next()
        em.op("pe", lambda e: e.matmul(pkv[:, 0:256], lhsT=KItm[d][:, c, :], rhs=V[:, c, :], start=True, stop=True),
              reads=[KTb[d][g], Vb[g]], writes=[pkvb])
        if prev is None:
            em.op("dve", lambda e: e.tensor_copy(out=R[pair][nxt], in_=pkv[:, 0:256]),
                  reads=[pkvb], writes=[Bh["R%d%d" % (pair, nxt)]])
        else:
            em.op("dve", lambda e: e.scalar_tensor_tensor(out=R[pair][nxt], in0=R[pair][cur],
                                                          scalar=DEC[:, d, prev:prev + 1],
                                                          in1=pkv[:, 0:256], op0=ALU.mult, op1=ALU.add),
                  reads=[Bh["R%d%d" % (pair, cur)], DECb[d][prev // 4], pkvb], writes=[Bh["R%d%d" % (pair, nxt)]])
        rstate[pair][0] = nxt
        rstate[pair][1] = c

    def state_bf16(pair, out_ap, outb):
        cur, prev, d = rstate[pair]
        em.op("act", lambda e: e.activation(out=out_ap, in_=R[pair][cur], func=AF.Identity,
                                            scale=DEC[:, d, prev:prev + 1]),
              reads=[Bh["R%d%d" % (pair, cur)], DECb[d][prev // 4]], writes=[outb])

    def out_a(h, c):
        ai = c % 2
        g = c // 4
        cs = slice(c * 128, (c + 1) * 128)
        pa, pab = ps_next()
        for d in range(2):
            em.op("pe", lambda e, d=d: e.matmul(pa[:, d * 128:(d + 1) * 128], lhsT=KI[d][:, cs], rhs=QD[d][:, cs],
                                                start=True, stop=True),
                  reads=[KIb[d][g], QDb[d][g]], writes=[pab])
        em.op("dve", lambda e: e.tensor_tensor(out=ATpair[ai], in0=pa[:, 0:256], in1=mask2, op=ALU.mult),
              reads=[pab, Bc["mask"]], writes=[Bh["AT%d0" % ai], Bh["AT%d1" % ai]])

    def out_b(h, c, ds, do_chain):
        pair = h % 2
        dc = 1 - ds
        ai = c % 2
        g = c // 4
        cs = slice(c * 128, (c + 1) * 128)
        state_bf16(pair, Sfb[ai], Bh["Sfb%d" % ai])
        if do_chain:
            chain_step(pair, c)
        pos = [ps_next(), ps_next()]
        for ec in range(2):
            es = slice(ec * 128, (ec + 1) * 128)
            po, pob = pos[ec]
            em.op("pe", lambda e, es=es, po=po: e.matmul(po[:, 0:128], lhsT=V[:, c, es], rhs=AT[ai][0], start=True,
                                                         stop=False),
                  reads=[Vb[g], Bh["AT%d0" % ai]], writes=[pob])
            em.op("pe", lambda e, es=es, po=po: e.matmul(po[:, 0:128], lhsT=V[:, c, es], rhs=AT[ai][1], start=False,
                                                         stop=False),
                  reads=[Vb[g], Bh["AT%d1" % ai]], writes=[pob])
            em.op("pe", lambda e, es=es, po=po: e.matmul(po[:, 0:128], lhsT=SBST[:, c, es], rhs=QD[dc][:, cs],
                                                         start=False, stop=False),
                  reads=[SBb[g], QDb[dc][g]], writes=[pob])
        cc = c % 4
        oi = g % 2
        for ec in range(2):
            es = slice(ec * 128, (ec + 1) * 128)
            po, pob = pos[ec]
            em.op("pe", lambda e, es=es, po=po: e.matmul(po[:, 0:128], lhsT=Sfb[ai][:, es], rhs=QD[ds][:, cs],
                                                         start=False, stop=True),
                  reads=[Bh["Sfb%d" % ai], QDb[ds][g]], writes=[pob])
            em.op("act", lambda e, ec=ec, po=po: e.activation(out=Og[oi][:, ec, cc * 128:(cc + 1) * 128],
                                                              in_=po[:, 0:128], func=AF.Copy),
                  reads=[pob], writes=[Bh["Og%d" % oi]])

    def epilogue12(h, g):
        t0, w = GROUPS[g]
        oi = g % 2
        Ogg, Oggb = Og[oi], Bh["Og%d" % oi]
        stats(lambda j: (Ogg[:, j, :], Oggb), 2, 1.0 / 256.0, RO, Bh["RO"], 512,
              sq2, [Bh["sq0"], Bh["sq1"]], tX[0], Bh["tX0"])
        for ec in range(2):
            pg, pgb = ps_next()
            for kc in range(8):
                em.op("pe", lambda e, kc=kc, ec=ec, pg=pg: e.matmul(
                    pg[:, :], lhsT=Wg[:, kc, ec * 128:(ec + 1) * 128], rhs=H[:, kc, t0:t0 + w],
                    start=(kc == 0), stop=(kc == 7)),
                    reads=[Wgb, Hb[g]], writes=[pgb])
            em.op("dve", lambda e, ec=ec: e.scalar_tensor_tensor(
                out=Ogg[:, ec, :], in0=Ogg[:, ec, :], scalar=vec1[:, 32 + 2 * h + ec:33 + 2 * h + ec], in1=RO,
                op0=ALU.mult, op1=ALU.mult),
                reads=[Oggb, Bc["vec1"], Bh["RO"]], writes=[Oggb])
            em.op("act", lambda e, pg=pg, ec=ec: e.activation(out=tX[0], in_=pg[:, :], func=AF.Silu), reads=[pgb],
                  writes=[Bh["tX0"]])
            em.op("pool", lambda e, ec=ec: e.tensor_tensor(out=OGT[:, ec, :], in0=Ogg[:, ec, :], in1=tX[0],
                                                           op=ALU.mult),
                  reads=[Oggb, Bh["tX0"]], writes=[Bh["OGT"]])

    def epilogue3(h, g):
        t0, w = GROUPS[g]
        for cch in range(8):
            pp, ppb = ps_next()
            for ec in range(2):
                em.op("pe", lambda e, ec=ec, cch=cch, pp=pp: e.matmul(
                    pp[:, :], lhsT=WOh[:, ec, cch * 128:(cch + 1) * 128], rhs=OGT[:, ec, :],
                    start=(ec == 0), stop=(ec == 1)),
                    reads=[WOhb, Bh["OGT"]], writes=[ppb])
            em.op("dve", lambda e, cch=cch, pp=pp: e.scalar_tensor_tensor(
                out=XS[:, cch, t0:t0 + w], in0=pp[:, :], scalar=MOD[:, 16 + cch, 0:1], in1=XS[:, cch, t0:t0 + w],
                op0=ALU.mult, op1=ALU.add),
                reads=[ppb, Bc["MOD1"], XSb[cch][g]], writes=[XSb[cch][g]])

    def final_group(g):
        t0, w = GROUPS[g]
        stats(lambda j: (XS[:, j, t0:t0 + w], XSb[j][g]), 8, 1.0 / D, RS2[g % 2][:, 0:w], RS2b[g % 2], w,
              sq2, [Bh["sq0"], Bh["sq1"]], tX[0], Bh["tX0"])
        for j in range(8):
            em.op("dve", lambda e, j=j: e.scalar_tensor_tensor(
                out=XS[:, j, t0:t0 + w], in0=XS[:, j, t0:t0 + w], scalar=vec1[:, 40 + j:41 + j],
                in1=RS2[g % 2][:, 0:w], op0=ALU.mult, op1=ALU.mult),
                reads=[XSb[j][g], Bc["vec1"], RS2b[g % 2]], writes=[XSb[j][g]])
            L1_OUTS.append(em.dma("sp", y_d[:, j, t0:t0 + w], XS[:, j, t0:t0 + w], reads=[XSb[j][g]]))

    def chunks_of(g, d):
        if g == 4:
            return [16, 17] if d == 0 else [17, 16]
        cs = list(range(4 * g, 4 * g + 4))
        return cs if d == 0 else cs[::-1]

    def stored_chain_group(h, g):
        pair = h % 2
        dc = h % 2
        step = 1 if dc == 0 else -1
        for c in chunks_of(g, dc):
            chain_step(pair, c)
            if g == 4:
                nxt = (0 if dc == 0 else 15) if c == chunks_of(4, dc)[-1] else None
            else:
                nxt = c + step
            if nxt is not None and 0 <= nxt <= 15:
                state_bf16(pair, SBST[:, nxt, :], SBb[nxt // 4])
            yield

    def sweep_group(h, g, next_first):
        pair = h % 2
        ds = 1 - (h % 2)
        cs = chunks_of(g, ds)
        for i, c in enumerate(cs):
            nxt = cs[i + 1] if i + 1 < len(cs) else next_first
            if nxt is not None:
                out_a(h, nxt)
            last_overall = (c == (0 if ds == 1 else 15))
            out_b(h, c, ds, not last_overall)
            yield
            if i == 0 and pending_ep3:
                gg = pending_ep3.pop()
                epilogue3(h, gg)
                if h == 3:
                    final_group(gg)
                yield
        epilogue12(h, g)
        pending_ep3.append(g)
        yield

    pending_ep3 = []

    def drive(gens_a, gens_b, ratio=2):
        A = list(gens_a)
        B = list(gens_b)
        if STOP == 77:
            for g_ in A:
                for _ in g_:
                    pass
            em.barrier()
            for g_ in B:
                for _ in g_:
                    pass
            em.barrier()
            return
        ia = ib = 0
        while ia < len(A) or ib < len(B):
            if ia < len(A):
                try:
                    next(A[ia])
                except StopIteration:
                    ia += 1
            for _ in range(ratio):
                if ib < len(B):
                    try:
                        next(B[ib])
                    except StopIteration:
                        ib += 1

    def prep_order(h):
        return [4, 0, 1, 2, 3] if h % 2 == 0 else [4, 3, 2, 1, 0]

    chain_reset(0, 0)
    po0 = prep_order(0)
    setup_stats(po0[0], 0)
    for i, g in enumerate(po0):
        if i + 1 < 5:
            setup_stats(po0[i + 1], (i + 1) % 2)
        setup_group(g, i % 2)
        if i == 0:
            for d in range(2):
                em.dma("pool", WA2[d][0:32, :], wa2_d[:, d, :], writes=[Bw["WA2"], XSb[6][4], XSb[7][4]])
        if i >= 1:
            for _ in prep_group(0, po0[i - 1]):
                pass
        if i >= 2:
            for _ in stored_chain_group(0, po0[i - 2]):
                pass
    for _ in prep_group(0, po0[4]):
        pass
    for _ in stored_chain_group(0, po0[3]):
        pass
    for _ in stored_chain_group(0, po0[4]):
        pass

    for h in range(4):
        pair = h % 2
        ds = 1 - (h % 2)
        if h < 3:
            load_qkv(h + 1)
        chain_reset(pair, ds)
        for c in chunks_of(4, ds):
            chain_step(pair, c)
        sg = [3, 2, 1, 0] if ds == 1 else [0, 1, 2, 3]
        out_a(h, chunks_of(sg[0], ds)[0])
        if h < 3:
            chain_reset(1 - pair, (h + 1) % 2)
            pg = prep_order(h + 1)
            assert pg[1:] == sg
        for k in range(5):
            A = []
            if k < 4:
                nf = chunks_of(sg[k + 1], ds)[0] if k + 1 < 4 else None
                A = [sweep_group(h, sg[k], nf)]
            B = []
            if h < 3:
                B.append(prep_group(h + 1, pg[k]))
                if k > 0:
                    B.append(stored_chain_group(h + 1, pg[k - 1]))
            drive(A, B)
        if h < 3:
            for _ in stored_chain_group(h + 1, pg[4]):
                pass
        while pending_ep3:
            gg = pending_ep3.pop()
            epilogue3(h, gg)
            if h == 3:
                final_group(gg)
        if h < 3:
            load_g(h + 1)


def _kc_layout(w):
    n = w.shape[1]
    return np.ascontiguousarray(w.reshape(8, 128, n).transpose(1, 0, 2))


def _vec_layout(v):
    return np.ascontiguousarray(v.reshape(-1, 128).T)


def _wa2_layout(wf, wb):
    o = np.zeros((32, 2, 512), np.float32)
    o[0:16, 0, :] = wf
    o[16:32, 1, :] = wb
    return o


def _rope_tables():
    pos = np.arange(S)
    row = (pos // 64).astype(np.float64)
    col = (pos % 64).astype(np.float64)
    inv_freq = 10000.0 ** (-np.arange(16, dtype=np.float64) / 16.0)
    C = np.zeros((128, S), np.float32)
    Sn = np.zeros((128, S), np.float32)
    for r in range(128):
        d = r % 64
        a = d // 32
        f = d % 16
        ang = (row if a == 0 else col) * inv_freq[f]
        C[r] = np.cos(ang).astype(np.float32)
        Sn[r] = np.sin(ang).astype(np.float32)
    P = np.zeros((128, 128), np.float32)
    for m in range(128):
        d = m % 64
        jj = (d % 32) // 16
        if jj == 0:
            P[m + 16, m] = -1.0
        else:
            P[m - 16, m] = 1.0
    return C, Sn, P


def prepare_inputs(x, c, ctx, c_ctx, l0_norm_g, l0_w_ada, l0_b_ada, l0_w_in, l0_sink, l0_w_out,
                   l1_norm_g, l1_w_ada, l1_b_ada, l1_w_in, l1_wa1_f, l1_wa2_f, l1_ba_f,
                   l1_wa1_b, l1_wa2_b, l1_ba_b, l1_head_norm_g, l1_w_out, final_norm_g):
    f = lambda a: np.asarray(a, dtype=np.float32)
    x, c, ctx, c_ctx = f(x), f(c), f(ctx), f(c_ctx)
    C, Sn, P = _rope_tables()
    ii = np.arange(128)
    mask = np.concatenate([(ii[:, None] <= ii[None, :]), (ii[:, None] >= ii[None, :])], axis=1).astype(np.float32)
    rmask = np.ones((128, 512), np.float32)
    rmask[:, ::128] = 0.0
    qcols = []
    for j in range(8):
        for r in range(128):
            hh = j if r < 64 else 8 + j
            qcols.append(hh * 64 + (r % 64))
    w_in0 = f(l0_w_in)
    cols = np.concatenate([np.array(qcols), np.arange(1024, 1152), np.arange(1152, 1280), np.arange(1280, 2304)])
    shared = {
        "ropec": C, "ropes": Sn, "perm": P, "ident": np.eye(128, dtype=np.float32), "mask": mask, "rmask": rmask,
        "sink": np.ascontiguousarray(np.broadcast_to(f(l0_sink)[None, :], (128, 16))),
        "vec0": np.concatenate([_vec_layout(f(l0_norm_g)), _vec_layout(f(l0_b_ada))], axis=1),
        "vec1": np.concatenate([_vec_layout(f(l1_norm_g)), _vec_layout(f(l1_b_ada)), _vec_layout(f(l1_head_norm_g)),
                                _vec_layout(f(final_norm_g)), _vec_layout(f(l1_ba_f)), _vec_layout(f(l1_ba_b))],
                               axis=1),
        "wada0": _kc_layout(f(l0_w_ada)), "win0": _kc_layout(w_in0[:, cols]), "wout0": _kc_layout(f(l0_w_out)),
        "wada1": _kc_layout(f(l1_w_ada)), "win1": _kc_layout(f(l1_w_in)), "wout1": _kc_layout(f(l1_w_out)),
        "wa1": _kc_layout(np.concatenate([f(l1_wa1_f), f(l1_wa1_b)], axis=1)),
        "wa2": _wa2_layout(f(l1_wa2_f), f(l1_wa2_b)),
    }
    in_maps = []
    for b in range(8):
        cat = np.concatenate([x[b], ctx[b]], axis=0)
        xs = np.ascontiguousarray(cat.T.reshape(8, 128, T).transpose(1, 0, 2))
        ccb = np.ascontiguousarray(np.stack([_vec_layout(c[b]), _vec_layout(c_ctx)], axis=2))
        m = {"xs": xs, "cc": ccb}
        m.update(shared)
        in_maps.append(m)
    return in_maps


_NC_CACHE = {}


def kernel(**inputs):
    in_maps = prepare_inputs(**inputs)
    if "nc" not in _NC_CACHE:
        _NC_CACHE["nc"] = build_program(stage=2)
    nc = _NC_CACHE["nc"]
    res = run_bass_kernel_spmd(nc, in_maps, core_ids=list(range(8)))
    out = np.empty((8, S, D), np.float32)
    for b in range(8):
        y = res.results[b]["y"]
        out[b] = y.transpose(2, 1, 0).reshape(S, D)
    return out
```

```python
import contextlib
import numpy as np
import concourse.bass as bass
import concourse.mybir as mybir
from concourse.bass_utils import run_bass_kernel_spmd

F32 = mybir.dt.float32
BF16 = mybir.dt.bfloat16
AF = mybir.ActivationFunctionType
ALU = mybir.AluOpType

D = 1024
S = 2048
CTX = 256
T = S + CTX
NT = T // 128
EPS = 1e-6
GROUPS = [(0, 512), (512, 512), (1024, 512), (1536, 512), (2048, 256)]


class Buf:
    __slots__ = ("name", "w", "r")

    def __init__(self, name):
        self.name = name
        self.w = None
        self.r = []


class Op:
    __slots__ = ("eng", "fn", "deps", "kind", "sem", "val", "needed", "seq", "waits", "W")

    def __init__(self, eng, fn, kind):
        self.eng = eng
        self.fn = fn
        self.kind = kind
        self.deps = []
        self.sem = None
        self.val = 0
        self.needed = False


class Em:
    ENGS = ("pe", "act", "dve", "pool", "sp")

    def __init__(self, nc, n_dma_sems=32):
        self.nc = nc
        self.ops = {e: [] for e in self.ENGS}
        self.n_dma_sems = n_dma_sems
        self.dma_last = [None] * n_dma_sems
        self.dma_cnt = [0] * n_dma_sems
        self.dma_rr = 0
        self.n_sw = 0
        self.seq = 0
        self.pending_barrier = {}

    def buf(self, name="b"):
        return Buf(name)

    def bufs(self, name, n):
        return [Buf(f"{name}{i}") for i in range(n)]

    def _track(self, op, reads, writes):
        deps = op.deps
        for b in reads:
            if b.w is not None:
                deps.append(b.w)
        for b in writes:
            if b.w is not None:
                deps.append(b.w)
            deps.extend(b.r)
        for b in reads:
            b.r.append(op)
        for b in writes:
            b.w = op
            b.r = []

    def op(self, eng, fn, reads=(), writes=()):
        o = Op(eng, fn, "c")
        o.seq = self.seq
        self.seq += 1
        if self.pending_barrier.get(eng):
            o.deps.extend(self.pending_barrier.pop(eng))
        self._track(o, reads, writes)
        self.ops[eng].append(o)
        return o

    def dma(self, eng, out, in_, reads=(), writes=(), **kw):
        o = Op(eng, (lambda e, out=out, in_=in_, kw=kw: e.dma_start(out=out, in_=in_, **kw)), "d")
        o.seq = self.seq
        self.seq += 1
        if eng == "pool":
            k = self.n_dma_sems + self.n_sw
            self.n_sw += 1
            o.sem = ("d", k)
            o.val = 16
        else:
            k = self.dma_rr
            self.dma_rr = (self.dma_rr + 1) % self.n_dma_sems
            if self.dma_last[k] is not None:
                o.deps.append(self.dma_last[k])
            self.dma_cnt[k] += 1
            o.sem = ("d", k)
            o.val = 16 * self.dma_cnt[k]
            self.dma_last[k] = o
        if self.pending_barrier.get(eng):
            o.deps.extend(self.pending_barrier.pop(eng))
        self._track(o, reads, writes)
        self.ops[eng].append(o)
        return o

    def barrier(self):
        lasts = []
        for e in self.ENGS:
            if self.ops[e]:
                lasts.append(self.ops[e][-1])
        for d in self.dma_last:
            if d is not None:
                lasts.append(d)
        self.pending_barrier = {e: list(lasts) for e in self.ENGS}

    def emit(self, stack, final_waits=()):
        nc = self.nc
        esem = {e: stack.enter_context(nc.semaphore(f"s_{e}")) for e in ("pe", "act", "dve", "pool")}
        dsem = [stack.enter_context(nc.semaphore(f"d_{k}")) for k in range(self.n_dma_sems + self.n_sw)]
        for e in self.ENGS:
            for o in self.ops[e]:
                for d in o.deps:
                    if d.kind == "c":
                        if d.eng == "pe" and o.eng == "pe" and o.kind == "c":
                            continue
                        d.needed = True
        for o in final_waits:
            if o.kind == "c":
                o.needed = True
        for e in ("pe", "act", "dve", "pool"):
            c = 0
            for o in self.ops[e]:
                if o.kind == "c" and o.needed:
                    c += 1
                    o.sem = ("e", e)
                    o.val = c

        def semh(s):
            return esem[s[1]] if s[0] == "e" else dsem[s[1]]

        allops = sorted((o for e in self.ENGS for o in self.ops[e]), key=lambda o: o.seq)
        known = {e: {} for e in self.ENGS}
        for o in allops:
            kn = known[o.eng]
            real = {}
            for d in o.deps:
                if d.kind == "c" and d.eng == "pe" and o.eng == "pe" and o.kind == "c":
                    continue
                if d.sem is None:
                    continue
                if kn.get(d.sem, 0) < d.val:
                    real[d.sem] = max(real.get(d.sem, 0), d.val)
                    kn[d.sem] = d.val
                    for s_, v_ in d.W.items():
                        if kn.get(s_, 0) < v_:
                            kn[s_] = v_
            o.waits = list(real.items())
            o.W = dict(kn)

        engmap = {"pe": "tensor", "act": "scalar", "dve": "vector", "pool": "gpsimd", "sp": "sync"}
        stats = {}
        block = stack.enter_context(nc.Block())

        def make(e):
            def body(eng):
                nw = 0
                for o in self.ops[e]:
                    for s, v in o.waits:
                        eng.wait_ge(semh(s), v)
                        nw += 1
                    ins = o.fn(eng)
                    if o.kind == "d":
                        ins.then_inc(semh(o.sem), 16)
                    elif o.needed:
                        ins.then_inc(semh(o.sem), 1)
                if e == "sp":
                    need = {}
                    for d in final_waits:
                        if need.get(d.sem, 0) < d.val:
                            need[d.sem] = d.val
                    for s, v in need.items():
                        eng.wait_ge(semh(s), v)
                stats[e] = (len(self.ops[e]), nw)
            return body

        for e in self.ENGS:
            getattr(block, engmap[e])(make(e))
        return stats


class Arena:
    def __init__(self, big, nfl):
        self.big = big
        self.n = nfl
        self.off = 0
        self.peak = 0

    def _alloc(self, nfl):
        a = self.off
        self.off += nfl
        self.peak = max(self.peak, self.off)
        assert self.off <= self.n, f"arena overflow {self.off} > {self.n}"
        return a

    @staticmethod
    def _shape(ap, shape):
        if len(shape) == 1:
            return ap
        if len(shape) == 2:
            return ap.rearrange("p (a b) -> p a b", a=shape[0])
        if len(shape) == 3:
            return ap.rearrange("p (a b c) -> p a b c", a=shape[0], b=shape[1])
        raise ValueError

    def f32(self, *shape):
        n = int(np.prod(shape))
        a = self._alloc(n)
        return self._shape(self.big[:, a:a + n], shape)

    def bf(self, *shape):
        n = int(np.prod(shape))
        nfl = (n + 1) // 2
        a = self._alloc(nfl)
        return self._shape(self.big[:, a:a + nfl].bitcast(BF16)[:, 0:n], shape)

    def mark(self):
        return self.off

    def release(self, m):
        self.off = m


def build_program(stage=2, stop=99):
    nc = bass.Bass("TRN2", target_bir_lowering=False)

    def din(name, shape):
        return nc.dram_tensor(name, list(shape), F32, kind="ExternalInput").ap()

    xs_d = din("xs", [128, 8, T])
    cc_d = din("cc", [128, 8, 2])
    ropec_d = din("ropec", [128, S])
    ropes_d = din("ropes", [128, S])
    perm_d = din("perm", [128, 128])
    ident_d = din("ident", [128, 128])
    mask_d = din("mask", [128, 256])
    rmask_d = din("rmask", [128, 512])
    sink_d = din("sink", [128, 16])
    vec0_d = din("vec0", [128, 32])
    vec1_d = din("vec1", [128, 56])
    wada0_d = din("wada0", [128, 8, 3072])
    win0_d = din("win0", [128, 8, 2304])
    wout0_d = din("wout0", [128, 8, 1024])
    wada1_d = din("wada1", [128, 8, 3072])
    win1_d = din("win1", [128, 8, 3072])
    wout1_d = din("wout1", [128, 8, 1024])
    wa1_d = din("wa1", [128, 8, 32])
    wa2_d = din("wa2", [32, 2, 512])
    y_d = nc.dram_tensor("y", [128, 8, S], F32, kind="ExternalOutput").ap()

    em = Em(nc)
    NFL = 53000
    with contextlib.ExitStack() as st:
        big = st.enter_context(nc.sbuf_tensor("big", [128, NFL], F32))
        ar = Arena(big, NFL)
        NPS = 6
        PS = [st.enter_context(nc.psum_tensor(f"ps{i}", [128, 512], F32)) for i in range(NPS)]
        PSD = [st.enter_context(nc.psum_tensor(f"psd{i}", [128, 512], F32)) for i in range(2)]
        PSDb = em.bufs("psd", 2)
        PSB = None
        PSb = em.bufs("ps", NPS)
        PSBb = [em.buf("psbA"), em.buf("psbB")]
        ps_rr = [0]
        psb_rr = [0]

        ring_n = [NPS + 2]
        PSall = PS + PSD
        PSallb = PSb + PSDb

        def ps_next():
            i = ps_rr[0] % ring_n[0]
            ps_rr[0] = (i + 1) % ring_n[0]
            return PSall[i], PSallb[i]

        def psb_next():
            i = psb_rr[0]
            psb_rr[0] = (i + 1) % 2
            return PSB[:, i * 512:(i + 1) * 512], PSBb[i]

        XS = ar.f32(8, T)
        XSb = [[em.buf(f"xs{j}_{g}") for g in range(5)] for j in range(8)]
        RSTD = ar.f32(T)
        RSTDb = em.bufs("rstd", 5)
        cc = ar.f32(8, 2)
        scb = ar.bf(8, 2)
        MOD = ar.f32(24, 2)
        GM = ar.f32(8, 2)
        MOD0, GM0 = MOD, GM
        MOD1 = ar.f32(24, 2)
        GM1 = ar.f32(8, 2)
        modstage = ar.f32(48)
        vec0 = ar.f32(32)
        vec1 = ar.f32(56)
        ones_bf = ar.bf(128)
        ident_bf = ar.bf(128)
        mask = ar.f32(256)
        sinkexp = ar.f32(16)
        Bc = {k: em.buf(k) for k in ["cc", "scb", "MOD", "GM", "vec0", "vec1", "ones", "ident", "mask",
                                     "sinkexp", "perm", "rmask", "nba", "MOD1", "GM1", "modstage"]}
        mle = mask[:, 0:128]
        mge = mask[:, 128:256]

        for g, (t0, w) in enumerate(GROUPS):
            for j in range(8):
                em.dma("sp", XS[:, j, t0:t0 + w], xs_d[:, j, t0:t0 + w], writes=[XSb[j][g]])
        em.dma("sp", cc, cc_d, writes=[Bc["cc"]])
        em.dma("sp", vec0, vec0_d, writes=[Bc["vec0"]])
        em.dma("sp", vec1, vec1_d, writes=[Bc["vec1"]])
        em.dma("sp", mask, mask_d, writes=[Bc["mask"]])
        em.dma("sp", sinkexp, sink_d, writes=[Bc["sinkexp"]])
        em.dma("pool", ident_bf, ident_d, writes=[Bc["ident"]])
        em.op("pool", lambda e: e.memset(ones_bf, 1.0), writes=[Bc["ones"]])
        em.op("act", lambda e: e.activation(out=sinkexp, in_=sinkexp, func=AF.Exp),
              reads=[Bc["sinkexp"]], writes=[Bc["sinkexp"]])
        em.op("act", lambda e: e.activation(out=scb, in_=cc, func=AF.Silu), reads=[Bc["cc"]], writes=[Bc["scb"]])

        class Modulation:
            NP = 24

            def __init__(self, wada_d, vec, vecb, ring32, ringbf, MOD, GM, MODb, GMb, alias_bufs, tag):
                self.a = (wada_d, vec, vecb, ring32, ringbf, MOD, GM, MODb, GMb)
                self.alias = list(alias_bufs)
                self.b32 = em.bufs("mr32" + tag, 2)
                self.bbf = em.bufs("mrbf" + tag, 2)
                self.first = [True, True]

            def dma(self, pc):
                wada_d, vec, vecb, ring32, ringbf, MOD, GM, MODb, GMb = self.a
                slot = pc % 2
                extra = self.alias if self.first[slot] else []
                self.first[slot] = False
                em.dma("sp", ring32[slot], wada_d[:, :, pc * 128:(pc + 1) * 128], writes=[self.b32[slot]] + extra)

            def mm(self, pc):
                wada_d, vec, vecb, ring32, ringbf, MOD, GM, MODb, GMb = self.a
                slot = pc % 2
                em.op("dve", lambda e: e.tensor_copy(out=ringbf[slot], in_=ring32[slot]),
                      reads=[self.b32[slot]], writes=[self.bbf[slot]])
                pm, pmb = ps_next()
                for kc in range(8):
                    em.op("pe", lambda e, kc=kc: e.matmul(pm[:, 0:2], lhsT=ringbf[slot][:, kc, :], rhs=scb[:, kc, :],
                                                          start=(kc == 0), stop=(kc == 7)),
                          reads=[self.bbf[slot], Bc["scb"]], writes=[pmb])
                em.op("act", lambda e: e.activation(out=modstage[:, pc * 2:(pc + 1) * 2], in_=pm[:, 0:2], func=AF.Copy),
                      reads=[pmb], writes=[Bc["modstage"]])

            def finish(self):
                wada_d, vec, vecb, ring32, ringbf, MOD, GM, MODb, GMb = self.a
                pmv = modstage.rearrange("p (a b) -> p a b", a=24)
                for s in range(2):
                    em.op("dve", lambda e, s=s: e.tensor_tensor(out=MOD[:, :, s], in0=pmv[:, :, s], in1=vec[:, 8:32],
                                                                op=ALU.add),
                          reads=[Bc["modstage"], vecb], writes=[MODb])
                for s in range(2):
                    em.op("dve", lambda e, s=s: e.scalar_tensor_tensor(out=GM[:, :, s], in0=MOD[:, 8:16, s],
                                                                       scalar=1.0, in1=vec[:, 0:8], op0=ALU.add,
                                                                       op1=ALU.mult),
                          reads=[MODb, vecb] + self.b32 + self.bbf, writes=[GMb] + self.alias)

            def run_all(self):
                self.dma(0)
                self.dma(1)
                for pc in range(self.NP):
                    self.mm(pc)
                    if pc + 2 < self.NP:
                        self.dma(pc + 2)
                self.finish()

        def norm_stats(g, sqr, sqrb, tmp, tmpb, src=None, srcb=None, nchunk=8, inv_n=1.0 / D, out=None, outb=None):
            t0, w = GROUPS[g]
            pss, pssb = ps_next()
            for j in range(nchunk):
                sl = j % 2
                if src is None:
                    sap, sb = XS[:, j, t0:t0 + w], XSb[j][g]
                else:
                    sap, sb = src[:, j, 0:w], srcb
                em.op("act", lambda e, sl=sl, sap=sap: e.activation(out=sqr[sl][:, 0:w], in_=sap, func=AF.Square),
                      reads=[sb], writes=[sqrb[sl]])
                em.op("pe", lambda e, sl=sl, j=j: e.matmul(pss[:, 0:w], lhsT=ones_bf, rhs=sqr[sl][:, 0:w],
                                                          start=(j == 0), stop=(j == nchunk - 1)),
                      reads=[sqrb[sl], Bc["ones"]], writes=[pssb])
            em.op("act", lambda e: e.activation(out=tmp[:, 0:w], in_=pss[:, 0:w], func=AF.Ln, bias=EPS_AP,
                                                scale=inv_n),
                  reads=[pssb, Bc["eps"]], writes=[tmpb])
            if out is None:
                oap, ob = RSTD[:, t0:t0 + w], RSTDb[g]
            else:
                oap, ob = out[:, 0:w], outb
            em.op("act", lambda e: e.activation(out=oap, in_=tmp[:, 0:w], func=AF.Exp, scale=-0.5),
                  reads=[tmpb], writes=[ob])

        def make_h(g, Hdst, Hb, tmpr, tmprb):
            t0, w = GROUPS[g]
            s = 1 if g == 4 else 0
            for j in range(8):
                sl = j % 2
                em.op("dve", lambda e, j=j, sl=sl: e.tensor_tensor(out=tmpr[sl][:, 0:w], in0=XS[:, j, t0:t0 + w],
                                                                   in1=RSTD[:, t0:t0 + w], op=ALU.mult),
                      reads=[XSb[j][g], RSTDb[g]], writes=[tmprb[sl]])
                em.op("act", lambda e, j=j, sl=sl: e.activation(out=Hdst[:, j, 0:w], in_=tmpr[sl][:, 0:w],
                                                                func=AF.Identity, bias=MOD[:, j, s:s + 1],
                                                                scale=GM[:, j, s:s + 1]),
                      reads=[tmprb[sl], Bc["MOD"], Bc["GM"]], writes=(Hb if isinstance(Hb, list) else [Hb]))

        epsT = ar.f32(2)
        Bc["eps"] = em.buf("eps")
        EPS_AP = epsT[:, 0:1]
        NEGHALF = epsT[:, 1:2]
        em.op("pool", lambda e: e.memset(epsT[:, 0:1], EPS), writes=[Bc["eps"]])
        em.op("pool", lambda e: e.memset(epsT[:, 1:2], -0.5), writes=[Bc["eps"]])

        layer_mark = ar.mark()

        W0 = ar.bf(8, 2304)
        W0b = em.buf("W0")
        WO0 = ar.bf(8, 1024)
        WO0b = em.buf("WO0")
        KT0 = ar.bf(T)
        KT1 = ar.bf(T)
        KTb = em.bufs("kt", NT)
        VA = ar.bf(NT, 2, 66)
        VAb = em.bufs("va", NT)
        Hg = ar.bf(8, 512)
        Hgb = em.buf("Hg")
        alias_mark = ar.mark()
        QTg = ar.bf(8, 512)
        QTgb = em.bufs("qtg", 8)
        Gg = ar.bf(4, 1024)
        Ggb = em.bufs("gg", 4)
        alias_end = ar.mark()
        ar.release(alias_mark)
        mring = [ar.f32(8, 128) for _ in range(2)]
        mringbf = [ar.bf(8, 128) for _ in range(2)]
        assert ar.mark() <= alias_end
        ar.release(alias_end)
        rope_mark = ar.mark()
        ropeC = [ar.f32(512) for _ in range(2)]
        ropeS = [ar.f32(512) for _ in range(2)]
        ropeb = em.bufs("rope", 2)
        perm = ar.f32(128)
        qtmp = [ar.f32(512), ar.f32(512)]
        qtmpb = em.bufs("qtmp", 2)
        _t0 = ar.f32(512)
        t1r = [_t0, _t0]
        _tb = em.buf("t1r")
        t1rb = [_tb, _tb]
        t2r0 = ar.f32(512)
        t2r = [t2r0, t2r0]
        _save = ar.mark()
        ar.release(rope_mark)
        mring1 = [ar.f32(8, 128) for _ in range(2)]
        ar._alloc(128)
        mring1bf = [ar.bf(8, 128) for _ in range(2)]
        assert ar.mark() <= _save
        ar.release(_save)
        t2rb0 = em.buf("t2r")
        t2rb = [t2rb0, t2rb0]
        tmpr = [ar.f32(512) for _ in range(2)]
        tmprb = em.bufs("tmpr", 2)
        sqr = [ar.bf(512) for _ in range(2)]
        sqrb = em.bufs("sqr", 2)
        rden = [ar.f32(8) for _ in range(2)]
        rdenb = em.bufs("rden", 2)
        PTset = []
        PTb = []
        for sset in range(2):
            PTset.append([ar.bf(512) for _ in range(5)])
            PTb.append(em.bufs(f"pt{sset}_", 5))

        em.dma("sp", perm, perm_d, writes=[Bc["perm"]])
        for g in range(5):
            norm_stats(g, sqr, sqrb, tmpr[0], tmprb[0])
        Modulation(wada0_d, vec0, Bc["vec0"], mring, mringbf, MOD, GM, Bc["MOD"], Bc["GM"],
                   QTgb + Ggb[0:2], "0").run_all()
        for kc in range(8):
            for hf in range(2):
                em.dma("pool", W0[:, kc, hf * 1152:(hf + 1) * 1152], win0_d[:, kc, hf * 1152:(hf + 1) * 1152],
                       writes=[W0b])
        for kc in range(8):
            em.dma("pool", WO0[:, kc, :], wout0_d[:, kc, :], writes=[WO0b])
        em.op("pool", lambda e: e.memset(VA[:, :, :, 64:65], 1.0), writes=VAb)
        em.op("pool", lambda e: e.memset(KT0[64:128, :], 0.0), writes=KTb)
        em.op("pool", lambda e: e.memset(KT1[0:64, :], 0.0), writes=KTb)

        def load_rope(g, sl):
            t0, w = GROUPS[g]
            em.dma("sp", ropeC[sl][:, 0:w], ropec_d[:, t0:t0 + w], writes=[ropeb[sl]])
            em.dma("sp", ropeS[sl][:, 0:w], ropes_d[:, t0:t0 + w], writes=[ropeb[sl]])

        rope_ctr = [0]

        def proj_fm(dst, dstb, wcols, g, rope_sl, Hs=None, Hsb=None, split=None):
            t0, w = GROUPS[g]
            Hs = Hg if Hs is None else Hs
            Hsb = [Hgb] if Hsb is None else Hsb
            pq, pqb = ps_next()
            for kc in range(8):
                em.op("pe", lambda e, kc=kc: e.matmul(pq[:, 0:w], lhsT=W0[:, kc, wcols:wcols + 128],
                                                      rhs=Hs[:, kc, 0:w], start=(kc == 0), stop=(kc == 7)),
                      reads=[W0b] + Hsb, writes=[pqb])
            if g == 4:
                if split is None:
                    em.op("act", lambda e: e.activation(out=dst, in_=pq[:, 0:w], func=AF.Copy), reads=[pqb],
                          writes=dstb)
                else:
                    for (lo, hi, dd) in ((0, 64, split[0]), (64, 128, split[1])):
                        em.op("act", lambda e, lo=lo, hi=hi, dd=dd: e.activation(
                            out=dd[lo:hi, t0:t0 + w], in_=pq[lo:hi, 0:w], func=AF.Copy), reads=[pqb], writes=dstb)
                return
            i = rope_ctr[0] % 2
            rope_ctr[0] += 1
            em.op("act", lambda e: e.activation(out=qtmp[i][:, 0:w], in_=pq[:, 0:w], func=AF.Copy),
                  reads=[pqb], writes=[qtmpb[i]])
            yield
            pr, prb = ps_next()
            em.op("pe", lambda e: e.matmul(pr[:, 0:w], lhsT=perm, rhs=qtmp[i][:, 0:w], start=True, stop=True),
                  reads=[qtmpb[i], Bc["perm"]], writes=[prb])
            em.op("pool", lambda e: e.tensor_tensor(out=t1r[i][:, 0:w], in0=qtmp[i][:, 0:w],
                                                    in1=ropeC[rope_sl][:, 0:w], op=ALU.mult),
                  reads=[qtmpb[i], ropeb[rope_sl]], writes=[t1rb[i]])
            em.op("dve", lambda e: e.tensor_tensor(out=t2r[i][:, 0:w], in0=pr[:, 0:w], in1=ropeS[rope_sl][:, 0:w],
                                                   op=ALU.mult),
                  reads=[prb, ropeb[rope_sl]], writes=[t2rb[i]])
            if split is None:
                em.op("pool", lambda e: e.tensor_tensor(out=dst, in0=t1r[i][:, 0:w], in1=t2r[i][:, 0:w], op=ALU.add),
                      reads=[t1rb[i], t2rb[i]], writes=dstb)
            else:
                for (lo, hi, dd) in ((0, 64, split[0]), (64, 128, split[1])):
                    em.op("pool", lambda e, lo=lo, hi=hi, dd=dd: e.tensor_tensor(
                        out=dd[lo:hi, t0:t0 + w], in0=t1r[i][lo:hi, 0:w], in1=t2r[i][lo:hi, 0:w], op=ALU.add),
                        reads=[t1rb[i], t2rb[i]], writes=dstb)

        def vproj(g, tt, Hs, Hsb):
            t0, w = GROUPS[g]
            ti = t0 // 128 + tt
            pv, pvb = ps_next()
            for kc in range(8):
                em.op("pe", lambda e, kc=kc: e.matmul(pv[:, 0:128], lhsT=Hs[:, kc, tt * 128:(tt + 1) * 128],
                                                      rhs=W0[:, kc, 1152:1280], start=(kc == 0), stop=(kc == 7)),
                      reads=[W0b] + Hsb, writes=[pvb])
            em.op("dve", lambda e: e.tensor_copy(
                out=VA[:, ti, :, 0:64], in_=pv[:, 0:128].rearrange("p (a b) -> p a b", a=2)),
                reads=[pvb], writes=[VAb[ti]])

        def pass_a(g):
            t0, w = GROUPS[g]
            if g % 2 == 0:
                Hs, Hsb = Hg, [Hgb]
            else:
                Hs, Hsb = QTg, QTgb
            make_h(g, Hs, Hsb, tmpr, tmprb)
            if g < 4:
                load_rope(g, g % 2)
            kgen = proj_fm(None, KTb[t0 // 128:(t0 + w) // 128], 1024, g, g % 2, Hs, Hsb, split=(KT0, KT1))
            next(kgen, None)
            for tt in range(w // 128):
                vproj(g, tt, Hs, Hsb)
            for _ in kgen:
                pass

        for g in range(5 if stop >= 2 else 0):
            pass_a(g)

        head_ctr = [0]
        mask_ctr = [0]

        def gproj(g, tt, half):
            pg, pgb = ps_next()
            for kc in range(8):
                em.op("pe", lambda e, kc=kc: e.matmul(
                    pg[:, :], lhsT=Hg[:, kc, tt * 128:(tt + 1) * 128],
                    rhs=W0[:, kc, 1280 + half * 512:1280 + (half + 1) * 512], start=(kc == 0), stop=(kc == 7)),
                    reads=[W0b, Hgb], writes=[pgb])
            em.op("act", lambda e: e.activation(out=Gg[:, tt, half * 512:(half + 1) * 512], in_=pg[:, :],
                                                func=AF.Silu),
                  reads=[pgb], writes=[Ggb[tt]])

        def attn_head(g, j, hp):
            t0, w = GROUPS[g]
            nb = w // 128
            n0 = t0 // 128
            h = j + 8 * hp
            r0 = 64 * hp
            sset = head_ctr[0] % 2
            head_ctr[0] += 1
            PTs, PTbs = PTset[sset], PTb[sset]
            raw = [(16, 0, w), (17, 0, w)]
            if g < 4:
                for m in range(max(0, n0 - 1), min(15, n0 + nb) + 1):
                    na = max(m - 1, n0)
                    nb_ = min(m + 1, n0 + nb - 1)
                    raw.append((m, (na - n0) * 128, (nb_ - n0 + 1) * 128))
            bins = []
            for (m, qa, qb) in sorted(raw, key=lambda t: -(t[2] - t[1])):
                wd = qb - qa
                for bn in bins:
                    if bn["used"] + wd <= 512:
                        bn["items"].append((m, qa, qb, bn["used"]))
                        bn["used"] += wd
                        break
                else:
                    bins.append({"used": wd, "items": [(m, qa, qb, 0)]})
            assert len(bins) <= 5
            KTp = KT0 if hp == 0 else KT1
            tiles = []
            for sl, bn in enumerate(bins):
                pss, pssb = ps_next()
                for (m, qa, qb, off) in bn["items"]:
                    em.op("pe", lambda e, m=m, qa=qa, qb=qb, off=off, pss=pss: e.matmul(
                        pss[:, off:off + qb - qa], lhsT=KTp[:, m * 128:(m + 1) * 128],
                        rhs=QTg[:, j, qa:qb], start=True, stop=True),
                        reads=[KTb[m], QTgb[j]], writes=[pssb])
                    tiles.append((sl, m, qa, qb, off))
                em.op("act", lambda e, sl=sl, used=bn["used"], pss=pss: e.activation(
                    out=PTs[sl][:, 0:used], in_=pss[:, 0:used], func=AF.Exp, scale=0.125),
                    reads=[pssb], writes=[PTbs[sl]])
                for (m, qa, qb, off) in bn["items"]:
                    if m >= 16:
                        continue
                    for n in range(n0 + qa // 128, n0 + qb // 128):
                        c0 = off + (n - n0) * 128 - qa
                        if n == m - 1:
                            mk = mle
                        elif n == m + 1:
                            mk = mge
                        else:
                            continue
                        mask_ctr[0] += 1
                        em.op("pool" if mask_ctr[0] % 2 else "dve", lambda e, sl=sl, c0=c0, mk=mk: e.tensor_tensor(
                            out=PTs[sl][:, c0:c0 + 128], in0=PTs[sl][:, c0:c0 + 128], in1=mk, op=ALU.mult),
                            reads=[PTbs[sl], Bc["mask"]], writes=[PTbs[sl]])
            return (g, j, hp, tiles, PTs, PTbs)

        def attn_pv(ctx):
            g, j, hp, tiles, PTs, PTbs = ctx
            t0, w = GROUPS[g]
            nb = w // 128
            h = j + 8 * hp
            po, pob = ps_next()
            pov = po[:, 0:4 * 65].rearrange("p (a b) -> p a b", a=4)
            for bi in range(nb):
                use = [(sl, m, qa, off) for (sl, m, qa, qb, off) in tiles if qa <= bi * 128 < qb]
                for ui, (sl, m, qa, off) in enumerate(use):
                    c0 = off + bi * 128 - qa
                    em.op("pe", lambda e, sl=sl, m=m, c0=c0, bi=bi, ui=ui, nu=len(use):
                          e.matmul(pov[:, bi, :], lhsT=PTs[sl][:, c0:c0 + 128], rhs=VA[:, m, hp, 0:65],
                                   start=(ui == 0), stop=(ui == nu - 1)),
                          reads=[PTbs[sl], VAb[m]], writes=[pob])
            ri = head_ctr[0] % 2
            em.op("dve", lambda e: e.tensor_scalar(
                out=rden[ri][:, 0:nb], in0=pov[:, 0:nb, 64], scalar1=sinkexp[:, h:h + 1], scalar2=None,
                op0=ALU.add),
                reads=[pob, Bc["sinkexp"]], writes=[rdenb[ri]])
            em.op("dve", lambda e: e.reciprocal(out=rden[ri][:, 4:4 + nb], in_=rden[ri][:, 0:nb]),
                  reads=[rdenb[ri]], writes=[rdenb[ri]])
            for bi in range(nb):
                em.op("dve", lambda e, bi=bi: e.scalar_tensor_tensor(
                    out=Gg[:, bi, h * 64:(h + 1) * 64], in0=pov[:, bi, 0:64], scalar=rden[ri][:, 4 + bi:5 + bi],
                    in1=Gg[:, bi, h * 64:(h + 1) * 64], op0=ALU.mult, op1=ALU.mult),
                    reads=[pob, rdenb[ri], Ggb[bi]], writes=[Ggb[bi]])

        def tr_chunk(g, k):
            t0, w = GROUPS[g]
            nb = w // 128
            pt, ptb = ps_next()
            for tt in range(nb):
                em.op("pe", lambda e, tt=tt: e.matmul(
                    pt[:, tt * 128:(tt + 1) * 128], lhsT=Gg[:, tt, k * 128:(k + 1) * 128], rhs=ident_bf,
                    start=True, stop=True),
                    reads=[Ggb[tt], Bc["ident"]], writes=[ptb])
            em.op("act", lambda e: e.activation(out=Hg[:, k, 0:w], in_=pt[:, 0:w], func=AF.Copy),
                  reads=[ptb], writes=[Hgb])

        def oproj(g, c):
            t0, w = GROUPS[g]
            s = 1 if g == 4 else 0
            pp, ppb = ps_next()
            for kc in range(8):
                em.op("pe", lambda e, kc=kc: e.matmul(
                    pp[:, 0:w], lhsT=WO0[:, kc, c * 128:(c + 1) * 128], rhs=Hg[:, kc, 0:w],
                    start=(kc == 0), stop=(kc == 7)),
                    reads=[WO0b, Hgb], writes=[ppb])
            em.op("dve", lambda e: e.scalar_tensor_tensor(
                out=XS[:, c, t0:t0 + w], in0=pp[:, 0:w], scalar=MOD[:, 16 + c, s:s + 1], in1=XS[:, c, t0:t0 + w],
                op0=ALU.mult, op1=ALU.add),
                reads=[ppb, Bc["MOD"], XSb[c][g]], writes=[XSb[c][g]])

        def pass_b(g):
            t0, w = GROUPS[g]
            nb = w // 128
            make_h(g, Hg, Hgb, tmpr, tmprb)
            if g < 4:
                load_rope(g, g % 2)
            pgens = [proj_fm(QTg[:, j, 0:w], [QTgb[j]], j * 128, g, g % 2) for j in range(8)]
            next(pgens[0], None)
            for j in range(8):
                if j + 1 < 8:
                    next(pgens[j + 1], None)
                for _ in pgens[j]:
                    pass
            mod1 = None
            if g == 3 and stage >= 1:
                mod1 = Modulation(wada1_d, vec1, Bc["vec1"], mring1, mring1bf, MOD1, GM1, Bc["MOD1"], Bc["GM1"],
                                  [ropeb[0], ropeb[1], qtmpb[0], qtmpb[1]], "1")
                mod1.dma(0)
                mod1.dma(1)
            for tt in range(nb):
                for half in range(2):
                    gproj(g, tt, half)
            hi = 0
            prev = None
            for j in range(8):
                for hp in range(2):
                    cur = attn_head(g, j, hp)
                    if prev is not None:
                        attn_pv(prev)
                    prev = cur
                    if mod1 is not None and hi < 12:
                        for pc in (2 * hi, 2 * hi + 1):
                            mod1.mm(pc)
                            if pc + 2 < 24:
                                mod1.dma(pc + 2)
                    hi += 1
            attn_pv(prev)
            if mod1 is not None:
                mod1.finish()
            for k in range(8):
                tr_chunk(g, k)
            for c in range(8):
                oproj(g, c)

        for g in range(5 if stop >= 3 else 0):
            pass_b(g)

        em.barrier()
        ar.release(layer_mark)

        if stage >= 1:
            build_layer1(nc, em, ar, locals())

        outs = []
        if stage == 0:
            for j in range(8):
                for g in range(4):
                    t0, w = GROUPS[g]
                    outs.append(em.dma("sp", y_d[:, j, t0:t0 + w], XS[:, j, t0:t0 + w], reads=[XSb[j][g]]))
        else:
            outs = L1_OUTS
        stats = em.emit(st, final_waits=outs)
        build_program.stats = (stats, ar.peak)
    return nc


L1_OUTS = []


def build_layer1(nc, em, ar, L):
    del L1_OUTS[:]
    XS, XSb, RSTD = L["XS"], L["XSb"], L["RSTD"]
    MOD, GM, Bc, vec1 = L["MOD1"], L["GM1"], L["Bc"], L["vec1"]
    ones_bf, ident_bf, mle, mge = L["ones_bf"], L["ident_bf"], L["mle"], L["mge"]
    ps_next = L["ps_next"]
    PSD, PSDb = L["PSD"], L["PSDb"]
    L["ring_n"][0] = 8
    EPS_AP = L["EPS_AP"]
    NEGHALF = L["NEGHALF"]
    win1_d, wout1_d, wa1_d, wa2_d, rmask_d, y_d = (L["win1_d"], L["wout1_d"], L["wa1_d"], L["wa2_d"],
                                                  L["rmask_d"], L["y_d"])
    KSCALE = 128.0 ** -0.5
    C16 = 1.0 / 16.0

    RS = RSTD[:, 0:512]
    RSb = em.buf("RS")
    RS2 = [RSTD[:, 0:512], RSTD[:, 1664:2176]]
    RS2b = [RSb, em.buf("RSx")]
    RTf = RSTD[:, 512:1664].bitcast(BF16)
    H = ar.bf(8, T)
    Hb = em.bufs("H", 5)
    Wqkv = ar.bf(8, 512)
    Wqkvb = em.buf("Wqkv")
    Wg = ar.bf(8, 256)
    Wgb = em.buf("Wg")
    WOh = ar.bf(2, 1024)
    WOhb = em.buf("WOh")
    WA1 = ar.bf(8, 32)
    NBA = ar.f32(8)
    rmask = ar.f32(512)
    Bw = {k: em.buf(k) for k in ["WA1", "WA2", "RT", "NBA", "rmask"]}
    XC = [XS[:, j, S:T] for j in range(8)]
    R = [[XC[0], XC[1]], [XC[2], XC[3]]]
    Sfb = [XC[4][:, 0:128].bitcast(BF16), XC[4][:, 128:256].bitcast(BF16)]
    AT = [[XC[5][:, 0:64].bitcast(BF16), XC[5][:, 64:128].bitcast(BF16)],
          [XC[5][:, 128:192].bitcast(BF16), XC[5][:, 192:256].bitcast(BF16)]]
    ATpair = [XC[5][:, 0:128].bitcast(BF16), XC[5][:, 128:256].bitcast(BF16)]
    mask2 = L["mask"]
    WA2 = [XC[6].bitcast(BF16), XC[7].bitcast(BF16)]
    head_mark = ar.mark()
    QD = [ar.bf(S), ar.bf(S)]
    KI = [ar.bf(T), ar.bf(T)]
    KItm = [ar.bf(NT, 128), ar.bf(NT, 128)]
    DEC = ar.f32(2, NT)
    V = ar.bf(NT, 256)
    SBST = ar.bf(16, 256)
    tA = [ar.f32(512), ar.f32(512)]
    tB = [ar.f32(512), ar.f32(512)]
    TOT = [ar.f32(4), ar.f32(4)]
    Og = [ar.f32(2, 512), ar.f32(2, 512)]
    RO = ar.f32(512)
    sq2 = [ar.bf(512), ar.bf(512)]
    _tx = ar.f32(512)
    tX = [_tx, _tx]
    OGT = ar.bf(2, 512)
    tmpr = [Og[0][:, 0, :], Og[1][:, 0, :]]
    sqr = sq2
    tmps = tX[0]
    names = ["R00", "R01", "R10", "R11", "Sfb0", "Sfb1",
             "AT00", "AT01", "AT10", "AT11", "tA0", "tA1", "tB0", "tB1", "TOT0", "TOT1",
             "Og0", "Og1", "RO", "sq0", "sq1", "tX0", "OGT"]
    Bh = {k: em.buf(k) for k in names}
    Bh["tm0"], Bh["tm1"], Bh["sr0"], Bh["sr1"], Bh["tms"] = Bh["Og0"], Bh["Og1"], Bh["sq0"], Bh["sq1"], Bh["tX0"]
    RTb = em.bufs("rt", 5)
    DECb = [[em.buf("dec%d_%d" % (d, g)) for g in range(5)] for d in range(2)]
    Vb = em.bufs("V", 5)
    KTb = [em.bufs("ktm0_", 5), em.bufs("ktm1_", 5)]
    KIb = [em.bufs("ki0_", 5), em.bufs("ki1_", 5)]
    QDb = [em.bufs("qd0_", 4), em.bufs("qd1_", 4)]
    SBb = em.bufs("sbst", 4)
    STOP = L["stop"]

    em.dma("pool", WA1, wa1_d, writes=[Bw["WA1"]])
    em.dma("sp", rmask, rmask_d, writes=[Bw["rmask"]])
    em.op("dve", lambda e: e.tensor_scalar(out=NBA, in0=vec1[:, 48:56], scalar1=-1.0, scalar2=None, op0=ALU.mult),
          reads=[Bc["vec1"]], writes=[Bw["NBA"]])

    def load_qkv(h):
        for (c_lo, n, dst_lo) in ((h * 128, 128, 0), (512 + h * 128, 128, 128), (1024 + h * 256, 256, 256)):
            em.dma("pool", Wqkv[:, :, dst_lo:dst_lo + n], win1_d[:, :, c_lo:c_lo + n], writes=[Wqkvb])

    def load_g(h):
        em.dma("pool", Wg, win1_d[:, :, 2048 + h * 256:2048 + (h + 1) * 256], writes=[Wgb])
        em.dma("pool", WOh, wout1_d[:, 2 * h:2 * h + 2, :], writes=[WOhb])

    load_qkv(0)
    load_g(0)

    def stats(src_fn, nchunk, inv_n, out_ap, outb, w, sq, sqb, tmp, tmpb):
        pss, pssb = ps_next()
        for j in range(nchunk):
            sl = j % 2
            sap, sb = src_fn(j)
            em.op("act", lambda e, sl=sl, sap=sap: e.activation(out=sq[sl][:, 0:w], in_=sap, func=AF.Square),
                  reads=[sb], writes=[sqb[sl]])
            em.op("pe", lambda e, sl=sl, j=j: e.matmul(pss[:, 0:w], lhsT=ones_bf, rhs=sq[sl][:, 0:w],
                                                      start=(j == 0), stop=(j == nchunk - 1)),
                  reads=[sqb[sl], Bc["ones"]], writes=[pssb])
        em.op("act", lambda e: e.activation(out=tmp[:, 0:w], in_=pss[:, 0:w], func=AF.Ln, bias=EPS_AP, scale=inv_n),
              reads=[pssb, Bc["eps"]], writes=[tmpb])
        em.op("act", lambda e: e.activation(out=out_ap, in_=tmp[:, 0:w], func=AF.Exp, scale=-0.5),
              reads=[tmpb], writes=[outb])

    def setup_group(g, ri):
        t0, w = GROUPS[g]
        s = 1 if g == 4 else 0
        RSg, RSgb = RS2[ri], RS2b[ri]
        for j in range(8):
            sl = j % 2
            em.op("dve", lambda e, j=j, sl=sl: e.tensor_tensor(out=tmpr[sl][:, 0:w], in0=XS[:, j, t0:t0 + w],
                                                               in1=RSg[:, 0:w], op=ALU.mult),
                  reads=[XSb[j][g], RSgb], writes=[Bh["tm%d" % sl]])
            em.op("act", lambda e, j=j, sl=sl: e.activation(out=H[:, j, t0:t0 + w], in_=tmpr[sl][:, 0:w],
                                                            func=AF.Identity, bias=MOD[:, j, s:s + 1],
                                                            scale=GM[:, j, s:s + 1]),
                  reads=[Bh["tm%d" % sl], Bc["MOD1"], Bc["GM1"]], writes=[Hb[g]])
        pr, prb = ps_next()
        for kc in range(8):
            em.op("pe", lambda e, kc=kc: e.matmul(pr[0:32, 0:w], lhsT=WA1[:, kc, 0:32], rhs=H[:, kc, t0:t0 + w],
                                                  start=(kc == 0), stop=(kc == 7)),
                  reads=[Bw["WA1"], Hb[g]], writes=[prb])
        em.op("act", lambda e: e.activation(out=RTf[0:32, t0:t0 + w], in_=pr[0:32, 0:w], func=AF.Copy),
              reads=[prb], writes=[RTb[g]])

    def setup_stats(g, ri):
        t0, w = GROUPS[g]
        stats(lambda j: (XS[:, j, t0:t0 + w], XSb[j][g]), 8, 1.0 / D, RS2[ri][:, 0:w], RS2b[ri], w,
              sqr, [Bh["sr0"], Bh["sr1"]], tmps, Bh["tms"])

    def prep_dir(h, g, d, P):
        t0, w = GROUPS[g]
        nch = w // 128
        c0 = t0 // 128
        lat = g < 4
        a_, b_, tot = tA[d], tB[d], TOT[d]
        ab, bb, totb = Bh["tA%d" % d], Bh["tB%d" % d], Bh["TOT%d" % d]
        pz, pzb = ps_next()
        em.op("pe", lambda e: e.matmul(pz[:, 0:w], lhsT=WA2[d][0:32, h * 128:(h + 1) * 128],
                                       rhs=RTf[0:32, t0:t0 + w], start=True, stop=True),
              reads=[Bw["WA2"], RTb[g]], writes=[pzb])
        em.op("act", lambda e: e.activation(out=a_[:, 0:w], in_=pz[:, 0:w], func=AF.Exp, scale=-1.0,
                                            bias=NBA[:, d * 4 + h:d * 4 + h + 1]),
              reads=[pzb, Bw["NBA"]], writes=[ab])
        yield
        em.op("act", lambda e: e.activation(out=a_[:, 0:w], in_=a_[:, 0:w], func=AF.Ln, bias=1.0),
              reads=[ab], writes=[ab])
        yield
        em.op("dve", lambda e: e.tensor_tensor_scan(out=b_[:, 0:w], data0=rmask[:, 0:w], data1=a_[:, 0:w],
                                                    initial=0.0, op0=ALU.mult, op1=ALU.add),
              reads=[ab, Bw["rmask"]], writes=[bb])
        yield
        em.op("act", lambda e: e.activation(out=DEC[:, d, c0:c0 + nch], in_=b_[:, 127:w:128], func=AF.Exp, scale=-C16),
              reads=[bb], writes=[DECb[d][g]])
        yield
        if d == 0:
            sc1, sc2 = -C16, C16
        else:
            em.op("dve", lambda e: e.tensor_copy(out=tot[:, 0:nch], in_=b_[:, 127:w:128]), reads=[bb], writes=[totb])
            b3 = b_[:, 0:w].rearrange("p (c t) -> p c t", t=128)
            em.op("dve", lambda e: e.tensor_tensor(out=b3, in0=b3,
                                                   in1=tot[:, 0:nch].unsqueeze(2).to_broadcast([128, nch, 128]),
                                                   op=ALU.subtract),
                  reads=[bb, totb], writes=[bb])
            em.op("dve", lambda e: e.tensor_tensor(out=b_[:, 0:w], in0=b_[:, 0:w], in1=a_[:, 0:w], op=ALU.subtract),
                  reads=[bb, ab], writes=[bb])
            sc1, sc2 = C16, -C16
        yield
        if lat:
            pq, pqb = P["pq"]
            em.op("act", lambda e: e.activation(out=a_[:, 0:w], in_=b_[:, 0:w], func=AF.Exp, scale=sc1),
                  reads=[bb], writes=[ab])
            em.op("dve", lambda e: e.scalar_tensor_tensor(
                out=QD[d][:, t0:t0 + w], in0=pq[:, 0:w], scalar=KSCALE, in1=a_[:, 0:w], op0=ALU.mult, op1=ALU.mult),
                reads=[pqb, ab], writes=[QDb[d][g]])
        yield
        pk, pkb = P["pk"]
        em.op("act", lambda e: e.activation(out=b_[:, 0:w], in_=b_[:, 0:w], func=AF.Exp, scale=sc2),
              reads=[bb], writes=[bb])
        em.op("dve", lambda e: e.tensor_tensor(out=KI[d][:, t0:t0 + w], in0=pk[:, 0:w], in1=b_[:, 0:w], op=ALU.mult),
              reads=[pkb, bb], writes=[KIb[d][g]])
        yield
        pt, ptb = ps_next()
        for c in range(nch):
            em.op("pe", lambda e, c=c: e.matmul(pt[:, c * 128:(c + 1) * 128],
                                                lhsT=KI[d][:, t0 + c * 128:t0 + (c + 1) * 128], rhs=ident_bf,
                                                start=True, stop=True),
                  reads=[KIb[d][g], Bc["ident"]], writes=[ptb])
        em.op("act", lambda e: e.activation(
            out=KItm[d][:, c0:c0 + nch, :], in_=pt[:, 0:w].rearrange("p (c t) -> p c t", t=128), func=AF.Copy),
            reads=[ptb], writes=[KTb[d][g]])

    def prep_group(h, g):
        t0, w = GROUPS[g]
        nch = w // 128
        lat = g < 4
        P = {}
        gens = [prep_dir(h, g, d, P) for d in range(2)]
        alive = [True, True]
        rnd = 0
        while any(alive):
            rnd += 1
            if rnd == 6 and lat:
                pq, pqb = ps_next()
                for kc in range(8):
                    em.op("pe", lambda e, kc=kc, pq=pq: e.matmul(pq[:, 0:w], lhsT=Wqkv[:, kc, 0:128],
                                                                 rhs=H[:, kc, t0:t0 + w], start=(kc == 0),
                                                                 stop=(kc == 7)),
                          reads=[Wqkvb, Hb[g]], writes=[pqb])
                P["pq"] = (pq, pqb)
            if rnd == 7:
                pk, pkb = ps_next()
                for kc in range(8):
                    em.op("pe", lambda e, kc=kc, pk=pk: e.matmul(pk[:, 0:w], lhsT=Wqkv[:, kc, 128:256],
                                                                 rhs=H[:, kc, t0:t0 + w], start=(kc == 0),
                                                                 stop=(kc == 7)),
                          reads=[Wqkvb, Hb[g]], writes=[pkb])
                P["pk"] = (pk, pkb)
            for d in range(2):
                if alive[d]:
                    try:
                        next(gens[d])
                    except StopIteration:
                        alive[d] = False
            yield
        for tt in range(nch):
            vproj(h, g, tt)
            yield

    def vproj(h, g, tt):
        t0, w = GROUPS[g]
        ti = t0 // 128 + tt
        pv, pvb = ps_next()
        for kc in range(8):
            em.op("pe", lambda e, kc=kc: e.matmul(pv[:, 0:256], lhsT=H[:, kc, ti * 128:(ti + 1) * 128],
                                                  rhs=Wqkv[:, kc, 256:512], start=(kc == 0), stop=(kc == 7)),
                  reads=[Wqkvb, Hb[g]], writes=[pvb])
        em.op("act", lambda e: e.activation(out=V[:, ti, :], in_=pv[:, 0:256], func=AF.Copy),
              reads=[pvb], writes=[Vb[g]])

    rstate = {0: [0, None, None], 1: [0, None, None]}

    def chain_reset(pair, d):
        rstate[pair] = [0, None, d]

    def chain_step(pair, c):
        cur, prev, d = rstate[pair]
        nxt = 1 - cur
        g = c // 4
        pkv, pkvb = ps_next()
        em.op("pe", lambda e: e.matmul(pkv[:, 0:256], lhsT=KItm[d][:, c, :], rhs=V[:, c, :], start=True, stop=True),
              reads=[KTb[d][g], Vb[g]], writes=[pkvb])
        if prev is None:
            em.op("dve", lambda e: e.tensor_copy(out=R[pair][nxt], in_=pkv[:, 0:256]),
                  reads=[pkvb], writes=[Bh["R%d%d" % (pair, nxt)]])
        else:
            em.op("dve", lambda e: e.scalar_tensor_tensor(out=R[pair][nxt], in0=R[pair][cur],
                                                          scalar=DEC[:, d, prev:prev + 1],
                                                          in1=pkv[:, 0:256], op0=ALU.mult, op1=ALU.add),
                  reads=[Bh["R%d%d" % (pair, cur)], DECb[d][prev // 4], pkvb], writes=[Bh["R%d%d" % (pair, nxt)]])
        rstate[pair][0] = nxt
        rstate[pair][1] = c

    def state_bf16(pair, out_ap, outb):
        cur, prev, d = rstate[pair]
        em.op("act", lambda e: e.activation(out=out_ap, in_=R[pair][cur], func=AF.Identity,
                                            scale=DEC[:, d, prev:prev + 1]),
              reads=[Bh["R%d%d" % (pair, cur)], DECb[d][prev // 4]], writes=[outb])

    def out_a(h, c):
        ai = c % 2
        g = c // 4
        cs = slice(c * 128, (c + 1) * 128)
        pa, pab = ps_next()
        for d in range(2):
            em.op("pe", lambda e, d=d: e.matmul(pa[:, d * 128:(d + 1) * 128], lhsT=KI[d][:, cs], rhs=QD[d][:, cs],
                                                start=True, stop=True),
                  reads=[KIb[d][g], QDb[d][g]], writes=[pab])
        em.op("dve", lambda e: e.tensor_tensor(out=ATpair[ai], in0=pa[:, 0:256], in1=mask2, op=ALU.mult),
              reads=[pab, Bc["mask"]], writes=[Bh["AT%d0" % ai], Bh["AT%d1" % ai]])

    def out_b(h, c, ds, do_chain):
        pair = h % 2
        dc = 1 - ds
        ai = c % 2
        g = c // 4
        cs = slice(c * 128, (c + 1) * 128)
        state_bf16(pair, Sfb[ai], Bh["Sfb%d" % ai])
        if do_chain:
            chain_step(pair, c)
        pos = [ps_next(), ps_next()]
        for ec in range(2):
            es = slice(ec * 128, (ec + 1) * 128)
            po, pob = pos[ec]
            em.op("pe", lambda e, es=es, po=po: e.matmul(po[:, 0:128], lhsT=V[:, c, es], rhs=AT[ai][0], start=True,
                                                         stop=False),
                  reads=[Vb[g], Bh["AT%d0" % ai]], writes=[pob])
            em.op("pe", lambda e, es=es, po=po: e.matmul(po[:, 0:128], lhsT=V[:, c, es], rhs=AT[ai][1], start=False,
                                                         stop=False),
                  reads=[Vb[g], Bh["AT%d1" % ai]], writes=[pob])
            em.op("pe", lambda e, es=es, po=po: e.matmul(po[:, 0:128], lhsT=SBST[:, c, es], rhs=QD[dc][:, cs],
                                                         start=False, stop=False),
                  reads=[SBb[g], QDb[dc][g]], writes=[pob])
        cc = c % 4
        oi = g % 2
        for ec in range(2):
            es = slice(ec * 128, (ec + 1) * 128)
            po, pob = pos[ec]
            em.op("pe", lambda e, es=es, po=po: e.matmul(po[:, 0:128], lhsT=Sfb[ai][:, es], rhs=QD[ds][:, cs],
                                                         start=False, stop=True),
                  reads=[Bh["Sfb%d" % ai], QDb[ds][g]], writes=[pob])
            em.op("act", lambda e, ec=ec, po=po: e.activation(out=Og[oi][:, ec, cc * 128:(cc + 1) * 128],
                                                              in_=po[:, 0:128], func=AF.Copy),
                  reads=[pob], writes=[Bh["Og%d" % oi]])

    def epilogue12(h, g):
        t0, w = GROUPS[g]
        oi = g % 2
        Ogg, Oggb = Og[oi], Bh["Og%d" % oi]
        stats(lambda j: (Ogg[:, j, :], Oggb), 2, 1.0 / 256.0, RO, Bh["RO"], 512,
              sq2, [Bh["sq0"], Bh["sq1"]], tX[0], Bh["tX0"])
        for ec in range(2):
            pg, pgb = ps_next()
            for kc in range(8):
                em.op("pe", lambda e, kc=kc, ec=ec, pg=pg: e.matmul(
                    pg[:, :], lhsT=Wg[:, kc, ec * 128:(ec + 1) * 128], rhs=H[:, kc, t0:t0 + w],
                    start=(kc == 0), stop=(kc == 7)),
                    reads=[Wgb, Hb[g]], writes=[pgb])
            em.op("dve", lambda e, ec=ec: e.scalar_tensor_tensor(
                out=Ogg[:, ec, :], in0=Ogg[:, ec, :], scalar=vec1[:, 32 + 2 * h + ec:33 + 2 * h + ec], in1=RO,
                op0=ALU.mult, op1=ALU.mult),
                reads=[Oggb, Bc["vec1"], Bh["RO"]], writes=[Oggb])
            em.op("act", lambda e, pg=pg, ec=ec: e.activation(out=tX[0], in_=pg[:, :], func=AF.Silu), reads=[pgb],
                  writes=[Bh["tX0"]])
            em.op("pool", lambda e, ec=ec: e.tensor_tensor(out=OGT[:, ec, :], in0=Ogg[:, ec, :], in1=tX[0],
                                                           op=ALU.mult),
                  reads=[Oggb, Bh["tX0"]], writes=[Bh["OGT"]])

    def epilogue3(h, g):
        t0, w = GROUPS[g]
        for cch in range(8):
            pp, ppb = ps_next()
            for ec in range(2):
                em.op("pe", lambda e, ec=ec, cch=cch, pp=pp: e.matmul(
                    pp[:, :], lhsT=WOh[:, ec, cch * 128:(cch + 1) * 128], rhs=OGT[:, ec, :],
                    start=(ec == 0), stop=(ec == 1)),
                    reads=[WOhb, Bh["OGT"]], writes=[ppb])
            em.op("dve", lambda e, cch=cch, pp=pp: e.scalar_tensor_tensor(
                out=XS[:, cch, t0:t0 + w], in0=pp[:, :], scalar=MOD[:, 16 + cch, 0:1], in1=XS[:, cch, t0:t0 + w],
                op0=ALU.mult, op1=ALU.add),
                reads=[ppb, Bc["MOD1"], XSb[cch][g]], writes=[XSb[cch][g]])

    def final_group(g):
        t0, w = GROUPS[g]
        stats(lambda j: (XS[:, j, t0:t0 + w], XSb[j][g]), 8, 1.0 / D, RS2[g % 2][:, 0:w], RS2b[g % 2], w,
              sq2, [Bh["sq0"], Bh["sq1"]], tX[0], Bh["tX0"])
        for j in range(8):
            em.op("dve", lambda e, j=j: e.scalar_tensor_tensor(
                out=XS[:, j, t0:t0 + w], in0=XS[:, j, t0:t0 + w], scalar=vec1[:, 40 + j:41 + j],
                in1=RS2[g % 2][:, 0:w], op0=ALU.mult, op1=ALU.mult),
                reads=[XSb[j][g], Bc["vec1"], RS2b[g % 2]], writes=[XSb[j][g]])
            L1_OUTS.append(em.dma("sp", y_d[:, j, t0:t0 + w], XS[:, j, t0:t0 + w], reads=[XSb[j][g]]))

    def chunks_of(g, d):
        if g == 4:
            return [16, 17] if d == 0 else [17, 16]
        cs = list(range(4 * g, 4 * g + 4))
        return cs if d == 0 else cs[::-1]

    def stored_chain_group(h, g):
        pair = h % 2
        dc = h % 2
        step = 1 if dc == 0 else -1
        for c in chunks_of(g, dc):
            chain_step(pair, c)
            if g == 4:
                nxt = (0 if dc == 0 else 15) if c == chunks_of(4, dc)[-1] else None
            else:
                nxt = c + step
            if nxt is not None and 0 <= nxt <= 15:
                state_bf16(pair, SBST[:, nxt, :], SBb[nxt // 4])
            yield

    def sweep_group(h, g, next_first):
        pair = h % 2
        ds = 1 - (h % 2)
        cs = chunks_of(g, ds)
        for i, c in enumerate(cs):
            nxt = cs[i + 1] if i + 1 < len(cs) else next_first
            if nxt is not None:
                out_a(h, nxt)
            last_overall = (c == (0 if ds == 1 else 15))
            out_b(h, c, ds, not last_overall)
            yield
            if i == 0 and pending_ep3:
                gg = pending_ep3.pop()
                epilogue3(h, gg)
                if h == 3:
                    final_group(gg)
                yield
        epilogue12(h, g)
        pending_ep3.append(g)
        yield

    pending_ep3 = []

    def drive(gens_a, gens_b, ratio=2):
        A = list(gens_a)
        B = list(gens_b)
        if STOP == 77:
            for g_ in A:
                for _ in g_:
                    pass
            em.barrier()
            for g_ in B:
                for _ in g_:
                    pass
            em.barrier()
            return
        ia = ib = 0
        while ia < len(A) or ib < len(B):
            if ia < len(A):
                try:
                    next(A[ia])
                except StopIteration:
                    ia += 1
            for _ in range(ratio):
                if ib < len(B):
                    try:
                        next(B[ib])
                    except StopIteration:
                        ib += 1

    def prep_order(h):
        return [4, 0, 1, 2, 3] if h % 2 == 0 else [4, 3, 2, 1, 0]

    chain_reset(0, 0)
    po0 = prep_order(0)
    setup_stats(po0[0], 0)
    for i, g in enumerate(po0):
        if i + 1 < 5:
            setup_stats(po0[i + 1], (i + 1) % 2)
        setup_group(g, i % 2)
        if i == 0:
            for d in range(2):
                em.dma("pool", WA2[d][0:32, :], wa2_d[:, d, :], writes=[Bw["WA2"], XSb[6][4], XSb[7][4]])
        if i >= 1:
            for _ in prep_group(0, po0[i - 1]):
                pass
        if i >= 2:
            for _ in stored_chain_group(0, po0[i - 2]):
                pass
    for _ in prep_group(0, po0[4]):
        pass
    for _ in stored_chain_group(0, po0[3]):
        pass
    for _ in stored_chain_group(0, po0[4]):
        pass

    for h in range(4):
        pair = h % 2
        ds = 1 - (h % 2)
        if h < 3:
            load_qkv(h + 1)
        chain_reset(pair, ds)
        for c in chunks_of(4, ds):
            chain_step(pair, c)
        sg = [3, 2, 1, 0] if ds == 1 else [0, 1, 2, 3]
        out_a(h, chunks_of(sg[0], ds)[0])
        if h < 3:
            chain_reset(1 - pair, (h + 1) % 2)
            pg = prep_order(h + 1)
            assert pg[1:] == sg
        for k in range(5):
            A = []
            if k < 4:
                nf = chunks_of(sg[k + 1], ds)[0] if k + 1 < 4 else None
                A = [sweep_group(h, sg[k], nf)]
            B = []
            if h < 3:
                B.append(prep_group(h + 1, pg[k]))
                if k > 0:
                    B.append(stored_chain_group(h + 1, pg[k - 1]))
            drive(A, B)
        if h < 3:
            for _ in stored_chain_group(h + 1, pg[4]):
                pass
        while pending_ep3:
            gg = pending_ep3.pop()
            epilogue3(h, gg)
            if h == 3:
                final_group(gg)
        if h < 3:
            load_g(h + 1)


def _kc_layout(w):
    n = w.shape[1]
    return np.ascontiguousarray(w.reshape(8, 128, n).transpose(1, 0, 2))


def _vec_layout(v):
    return np.ascontiguousarray(v.reshape(-1, 128).T)


def _wa2_layout(wf, wb):
    o = np.zeros((32, 2, 512), np.float32)
    o[0:16, 0, :] = wf
    o[16:32, 1, :] = wb
    return o


def _rope_tables():
    pos = np.arange(S)
    row = (pos // 64).astype(np.float64)
    col = (pos % 64).astype(np.float64)
    inv_freq = 10000.0 ** (-np.arange(16, dtype=np.float64) / 16.0)
    C = np.zeros((128, S), np.float32)
    Sn = np.zeros((128, S), np.float32)
    for r in range(128):
        d = r % 64
        a = d // 32
        f = d % 16
        ang = (row if a == 0 else col) * inv_freq[f]
        C[r] = np.cos(ang).astype(np.float32)
        Sn[r] = np.sin(ang).astype(np.float32)
    P = np.zeros((128, 128), np.float32)
    for m in range(128):
        d = m % 64
        jj = (d % 32) // 16
        if jj == 0:
            P[m + 16, m] = -1.0
        else:
            P[m - 16, m] = 1.0
    return C, Sn, P


def prepare_inputs(x, c, ctx, c_ctx, l0_norm_g, l0_w_ada, l0_b_ada, l0_w_in, l0_sink, l0_w_out,
                   l1_norm_g, l1_w_ada, l1_b_ada, l1_w_in, l1_wa1_f, l1_wa2_f, l1_ba_f,
                   l1_wa1_b, l1_wa2_b, l1_ba_b, l1_head_norm_g, l1_w_out, final_norm_g):
    f = lambda a: np.asarray(a, dtype=np.float32)
    x, c, ctx, c_ctx = f(x), f(c), f(ctx), f(c_ctx)
    C, Sn, P = _rope_tables()
    ii = np.arange(128)
    mask = np.concatenate([(ii[:, None] <= ii[None, :]), (ii[:, None] >= ii[None, :])], axis=1).astype(np.float32)
    rmask = np.ones((128, 512), np.float32)
    rmask[:, ::128] = 0.0
    qcols = []
    for j in range(8):
        for r in range(128):
            hh = j if r < 64 else 8 + j
            qcols.append(hh * 64 + (r % 64))
    w_in0 = f(l0_w_in)
    cols = np.concatenate([np.array(qcols), np.arange(1024, 1152), np.arange(1152, 1280), np.arange(1280, 2304)])
    shared = {
        "ropec": C, "ropes": Sn, "perm": P, "ident": np.eye(128, dtype=np.float32), "mask": mask, "rmask": rmask,
        "sink": np.ascontiguousarray(np.broadcast_to(f(l0_sink)[None, :], (128, 16))),
        "vec0": np.concatenate([_vec_layout(f(l0_norm_g)), _vec_layout(f(l0_b_ada))], axis=1),
        "vec1": np.concatenate([_vec_layout(f(l1_norm_g)), _vec_layout(f(l1_b_ada)), _vec_layout(f(l1_head_norm_g)),
                                _vec_layout(f(final_norm_g)), _vec_layout(f(l1_ba_f)), _vec_layout(f(l1_ba_b))],
                               axis=1),
        "wada0": _kc_layout(f(l0_w_ada)), "win0": _kc_layout(w_in0[:, cols]), "wout0": _kc_layout(f(l0_w_out)),
        "wada1": _kc_layout(f(l1_w_ada)), "win1": _kc_layout(f(l1_w_in)), "wout1": _kc_layout(f(l1_w_out)),
        "wa1": _kc_layout(np.concatenate([f(l1_wa1_f), f(l1_wa1_b)], axis=1)),
        "wa2": _wa2_layout(f(l1_wa2_f), f(l1_wa2_b)),
    }
    in_maps = []
    for b in range(8):
        cat = np.concatenate([x[b], ctx[b]], axis=0)
        xs = np.ascontiguousarray(cat.T.reshape(8, 128, T).transpose(1, 0, 2))
        ccb = np.ascontiguousarray(np.stack([_vec_layout(c[b]), _vec_layout(c_ctx)], axis=2))
        m = {"xs": xs, "cc": ccb}
        m.update(shared)
        in_maps.append(m)
    return in_maps


_NC_CACHE = {}


def kernel(**inputs):
    in_maps = prepare_inputs(**inputs)
    if "nc" not in _NC_CACHE:
        _NC_CACHE["nc"] = build_program(stage=2)
    nc = _NC_CACHE["nc"]
    res = run_bass_kernel_spmd(nc, in_maps, core_ids=list(range(8)))
    out = np.empty((8, S, D), np.float32)
    for b in range(8):
        y = res.results[b]["y"]
        out[b] = y.transpose(2, 1, 0).reshape(S, D)
    return out
```
